# Optimizing a Trainium2 kernel written in Bass

```python
import math
import jax, jax.numpy as jnp
from jax import lax
import numpy as np

D_MODEL = 2048
BATCH = 16
SEQ = 256
DEPTH = 2
DEC_BATCH = 8
DEC_SEQ = 4096
PAST_LEN = 256

GRID_W = 64
MIX_HEAD = 128
N_MIX_HEADS = D_MODEL // MIX_HEAD
H_RET = N_MIX_HEADS // 4
DK_RET = MIX_HEAD
DV_RET = MIX_HEAD
H_GQA = N_MIX_HEADS // 2
KV_GQA = H_GQA // 4
HD_GQA = MIX_HEAD
H_DIFF = N_MIX_HEADS // 4
DK_DIFF = MIX_HEAD // 2
DV_DIFF = MIX_HEAD
D_FF = 11 * D_MODEL // 4
N_MOD = 9
Q_BLOCK = 128
RET_CHUNK = 128
ROPE_THETA = 10000.0
EPS = 1e-6
IN_SIZES = (H_RET * DK_RET, H_RET * DK_RET, H_RET * DV_RET, H_RET * DV_RET,
            H_GQA * HD_GQA, KV_GQA * HD_GQA, KV_GQA * HD_GQA,
            H_DIFF * 2 * DK_DIFF, H_DIFF * 2 * DK_DIFF, H_DIFF * DV_DIFF)
IN_COLS = sum(IN_SIZES)

kernel_name = 'hybrid_retention_gqa_diffattn_macaron_dit_step'

F32 = jnp.float32


def rmsnorm(x, w):
    xf = x.astype(F32)
    y = xf * lax.rsqrt(jnp.mean(xf * xf, axis=-1, keepdims=True) + EPS)
    return (y * w.astype(F32)).astype(x.dtype)


def head_groupnorm(x, w):
    xf = x.astype(F32)
    xc = xf - jnp.mean(xf, axis=-1, keepdims=True)
    var = jnp.mean(xc * xc, axis=-1, keepdims=True)
    return xc * lax.rsqrt(var + EPS) * w.astype(F32)


def swiglu(h, w_in, w_out):
    gate, up = jnp.split(h @ w_in, 2, axis=-1)
    return (jax.nn.silu(gate) * up) @ w_out


def modulation(cvec, w, b):
    m = jax.nn.silu(cvec) @ w + b
    return m.reshape(-1, N_MOD, 1, D_MODEL)


def axial_rope(rows, d):
    axis = d // 2
    inv = 1.0 / (ROPE_THETA ** (jnp.arange(0, axis, 2, dtype=F32) / axis))
    r = jnp.repeat(jnp.arange(rows, dtype=F32), GRID_W)
    cl = jnp.tile(jnp.arange(GRID_W, dtype=F32), rows)
    ang = jnp.concatenate([r[:, None] * inv, cl[:, None] * inv], axis=-1)
    return jnp.cos(ang), jnp.sin(ang)


def apply_rope(x, cos, sin):
    xf = x.astype(F32)
    x1, x2 = xf[..., 0::2], xf[..., 1::2]
    shape = (x.shape[1],) + (1,) * (x.ndim - 3) + (cos.shape[-1],)
    c = cos.reshape(shape)
    s = sin.reshape(shape)
    out = jnp.stack([x1 * c - x2 * s, x1 * s + x2 * c], axis=-1).reshape(x.shape)
    return out.astype(x.dtype)


def retention_scan(q, k, v, log_gamma, state0):
    B, T, H, _ = q.shape
    dv = v.shape[-1]
    n = T // RET_CHUNK
    idx = jnp.arange(RET_CHUNK, dtype=F32)
    dif = idx[:, None] - idx[None, :]
    lg = log_gamma.astype(F32)
    inner_decay = jnp.where((dif >= 0)[None], jnp.exp(jnp.maximum(dif, 0.0)[None] * lg[:, None, None]), 0.0)
    q_decay = jnp.exp((idx + 1.0)[:, None] * lg[None])
    k_decay = jnp.exp((RET_CHUNK - 1.0 - idx)[:, None] * lg[None])
    chunk_decay = jnp.exp(RET_CHUNK * lg)

    def to_chunks(a):
        return a.astype(F32).reshape(B, n, RET_CHUNK, H, a.shape[-1]).swapaxes(0, 1)

    def step(S, inp):
        qc, kc, vc = inp
        att = jnp.einsum('bihd,bjhd->bhij', qc, kc) * inner_decay
        o = (jnp.einsum('bhij,bjhe->bihe', att, vc)
             + jnp.einsum('bihd,bhde->bihe', qc, S) * q_decay[None, :, :, None])
        S = S * chunk_decay[None, :, None, None] + jnp.einsum(
            'bjhd,bjhe->bhde', kc * k_decay[None, :, :, None], vc)
        return S, o

    S, o = lax.scan(step, state0.astype(F32), (to_chunks(q), to_chunks(k), to_chunks(v)))
    return o.swapaxes(0, 1).reshape(B, T, H, dv), S


def bidir_retention(q, k, v, log_gammas, s_fwd, s_bwd):
    o_f, S_f = retention_scan(q, k, v, log_gammas[0], s_fwd)
    o_b, S_b = retention_scan(q[:, ::-1], k[:, ::-1], v[:, ::-1], log_gammas[1], s_bwd)
    return o_f + o_b[:, ::-1], jnp.stack([S_f, S_b], axis=1)


def gqa_blocks(q, k, v, scale):
    B, T = q.shape[:2]
    nb = T // Q_BLOCK
    qb = q.reshape((B, nb, Q_BLOCK) + q.shape[2:]).swapaxes(0, 1)

    def one(qi):
        s = jnp.einsum('bqkgd,bskd->bkgqs', qi, k, preferred_element_type=F32) * scale
        p = jax.nn.softmax(s, axis=-1).astype(v.dtype)
        return jnp.einsum('bkgqs,bskd->bqkgd', p, v)

    out = lax.map(one, qb)
    return out.swapaxes(0, 1).reshape((B, T) + out.shape[3:])


def diff_blocks(q, k, v, lam, scale):
    B, T = q.shape[:2]
    nb = T // Q_BLOCK
    qb = q.reshape((B, nb, Q_BLOCK) + q.shape[2:]).swapaxes(0, 1)

    def one(qi):
        s = jnp.einsum('bqhcd,bshcd->bhcqs', qi, k, preferred_element_type=F32) * scale
        p = jax.nn.softmax(s, axis=-1)
        w = (p[:, :, 0] - lam * p[:, :, 1]).astype(v.dtype)
        return jnp.einsum('bhqs,bshe->bqhe', w, v)

    out = lax.map(one, qb)
    return out.swapaxes(0, 1).reshape((B, T) + out.shape[3:])


def token_mixer(h, lp, lam_init, rope, ctx):
    B, T, _ = h.shape
    split_points = np.cumsum(IN_SIZES)[:-1].tolist()
    rq, rk, rv, rg, gq, gk, gv, dq, dk, dv = jnp.split(h @ lp['w_in'], split_points, axis=-1)

    q_r = rq.reshape(B, T, H_RET, DK_RET)
    k_r = rk.reshape(B, T, H_RET, DK_RET) * (DK_RET ** -0.5)
    v_r = rv.reshape(B, T, H_RET, DV_RET)
    log_g = jax.nn.log_sigmoid(lp['ret_decay'].astype(F32))
    if ctx is None:
        s_f = jnp.zeros((B, H_RET, DK_RET, DV_RET), F32)
        s_b = s_f
    else:
        s_f = ctx['state_ret'][:, 0]
        s_b = ctx['state_ret'][:, 1]
    o_r, s_new = bidir_retention(q_r, k_r, v_r, log_g, s_f, s_b)
    ret_out = head_groupnorm(o_r, lp['ret_gn_w']).astype(h.dtype) * jax.nn.silu(rg).reshape(B, T, H_RET, DV_RET)

    q_g = rmsnorm(gq.reshape(B, T, H_GQA, HD_GQA), lp['gqa_qk_norm'][0])
    k_g = rmsnorm(gk.reshape(B, T, KV_GQA, HD_GQA), lp['gqa_qk_norm'][1])
    v_g = gv.reshape(B, T, KV_GQA, HD_GQA)

    q_d = dq.reshape(B, T, H_DIFF, 2, DK_DIFF)
    k_d = dk.reshape(B, T, H_DIFF, 2, DK_DIFF)
    v_d = dv.reshape(B, T, H_DIFF, DV_DIFF)

    if ctx is None:
        keys_g, vals_g, keys_d, vals_d = k_g, v_g, k_d, v_d
        new = (s_new, k_g, v_g, k_d.reshape(B, T, H_DIFF, 2 * DK_DIFF), v_d)
    else:
        (cos_g, sin_g), (cos_d, sin_d) = rope
        q_g = apply_rope(q_g, cos_g, sin_g)
        keys_g = jnp.concatenate([apply_rope(k_g, cos_g, sin_g), ctx['k_gqa'].astype(h.dtype)], axis=1)
        vals_g = jnp.concatenate([v_g, ctx['v_gqa'].astype(h.dtype)], axis=1)
        q_d = apply_rope(q_d, cos_d, sin_d)
        c_kd = ctx['k_diff'].astype(h.dtype).reshape(B, -1, H_DIFF, 2, DK_DIFF)
        keys_d = jnp.concatenate([apply_rope(k_d, cos_d, sin_d), c_kd], axis=1)
        vals_d = jnp.concatenate([v_d, ctx['v_diff'].astype(h.dtype)], axis=1)
        new = None

    gqa_out = gqa_blocks(q_g.reshape(B, T, KV_GQA, H_GQA // KV_GQA, HD_GQA), keys_g, vals_g,
                         HD_GQA ** -0.5).reshape(B, T, H_GQA * HD_GQA)

    dl = lp['diff_lambda'].astype(F32)
    lam = jnp.exp(jnp.sum(dl[0] * dl[1])) - jnp.exp(jnp.sum(dl[2] * dl[3])) + lam_init
    o_d = diff_blocks(q_d, keys_d, vals_d, lam, DK_DIFF ** -0.5)
    diff_out = rmsnorm(o_d, lp['diff_norm_w']) * (1.0 - lam_init)

    mixed = jnp.concatenate([ret_out.reshape(B, T, -1), gqa_out, diff_out.reshape(B, T, -1)], axis=-1)
    return mixed @ lp['w_out'], new


def trunk_layer(x, mod, lp, lam_init, rope, ctx):
    def modulated(x, i):
        h = rmsnorm(x, lp['norm_w'][i])
        return h * (1.0 + mod[:, 3 * i + 1]) + mod[:, 3 * i]

    x = x + 0.5 * mod[:, 2] * swiglu(modulated(x, 0), lp['ffn_w_in'][0], lp['ffn_w_out'][0])
    out, new = token_mixer(modulated(x, 1), lp, lam_init, rope, ctx)
    x = x + mod[:, 5] * out
    x = x + 0.5 * mod[:, 8] * swiglu(modulated(x, 2), lp['ffn_w_in'][1], lp['ffn_w_out'][1])
    return x, new


def setup_inputs(seed: int = 0) -> dict:
    key = jax.random.key(seed)
    ks = jax.random.split(key, 24)

    def nrm(k, shape, scale):
        return jax.random.normal(k, shape, F32) * scale

    g = 1.0 - 2.0 ** (-5.0 - np.arange(H_RET))
    decay_logit = jnp.asarray(np.log(g / (1.0 - g)), dtype=F32)
    return {
        'x_prompt': nrm(ks[0], (BATCH, SEQ, D_MODEL), 1.0),
        'x_sample': nrm(ks[1], (DEC_BATCH, DEC_SEQ, D_MODEL), 1.0),
        'c': nrm(ks[2], (DEC_BATCH, D_MODEL), 1.0),
        'state_ret': nrm(ks[3], (DEC_BATCH, DEPTH, 2, H_RET, DK_RET, DV_RET), 0.5),
        'cache_gqa_k': nrm(ks[4], (DEC_BATCH, DEPTH, PAST_LEN, KV_GQA, HD_GQA), 1.0),
        'cache_gqa_v': nrm(ks[5], (DEC_BATCH, DEPTH, PAST_LEN, KV_GQA, HD_GQA), 1.0),
        'cache_diff_k': nrm(ks[6], (DEC_BATCH, DEPTH, PAST_LEN, H_DIFF, 2 * DK_DIFF), 1.0),
        'cache_diff_v': nrm(ks[7], (DEC_BATCH, DEPTH, PAST_LEN, H_DIFF, DV_DIFF), 1.0),
        'c_ctx': nrm(ks[8], (D_MODEL,), 1.0),
        'w_mod': nrm(ks[9], (DEPTH, D_MODEL, N_MOD * D_MODEL), 0.5 * D_MODEL ** -0.5),
        'b_mod': nrm(ks[10], (DEPTH, N_MOD * D_MODEL), 0.01),
        'norm_w': 1.0 + nrm(ks[11], (DEPTH, 3, D_MODEL), 0.02),
        'ffn_w_in': nrm(ks[12], (DEPTH, 2, D_MODEL, 2 * D_FF), D_MODEL ** -0.5),
        'ffn_w_out': nrm(ks[13], (DEPTH, 2, D_FF, D_MODEL), D_FF ** -0.5),
        'w_in': nrm(ks[14], (DEPTH, D_MODEL, IN_COLS), D_MODEL ** -0.5),
        'w_out': nrm(ks[15], (DEPTH, D_MODEL, D_MODEL), D_MODEL ** -0.5),
        'ret_decay': decay_logit[None, None, :] + nrm(ks[16], (DEPTH, 2, H_RET), 0.1),
        'ret_gn_w': 1.0 + nrm(ks[17], (DEPTH, H_RET, DV_RET), 0.02),
        'gqa_qk_norm': 1.0 + nrm(ks[18], (DEPTH, 2, HD_GQA), 0.02),
        'diff_lambda': nrm(ks[19], (DEPTH, 4, DK_DIFF), 0.1),
        'diff_norm_w': 1.0 + nrm(ks[20], (DEPTH, H_DIFF, DV_DIFF), 0.02),
        'final_norm_w': 1.0 + nrm(ks[21], (D_MODEL,), 0.02),
    }


def reference(x_prompt, x_sample, c, state_ret, cache_gqa_k, cache_gqa_v, cache_diff_k, cache_diff_v,
              c_ctx, w_mod, b_mod, norm_w, ffn_w_in, ffn_w_out, w_in, w_out, ret_decay, ret_gn_w,
              gqa_qk_norm, diff_lambda, diff_norm_w, final_norm_w):
    rows = x_sample.shape[1] // GRID_W
    rope = (axial_rope(rows, HD_GQA), axial_rope(rows, DK_DIFF))
    xp, xs = x_prompt, x_sample
    new_ret, new_gk, new_gv, new_dk, new_dv = [], [], [], [], []
    for l in range(DEPTH):
        lp = {'w_in': w_in[l], 'w_out': w_out[l], 'norm_w': norm_w[l],
              'ffn_w_in': ffn_w_in[l], 'ffn_w_out': ffn_w_out[l],
              'ret_decay': ret_decay[l], 'ret_gn_w': ret_gn_w[l], 'gqa_qk_norm': gqa_qk_norm[l],
              'diff_lambda': diff_lambda[l], 'diff_norm_w': diff_norm_w[l]}
        lam_init = 0.8 - 0.6 * math.exp(-0.3 * l)
        xp, new = trunk_layer(xp, modulation(c_ctx, w_mod[l], b_mod[l]), lp, lam_init, None, None)
        new_ret.append(new[0].astype(x_prompt.dtype))
        new_gk.append(new[1])
        new_gv.append(new[2])
        new_dk.append(new[3])
        new_dv.append(new[4])
        ctx_l = {'state_ret': state_ret[:, l], 'k_gqa': cache_gqa_k[:, l], 'v_gqa': cache_gqa_v[:, l],
                 'k_diff': cache_diff_k[:, l], 'v_diff': cache_diff_v[:, l]}
        xs, _ = trunk_layer(xs, modulation(c, w_mod[l], b_mod[l]), lp, lam_init, rope, ctx_l)
    y_prompt = rmsnorm(xp, final_norm_w)
    y_sample = rmsnorm(xs, final_norm_w)
    new_state_ret = jnp.stack(new_ret, axis=1)
    new_gqa_k = jnp.stack(new_gk, axis=1)
    new_gqa_v = jnp.stack(new_gv, axis=1)
    new_diff_k = jnp.stack(new_dk, axis=1)
    new_diff_v = jnp.stack(new_dv, axis=1)
    return (y_prompt, y_sample, new_state_ret, new_gqa_k, new_gqa_v, new_diff_k, new_diff_v)
```

```python
import math
import numpy as np
import ml_dtypes
import concourse.bass as bass
import concourse.mybir as mybir
from concourse.bass_utils import run_bass_kernel_spmd

F32 = mybir.dt.float32
BF16 = mybir.dt.bfloat16
AF = mybir.ActivationFunctionType
ALU = mybir.AluOpType
AX = mybir.AxisListType
ENGS = ('tensor', 'vector', 'scalar', 'gpsimd', 'sync')
NDS = 40
EPS = 1e-6


class Cfg:
    def __init__(self, TS=4096, DFF=5632, L=2):
        self.D = 2048; self.NCH = 16
        self.DFF = DFF; self.NF = DFF // 128
        self.TS = TS; self.TP = 256; self.NPS = 2; self.PAST = 256
        self.L = L
        self.NTOK = TS + self.NPS * self.TP
        self.NT = self.NTOK // 512
        self.NB = self.NTOK // 128


class Buf:
    __slots__ = ('w', 'r')

    def __init__(self):
        self.w = None
        self.r = {}


class T:
    __slots__ = ('ap', 'b')

    def __init__(self, ap, b=None):
        self.ap = ap
        self.b = b if b is not None else Buf()

    def v(self, ap):
        return T(ap, self.b)


class Sched:
    def __init__(self, nc):
        self.nc = nc
        self.q = {e: [] for e in ENGS}
        self.cnt = {e: 0 for e in ENGS}
        self.sems = {}
        for e in ENGS:
            self.sems['p_' + e] = nc.alloc_semaphore(name='p_' + e)
        for i in range(NDS):
            self.sems[('d', i)] = nc.alloc_semaphore(name='d%d' % i)
        self.duse = [0] * NDS
        self.dpool = {'sync': list(range(0, 24)), 'gpsimd': list(range(24, NDS))}
        self.dnext = {'sync': 0, 'gpsimd': 0}
        self.waited = {e: {} for e in ENGS}

    def _wait(self, eng, deps):
        wd = self.waited[eng]
        for key, val in deps:
            if wd.get(key, 0) >= val:
                continue
            wd[key] = val
            self.q[eng].append(('wait', key, val))

    def _deps(self, eng, r, w):
        deps = []
        for t in r:
            if t.b.w is not None:
                deps.append(t.b.w)
        for t in w:
            if t.b.w is not None:
                deps.append(t.b.w)
            deps.extend(t.b.r.items())
        if eng == 'tensor':
            deps = [d for d in deps if d[0] != 'p_tensor']
        return deps

    def _mark(self, tok, r, w):
        for t in r:
            if t.b.r.get(tok[0], 0) < tok[1]:
                t.b.r[tok[0]] = tok[1]
        for t in w:
            t.b.w = tok
            t.b.r = {}

    def op(self, eng, meth, r=(), w=(), **kw):
        self._wait(eng, self._deps(eng, r, w))
        self.cnt[eng] += 1
        tok = ('p_' + eng, self.cnt[eng])
        self.q[eng].append(('op', meth, kw))
        self._mark(tok, r, w)

    def dma(self, eng, out, in_, r=(), w=(), **kw):
        pl = self.dpool[eng]
        i = pl[self.dnext[eng] % len(pl)]
        self.dnext[eng] += 1
        deps = self._deps(eng, r, w)
        if self.duse[i] > 0:
            deps.append((('d', i), 16 * self.duse[i]))
        self._wait(eng, deps)
        self.duse[i] += 1
        tok = (('d', i), 16 * self.duse[i])
        self.q[eng].append(('dma', out, in_, kw, ('d', i)))
        self._mark(tok, r, w)

    def barrier(self):
        allv = [('p_' + e, self.cnt[e]) for e in ENGS if self.cnt[e] > 0]
        allv += [(('d', i), 16 * self.duse[i]) for i in range(NDS) if self.duse[i] > 0]
        for e in ENGS:
            self._wait(e, allv)

    def emit(self, block):
        nc = self.nc
        sems = self.sems
        q = self.q

        def run(eng, name):
            psem = sems['p_' + name]
            for it in q[name]:
                if it[0] == 'wait':
                    eng.wait_ge(sems[it[1]], it[2])
                elif it[0] == 'op':
                    getattr(eng, it[1])(**it[2]).then_inc(psem, 1)
                else:
                    eng.dma_start(out=it[1], in_=it[2], **it[3]).then_inc(sems[it[4]], 16)

        @block.tensor
        def _(e):
            run(e, 'tensor')

        @block.vector
        def _(e):
            run(e, 'vector')

        @block.scalar
        def _(e):
            run(e, 'scalar')

        @block.gpsimd
        def _(e):
            run(e, 'gpsimd')

        @block.sync
        def _(e):
            run(e, 'sync')


class Pool:
    def __init__(self, ap, ncols):
        self.ap = ap
        self.n = ncols
        self.off = 0

    def mark(self):
        return self.off

    def release(self, m):
        self.off = m

    def alloc(self, cols, dt=F32, parts=128):
        n32 = cols if dt == F32 else (cols + 1) // 2
        assert self.off + n32 <= self.n, ("SBUF pool overflow", self.off, n32, self.n)
        v = self.ap[0:parts, self.off:self.off + n32]
        self.off += n32
        if dt != F32:
            v = v.bitcast(dt)
        return T(v)


def _perm_deint(n):
    return np.concatenate([np.arange(0, n, 2), np.arange(1, n, 2)])


def _win_cols():
    sizes = [512, 512, 512, 512, 1024, 256, 256, 512, 512, 512]
    st = np.concatenate([[0], np.cumsum(sizes)])
    rq, rk, rv, rg, gq, gk, gv, dq, dk, dv = [np.arange(st[i], st[i + 1]) for i in range(10)]
    p128 = _perm_deint(128)
    p64 = _perm_deint(64)
    fm = []
    for h in range(4):
        fm.append(rq[h * 128:(h + 1) * 128])
    for h in range(4):
        fm.append(rk[h * 128:(h + 1) * 128])
    for h in range(4):
        fm.append(rg[h * 128:(h + 1) * 128])
    for h in range(8):
        fm.append(gq[h * 128:(h + 1) * 128][p128])
    for h in range(2):
        fm.append(gk[h * 128:(h + 1) * 128][p128])
    for h in range(4):
        c = dq[h * 128:(h + 1) * 128]
        fm.append(np.concatenate([c[0:64][p64], c[64:128][p64]]))
    for h in range(4):
        c = dk[h * 128:(h + 1) * 128]
        fm.append(np.concatenate([c[0:64][p64], c[64:128][p64]]))
    tm = [rk, rv, gv, dv]
    return fm, tm


G_RQ, G_RK, G_RG, G_GQ, G_GK, G_DQ, G_DK, NFM = 0, 4, 8, 12, 20, 22, 26, 30


def _rope_tables(TS):
    rows = TS // 64

    def tab(d):
        axis = d // 2
        inv = 1.0 / (10000.0 ** (np.arange(0, axis, 2, dtype=np.float32) / axis))
        r = np.repeat(np.arange(rows, dtype=np.float32), 64)
        cl = np.tile(np.arange(64, dtype=np.float32), rows)
        ang = np.concatenate([r[:, None] * inv, cl[:, None] * inv], axis=-1).astype(np.float32)
        return np.cos(ang).T.astype(np.float32), np.sin(ang).T.astype(np.float32)

    cg, sg = tab(128)
    cd, sd = tab(64)
    CG = np.concatenate([cg, cg], 0)
    SG = np.concatenate([-sg, sg], 0)
    CD = np.concatenate([cd, cd, cd, cd], 0)
    SD = np.concatenate([-sd, sd, -sd, sd], 0)
    return np.ascontiguousarray(np.stack([CG, SG, CD, SD], 0)).astype(np.float32)


C_ID, C_ONE, C_PG, C_PD, C_POS, C_NEG, C_MF, C_MB, C_I1, C_IB, C_KF, C_KB, NCONST = \
    0, 128, 256, 384, 512, 640, 768, 896, 1024, 1152, 1280, 1281, 1282


def _consts():
    c = np.zeros((128, NCONST), np.float32)
    c[:, C_ID:C_ID + 128] = np.eye(128)
    c[:, C_ONE:C_ONE + 128] = 1.0
    k = np.arange(128)
    pg = np.zeros((128, 128), np.float32)
    pg[(k + 64) % 128, k] = 1.0
    c[:, C_PG:C_PG + 128] = pg
    pd = np.zeros((128, 128), np.float32)
    for m in range(128):
        blk = (m // 64) * 64
        pd[blk + ((m % 64) + 32) % 64, m] = 1.0
    c[:, C_PD:C_PD + 128] = pd
    j = np.arange(128)[:, None].astype(np.float32)
    i = np.arange(128)[None, :].astype(np.float32)
    dif = i - j
    c[:, C_POS:C_POS + 128] = np.maximum(dif, 0)
    c[:, C_NEG:C_NEG + 128] = np.maximum(-dif, 0)
    c[:, C_MF:C_MF + 128] = (dif >= 0)
    c[:, C_MB:C_MB + 128] = (dif <= 0)
    c[:, C_I1:C_I1 + 128] = i + 1.0
    c[:, C_IB:C_IB + 128] = 128.0 - i
    c[:, C_KF] = 127.0 - np.arange(128)
    c[:, C_KB] = np.arange(128)
    return c


def build(cfg):
    nc = bass.Bass("TRN2", target_bir_lowering=False)
    D, NCH, NF, L = cfg.D, cfg.NCH, cfg.NF, cfg.L
    TS, TP, NPS, PAST = cfg.TS, cfg.TP, cfg.NPS, cfg.PAST
    NTOK, NT, NB = cfg.NTOK, cfg.NT, cfg.NB

    def din(name, shape, dt=F32):
        return nc.dram_tensor(name, list(shape), dt, kind="ExternalInput").ap()

    def dout(name, shape):
        return nc.dram_tensor(name, list(shape), F32, kind="ExternalOutput").ap()

    def dscr(name, shape, dt):
        return nc.dram_tensor(name, list(shape), dt, kind="Internal").ap()

    xin = din("xin", [NTOK, D])
    cv_in = din("cv", [128, 32])
    wmod = din("wmod", [L * 72, 128, 4096])
    bmod = din("bmod", [128, L * 144])
    normw = din("normw", [128, L * 48])
    fnw = din("fnw", [128, 16])
    w1 = din("w1", [L * 2 * NF, 128, 4096])
    w2 = din("w2", [L * 2 * 16, 128, NF * 128])
    winf = din("winf", [L * NFM, 128, 2048])
    wint = din("wint", [L * 4, 128, 16 * 512])
    wout = din("wout", [L * 16, 128, 2048])
    smalls = din("smalls", [128, L * 10])
    rdec = din("rdec", [128, L * 8])
    dlam = din("dlam", [128, L * 256])
    stin = din("stin", [L * 2, 128, 512])
    cgk = din("cgk", [L * 2, 128, PAST])
    cgv = din("cgv", [L * 2 * 2, 128, 128])
    cdk = din("cdk", [L * 4, 128, PAST])
    cdv = din("cdv", [L * 4 * 2, 128, 128])
    rope = din("rope", [4, 128, TS])
    consts = din("consts", [128, NCONST])

    y = dout("y", [NTOK, D])
    nst = dout("nst", [NPS * L * 2, 128, 512])
    ngk = dout("ngk", [NPS * L, 256, 256])
    ngv = dout("ngv", [NPS * L, 256, 256])
    ndk = dout("ndk", [NPS * L, 256, 512])
    ndv = dout("ndv", [NPS * L, 256, 512])

    XT = dscr("XT", [16, 128, NTOK], F32)
    W1s = dscr("W1s", [L * 2 * NF, 128, 4096], BF16)
    W2s = dscr("W2s", [L * 2 * 16, 128, NF * 128], BF16)
    WFs = dscr("WFs", [L * NFM, 128, 2048], BF16)
    WTs = dscr("WTs", [L * 4, 128, 16 * 512], BF16)
    WOs = dscr("WOs", [L * 16, 128, 2048], BF16)
    FMS = dscr("FMS", [NFM, 128, NTOK], BF16)
    KTOK = dscr("KTOK", [NB, 128, 512], BF16)
    VTOK = dscr("VTOK", [NB, 128, 512], BF16)
    GVT = dscr("GVT", [NB, 128, 256], BF16)
    DVT = dscr("DVT", [NB, 128, 512], BF16)
    MIX = dscr("MIX", [16, 128, NTOK], BF16)
    B_XT = [[Buf() for _ in range(NT)] for _ in range(16)]
    B_W = {}
    B_FMS = [[Buf() for _ in range(NB)] for _ in range(NFM)]
    B_TOK = {n: [Buf() for _ in range(NB)] for n in ('k', 'v', 'gv', 'dv')}
    B_MIX = [[Buf() for _ in range(NB)] for _ in range(16)]
    RO = T(None)

    S = Sched(nc)
    NPOOL = 44000
    from contextlib import ExitStack
    es = ExitStack()
    pool_t = es.enter_context(nc.sbuf_tensor("pool", [128, NPOOL], F32))
    ps_t = es.enter_context(nc.psum_tensor("ps", [128, 8 * 512], F32))
    pool = Pool(pool_t[:], NPOOL)
    PS = [T(ps_t[:, b * 512:(b + 1) * 512]) for b in range(8)]

    def fmbufs(g, t0, n):
        return [T(None, B_FMS[g][b]) for b in range(t0 // 128, (t0 + n) // 128)]

    CON = pool.alloc(NCONST)
    S.dma('sync', CON.ap, consts[:, :], w=[CON])
    IDF = CON.v(CON.ap[:, C_ID:C_ID + 128])
    CB = pool.alloc(512, BF16)
    S.op('vector', 'tensor_copy', r=[CON], w=[CB], out=CB.ap, in_=CON.ap[:, 0:512])
    IDB = CB.v(CB.ap[:, 0:128]); ONEB = CB.v(CB.ap[:, 128:256])
    PGB = CB.v(CB.ap[:, 256:384]); PDB = CB.v(CB.ap[:, 384:512])
    EPSC = pool.alloc(1)
    S.op('vector', 'memset', w=[EPSC], ap=EPSC.ap, constant=EPS)
    SM = pool.alloc(L * 10)
    S.dma('sync', SM.ap, smalls[:, :], w=[SM])
    FNW = pool.alloc(16)
    S.dma('sync', FNW.ap, fnw[:, :], w=[FNW])
    DER = pool.alloc(L * 2 * 9 * 16)

    def der(l, cond, i, kind):
        o = (((l * 2 + cond) * 3 + i) * 3 + kind) * 16
        return DER.v(DER.ap[:, o:o + 16])

    m0 = pool.mark()
    xl = [pool.alloc(2048) for _ in range(2)]
    xs = [pool.alloc(2048) for _ in range(2)]
    for tb in range(NB):
        a = xl[tb % 2]
        S.dma('sync', a.ap, xin[tb * 128:(tb + 1) * 128, :], w=[a])
        st = xs[tb % 2]
        for q4 in range(4):
            pb = PS[(tb * 4 + q4) % 8]
            for k in range(4):
                c = q4 * 4 + k
                S.op('tensor', 'transpose', r=[a, IDF], w=[pb], out=pb.ap[:, k * 128:(k + 1) * 128],
                     in_=a.ap[:, c * 128:(c + 1) * 128], identity=IDF.ap)
            eng = 'vector' if q4 % 2 == 0 else 'scalar'
            if eng == 'vector':
                S.op('vector', 'tensor_copy', r=[pb], w=[st], out=st.ap[:, q4 * 512:(q4 + 1) * 512], in_=pb.ap)
            else:
                S.op('scalar', 'activation', r=[pb], w=[st], out=st.ap[:, q4 * 512:(q4 + 1) * 512], in_=pb.ap,
                     func=AF.Copy)
        wb = [T(None, B_XT[c][tb // 4]) for c in range(16)]
        S.dma('sync', XT[:, :, tb * 128:(tb + 1) * 128].rearrange("c p t -> p c t"),
              st.ap.rearrange("p (c t) -> p c t", c=16), r=[st], w=wb)
    S.barrier()
    pool.release(m0)

    m0 = pool.mark()
    CV = pool.alloc(32)
    S.dma('sync', CV.ap, cv_in[:, :], w=[CV])
    SC_ = pool.alloc(32)
    S.op('scalar', 'activation', r=[CV], w=[SC_], out=SC_.ap, in_=CV.ap, func=AF.Silu)
    SCr = SC_.ap.rearrange("p (k c) -> p c k", k=2)
    SCc = pool.alloc(32)
    S.op('vector', 'tensor_copy', r=[SC_], w=[SCc], out=SCc.ap.rearrange("p (c k) -> p c k", k=2), in_=SCr)
    BM = pool.alloc(L * 144)
    S.dma('sync', BM.ap, bmod[:, :], w=[BM])
    NW = pool.alloc(L * 48)
    S.dma('sync', NW.ap, normw[:, :], w=[NW])
    MODT = pool.alloc(L * 2 * 144)
    wmb = [pool.alloc(4096) for _ in range(3)]
    for l in range(L):
        mp = PS[l % 2]
        for g in range(72):
            wt = wmb[(l * 72 + g) % 3]
            S.dma('sync', wt.ap, wmod[l * 72 + g, :, :], w=[wt])
            for mm in range(2):
                m = 2 * g + mm
                for c in range(16):
                    S.op('tensor', 'matmul', r=[wt, SCc], w=[mp], out=mp.ap[:, 2 * m:2 * m + 2],
                         lhsT=wt.ap[:, (mm * 16 + c) * 128:(mm * 16 + c + 1) * 128],
                         rhs=SCc.ap[:, 2 * c:2 * c + 2], start=(c == 0), stop=(c == 15))
        for cond in range(2):
            o = (l * 2 + cond) * 144
            S.op('vector', 'tensor_tensor', r=[mp, BM], w=[MODT],
                 out=MODT.ap[:, o:o + 144],
                 in0=mp.ap[:, 0:288].rearrange("p (m k) -> p m k", k=2)[:, :, cond],
                 in1=BM.ap[:, l * 144:(l + 1) * 144], op=ALU.add)
        for cond in range(2):
            o = (l * 2 + cond) * 144
            for i in range(3):
                S.op('vector', 'scalar_tensor_tensor', r=[MODT, NW], w=[DER],
                     out=der(l, cond, i, 0).ap, in0=MODT.ap[:, o + (3 * i + 1) * 16:o + (3 * i + 2) * 16],
                     scalar=1.0, in1=NW.ap[:, l * 48 + i * 16:l * 48 + (i + 1) * 16], op0=ALU.add, op1=ALU.mult)
                S.op('vector', 'tensor_copy', r=[MODT], w=[DER],
                     out=der(l, cond, i, 1).ap, in_=MODT.ap[:, o + (3 * i) * 16:o + (3 * i + 1) * 16])
                S.op('vector', 'tensor_scalar', r=[MODT], w=[DER],
                     out=der(l, cond, i, 2).ap, in0=MODT.ap[:, o + (3 * i + 2) * 16:o + (3 * i + 3) * 16],
                     scalar1=(1.0 if i == 1 else 0.5), scalar2=None, op0=ALU.mult)
    S.barrier()
    pool.release(m0)

    CW = 2048
    stg = [pool.alloc(CW, BF16) for _ in range(3)]
    chunks = []

    def add_rows(src, dst, n, width, key):
        for r_ in range(n):
            B_W[(key, r_)] = Buf()
            for c0 in range(0, width, CW):
                cw = min(CW, width - c0)
                chunks.append((src[r_, :, c0:c0 + cw], dst[r_, :, c0:c0 + cw], cw, (key, r_)))

    order = []
    for l in range(L):
        order.append((w1[l * 2 * NF:(l * 2 + 1) * NF], W1s[l * 2 * NF:(l * 2 + 1) * NF], NF, 4096, ('w1', l, 0)))
        order.append((w2[l * 32:l * 32 + 16], W2s[l * 32:l * 32 + 16], 16, NF * 128, ('w2', l, 0)))
        order.append((winf[l * NFM:(l + 1) * NFM], WFs[l * NFM:(l + 1) * NFM], NFM, 2048, ('wf', l)))
        order.append((wint[l * 4:(l + 1) * 4], WTs[l * 4:(l + 1) * 4], 4, 8192, ('wt', l)))
        order.append((wout[l * 16:(l + 1) * 16], WOs[l * 16:(l + 1) * 16], 16, 2048, ('wo', l)))
        order.append((w1[(l * 2 + 1) * NF:(l * 2 + 2) * NF], W1s[(l * 2 + 1) * NF:(l * 2 + 2) * NF], NF, 4096, ('w1', l, 1)))
        order.append((w2[l * 32 + 16:l * 32 + 32], W2s[l * 32 + 16:l * 32 + 32], 16, NF * 128, ('w2', l, 1)))
    for o_ in order:
        add_rows(*o_)
    bgs = {'i': 0, 'pend': None, 'emitted': set()}

    def bg_step(n=1):
        for _ in range(n):
            i = bgs['i']
            if i < len(chunks):
                src_, dst_, cw, kr = chunks[i]
                s_ = stg[i % 3]
                S.dma('gpsimd', s_.ap[:, 0:cw], src_, w=[s_], max_dma_last_dim=8192)
                bgs['i'] = i + 1
            else:
                src_ = None
            p_ = bgs['pend']
            if p_ is not None:
                S.dma('gpsimd', p_[1], p_[0].ap[:, 0:p_[2]], r=[p_[0]], w=[T(None, B_W[p_[3]])])
                bgs['emitted'].add(p_[3][0])
                bgs['pend'] = None
            if src_ is not None:
                bgs['pend'] = (s_, dst_, cw, kr)

    def bg_ensure(key):
        last = max(i for i, c_ in enumerate(chunks) if c_[3][0] == key)
        while bgs['i'] <= last or (bgs['pend'] is not None and bgs['pend'][3][0] == key):
            bg_step()

    bg_ensure(('w1', 0, 0))
    bg_ensure(('w2', 0, 0))

    def wbuf(key, r_):
        return T(None, B_W[(key, r_)])

    def prologue(t, XB, HT, SQ, RS, TMP, sc, sh, psb):
        t0 = t * 512
        for c in range(16):
            S.dma('sync', XB[c].ap, XT[c, :, t0:t0 + 512], r=[T(None, B_XT[c][t])], w=[XB[c]])
        for c in range(16):
            sq = SQ[c % 2]
            S.op('scalar', 'activation', r=[XB[c]], w=[sq], out=sq.ap, in_=XB[c].ap, func=AF.Square)
            S.op('tensor', 'matmul', r=[ONEB, sq], w=[psb], out=psb.ap, lhsT=ONEB.ap, rhs=sq.ap,
                 start=(c == 0), stop=(c == 15))
        S.op('scalar', 'activation', r=[psb, EPSC], w=[RS], out=RS.ap, in_=psb.ap, func=AF.Sqrt,
             bias=EPSC.ap, scale=1.0 / D)
        if HT is None:
            S.op('vector', 'reciprocal', r=[RS], w=[RS], out=RS.ap, in_=RS.ap)
            return
        RP = PS[7]
        S.op('vector', 'reciprocal', r=[RS], w=[RP], out=RP.ap, in_=RS.ap)
        for c in range(16):
            tm = TMP[c % 2]
            S.op('vector', 'scalar_tensor_tensor', r=[XB[c], RP, sc], w=[tm], out=tm.ap, in0=XB[c].ap,
                 scalar=sc.ap[:, c:c + 1], in1=RP.ap, op0=ALU.mult, op1=ALU.mult)
            S.op('scalar', 'activation', r=[tm, sh], w=[HT[c]], out=HT[c].ap, in_=tm.ap, func=AF.Identity,
                 bias=sh.ap[:, c:c + 1], scale=1.0)

    def tile_cond(t):
        return 0 if t * 512 < TS else 1

    def ffn_phase(l, i):
        bg_ensure(('w1', l, i))
        bg_ensure(('w2', l, i))
        m0 = pool.mark()
        XB = [pool.alloc(512) for _ in range(16)]
        HT = [pool.alloc(512, BF16) for _ in range(16)]
        AT = [pool.alloc(512, BF16) for _ in range(NF)]
        W1B = [pool.alloc(4096, BF16) for _ in range(3)]
        W2B = [pool.alloc(NF * 128, BF16) for _ in range(2)]
        SQ = [pool.alloc(512, BF16) for _ in range(2)]
        RS = pool.alloc(512)
        TMP = [pool.alloc(512) for _ in range(2)]
        SG = [pool.alloc(512) for _ in range(2)]
        ii = 0 if i == 0 else 2
        n1 = 0
        n2 = 0
        for t in range(NT):
            cond = tile_cond(t)
            t0 = t * 512
            prologue(t, XB, HT, SQ, RS, TMP, der(l, cond, ii, 0), der(l, cond, ii, 1), PS[6])
            for j in range(NF):
                wt = W1B[n1 % 3]
                n1 += 1
                S.dma('sync', wt.ap, W1s[(l * 2 + i) * NF + j, :, :], r=[wbuf(('w1', l, i), j)], w=[wt])
                pg = PS[(j % 2) * 2]
                pu = PS[(j % 2) * 2 + 1]
                for c in range(16):
                    S.op('tensor', 'matmul', r=[wt, HT[c]], w=[pg], out=pg.ap,
                         lhsT=wt.ap[:, c * 256:c * 256 + 128], rhs=HT[c].ap, start=(c == 0), stop=(c == 15))
                for c in range(16):
                    S.op('tensor', 'matmul', r=[wt, HT[c]], w=[pu], out=pu.ap,
                         lhsT=wt.ap[:, c * 256 + 128:c * 256 + 256], rhs=HT[c].ap, start=(c == 0), stop=(c == 15))
                bg_step()
                sg = SG[j % 2]
                S.op('scalar', 'activation', r=[pg], w=[sg], out=sg.ap, in_=pg.ap, func=AF.Silu)
                S.op('vector', 'tensor_tensor', r=[sg, pu], w=[AT[j]], out=AT[j].ap, in0=sg.ap, in1=pu.ap, op=ALU.mult)
            gate = der(l, cond, ii, 2)
            for dj in range(16):
                wt = W2B[n2 % 2]
                n2 += 1
                S.dma('sync', wt.ap, W2s[(l * 2 + i) * 16 + dj, :, :], r=[wbuf(('w2', l, i), dj)], w=[wt])
                bg_step()
                po = PS[4 + dj % 2]
                for c in range(NF):
                    S.op('tensor', 'matmul', r=[wt, AT[c]], w=[po], out=po.ap,
                         lhsT=wt.ap[:, c * 128:(c + 1) * 128], rhs=AT[c].ap, start=(c == 0), stop=(c == NF - 1))
                S.op('vector', 'scalar_tensor_tensor', r=[po, gate, XB[dj]], w=[XB[dj]], out=XB[dj].ap, in0=po.ap,
                     scalar=gate.ap[:, dj:dj + 1], in1=XB[dj].ap, op0=ALU.mult, op1=ALU.add)
                S.dma('gpsimd', XT[dj, :, t0:t0 + 512], XB[dj].ap, r=[XB[dj]], w=[T(None, B_XT[dj][t])])
        S.barrier()
        pool.release(m0)

    def inproj_phase(l):
        bg_ensure(('wf', l))
        bg_ensure(('wt', l))
        m0 = pool.mark()
        XB = [pool.alloc(512) for _ in range(16)]
        HT = [pool.alloc(512, BF16) for _ in range(16)]
        WFB = [pool.alloc(2048, BF16) for _ in range(3)]
        WTB = [pool.alloc(8192, BF16) for _ in range(2)]
        SQ = [pool.alloc(512, BF16) for _ in range(2)]
        RS = pool.alloc(512)
        TMP = [pool.alloc(512) for _ in range(2)]
        RT = [pool.alloc(512) for _ in range(4)]
        STG = [pool.alloc(512, BF16) for _ in range(4)]
        QN = [pool.alloc(512, BF16) for _ in range(2)]
        R1 = [pool.alloc(512) for _ in range(2)]
        R2 = [pool.alloc(512) for _ in range(2)]
        RQ = [pool.alloc(512) for _ in range(2)]
        TST = [pool.alloc(512, BF16) for _ in range(2)]
        OF = [pool.alloc(512) for _ in range(2)]
        OT = [pool.alloc(512) for _ in range(2)]
        nw = 0
        nst_ = 0
        nq = 0
        for t in range(NT):
            cond = tile_cond(t)
            t0 = t * 512
            sample = (cond == 0)
            prologue(t, XB, HT, SQ, RS, TMP, der(l, cond, 1, 0), der(l, cond, 1, 1), PS[6])
            if sample:
                for k in range(4):
                    S.dma('sync', RT[k].ap, rope[k, :, t0:t0 + 512], w=[RT[k]])
            dbg = getattr(cfg, 'dbg', ())
            for g in range(NFM):
                if 'fmx' in dbg and g >= G_GQ:
                    continue
                wt = WFB[nw % 3]
                nw += 1
                S.dma('sync', wt.ap, WFs[l * NFM + g, :, :], r=[wbuf(('wf', l), g)], w=[wt])
                pa = PS[(g % 2) * 2]
                for c in range(16):
                    S.op('tensor', 'matmul', r=[wt, HT[c]], w=[pa], out=pa.ap,
                         lhsT=wt.ap[:, c * 128:(c + 1) * 128], rhs=HT[c].ap, start=(c == 0), stop=(c == 15))
                dstb = fmbufs(g, t0, 512)
                dst = FMS[g, :, t0:t0 + 512]
                st = STG[nst_ % 4]
                nst_ += 1
                if g < G_RG and g < G_RK:
                    S.op('scalar', 'activation', r=[pa], w=[st], out=st.ap, in_=pa.ap, func=AF.Copy)
                    S.dma('gpsimd', dst, st.ap, r=[st], w=dstb)
                elif g < G_RG:
                    S.op('scalar', 'activation', r=[pa], w=[st], out=st.ap, in_=pa.ap, func=AF.Copy,
                         scale=128.0 ** -0.5)
                    S.dma('gpsimd', dst, st.ap, r=[st], w=dstb)
                elif g < G_GQ:
                    S.op('scalar', 'activation', r=[pa], w=[st], out=st.ap, in_=pa.ap, func=AF.Silu)
                    S.dma('gpsimd', dst, st.ap, r=[st], w=dstb)
                else:
                    isg = g < G_DQ
                    k_ = nq % 2
                    nq += 1
                    pb = PS[(g % 2) * 2 + 1]
                    if isg:
                        sq = SQ[k_]
                        S.op('scalar', 'activation', r=[pa], w=[sq], out=sq.ap, in_=pa.ap, func=AF.Square)
                        S.op('tensor', 'matmul', r=[ONEB, sq], w=[pb], out=pb.ap, lhsT=ONEB.ap, rhs=sq.ap,
                             start=True, stop=True)
                        rq_ = RQ[k_]
                        S.op('scalar', 'activation', r=[pb, EPSC], w=[rq_], out=rq_.ap, in_=pb.ap, func=AF.Sqrt,
                             bias=EPSC.ap, scale=1.0 / 128)
                        S.op('vector', 'reciprocal', r=[rq_], w=[rq_], out=rq_.ap, in_=rq_.ap)
                        wcol = l * 10 + (0 if g < G_GK else 1)
                        qf = OF[k_]
                        S.op('vector', 'scalar_tensor_tensor', r=[pa, SM, rq_], w=[qf], out=qf.ap, in0=pa.ap,
                             scalar=SM.ap[:, wcol:wcol + 1], in1=rq_.ap, op0=ALU.mult, op1=ALU.mult)
                    else:
                        qf = OF[k_]
                        S.op('vector', 'tensor_copy', r=[pa], w=[qf], out=qf.ap, in_=pa.ap)
                    if sample:
                        qn = QN[k_]
                        S.op('scalar', 'activation', r=[qf], w=[qn], out=qn.ap, in_=qf.ap, func=AF.Copy)
                        pm = PGB if isg else PDB
                        S.op('tensor', 'matmul', r=[pm, qn], w=[pb], out=pb.ap, lhsT=pm.ap, rhs=qn.ap,
                             start=True, stop=True)
                        ct, stb = (RT[0], RT[1]) if isg else (RT[2], RT[3])
                        r1 = R1[k_]
                        r2 = R2[k_]
                        S.op('gpsimd', 'tensor_tensor', r=[qf, ct], w=[r1], out=r1.ap, in0=qf.ap, in1=ct.ap, op=ALU.mult)
                        S.op('vector', 'tensor_tensor', r=[pb, stb], w=[r2], out=r2.ap, in0=pb.ap, in1=stb.ap, op=ALU.mult)
                        S.op('gpsimd', 'tensor_tensor', r=[r1, r2], w=[st], out=st.ap, in0=r1.ap, in1=r2.ap, op=ALU.add)
                        S.dma('gpsimd', dst, st.ap, r=[st], w=dstb)
                    else:
                        S.op('scalar', 'activation', r=[qf], w=[st], out=st.ap, in_=qf.ap, func=AF.Copy)
                        S.dma('gpsimd', dst, st.ap, r=[st], w=dstb)
                        if (G_GK <= g < G_DQ) or g >= G_DK:
                            for sb in range(4):
                                S.op('tensor', 'transpose', r=[qf, IDF], w=[pb], out=pb.ap[:, sb * 128:(sb + 1) * 128],
                                     in_=qf.ap[:, sb * 128:(sb + 1) * 128], identity=IDF.ap)
                            ot = OT[k_]
                            S.op('vector', 'tensor_copy', r=[pb], w=[ot], out=ot.ap, in_=pb.ap)
                            for s_ in range(NPS):
                                if g < G_DQ:
                                    h = g - G_GK
                                    o_ = ngk[s_ * L + l, :, h * 128:(h + 1) * 128]
                                else:
                                    h = g - G_DK
                                    o_ = ndk[s_ * L + l, :, h * 128:(h + 1) * 128]
                                S.dma('gpsimd', o_.rearrange("(b p) f -> p b f", p=128),
                                      ot.ap[:, s_ * 256:(s_ + 1) * 256].rearrange("p (b f) -> p b f", b=2),
                                      r=[ot], w=[T(None)])
            for blk in range(4):
                if 'tm' in dbg:
                    continue
                wt = WTB[blk % 2]
                width = 256 if blk == 2 else 512
                S.dma('sync', wt.ap, WTs[l * 4 + blk, :, :], r=[wbuf(('wt', l), blk)], w=[wt])
                for sb in range(4):
                    pa = PS[4 + (sb % 2)]
                    for c in range(16):
                        S.op('tensor', 'matmul', r=[wt, HT[c]], w=[pa], out=pa.ap[:, 0:width],
                             lhsT=HT[c].ap[:, sb * 128:(sb + 1) * 128], rhs=wt.ap[:, c * 512:c * 512 + width],
                             start=(c == 0), stop=(c == 15))
                    st = TST[sb % 2]
                    bi = t * 4 + sb
                    if blk == 0:
                        S.op('scalar', 'activation', r=[pa], w=[st], out=st.ap, in_=pa.ap, func=AF.Copy,
                             scale=128.0 ** -0.5)
                        S.dma('gpsimd', KTOK[bi, :, :], st.ap, r=[st], w=[T(None, B_TOK['k'][bi])])
                    elif blk == 1:
                        S.op('vector', 'tensor_copy', r=[pa], w=[st], out=st.ap, in_=pa.ap)
                        S.dma('gpsimd', VTOK[bi, :, :], st.ap, r=[st], w=[T(None, B_TOK['v'][bi])])
                    else:
                        S.op('vector', 'tensor_copy', r=[pa], w=[st], out=st.ap[:, 0:width], in_=pa.ap[:, 0:width])
                        if blk == 2:
                            S.dma('gpsimd', GVT[bi, :, :], st.ap[:, 0:256], r=[st], w=[T(None, B_TOK['gv'][bi])])
                        else:
                            S.dma('gpsimd', DVT[bi, :, :], st.ap, r=[st], w=[T(None, B_TOK['dv'][bi])])
                        if not sample and 'po' not in dbg:
                            of = OF[sb % 2]
                            of = OT[sb % 2]
                            S.op('vector', 'tensor_copy', r=[pa], w=[of], out=of.ap[:, 0:width], in_=pa.ap[:, 0:width])
                            s_ = sb // 2
                            rr = (sb % 2) * 128
                            od = ngv if blk == 2 else ndv
                            S.dma('gpsimd', od[s_ * L + l, rr:rr + 128, :], of.ap[:, 0:width], r=[of], w=[T(None)])
        S.barrier()
        pool.release(m0)

    def ret_phase(l):
        m0 = pool.mark()
        RD = pool.alloc(8)
        S.dma('sync', RD.ap, rdec[:, l * 8:(l + 1) * 8], w=[RD])
        LG = pool.alloc(8)
        S.op('scalar', 'activation', r=[RD], w=[LG], out=LG.ap, in_=RD.ap, func=AF.Exp, scale=-1.0)
        S.op('vector', 'tensor_scalar', r=[LG], w=[LG], out=LG.ap, in0=LG.ap, scalar1=1.0, scalar2=None, op0=ALU.add)
        S.op('scalar', 'activation', r=[LG], w=[LG], out=LG.ap, in_=LG.ap, func=AF.Ln)
        S.op('vector', 'tensor_scalar', r=[LG], w=[LG], out=LG.ap, in0=LG.ap, scalar1=-1.0, scalar2=None, op0=ALU.mult)
        MT = [pool.alloc(128) for _ in range(4)]
        QDF = [pool.alloc(128) for _ in range(4)]
        QDB = [pool.alloc(128) for _ in range(4)]
        KD = pool.alloc(8)
        CD_ = pool.alloc(8)
        e1 = pool.alloc(128)
        e2 = pool.alloc(128)
        for h in range(4):
            lf = LG.ap[:, h:h + 1]
            lb = LG.ap[:, 4 + h:5 + h]
            S.op('scalar', 'activation', r=[CON, LG], w=[e1], out=e1.ap, in_=CON.ap[:, C_POS:C_POS + 128], func=AF.Exp, scale=lf)
            S.op('vector', 'tensor_tensor', r=[e1, CON], w=[e1], out=e1.ap, in0=e1.ap, in1=CON.ap[:, C_MF:C_MF + 128], op=ALU.mult)
            S.op('scalar', 'activation', r=[CON, LG], w=[e2], out=e2.ap, in_=CON.ap[:, C_NEG:C_NEG + 128], func=AF.Exp, scale=lb)
            S.op('vector', 'tensor_tensor', r=[e2, CON], w=[e2], out=e2.ap, in0=e2.ap, in1=CON.ap[:, C_MB:C_MB + 128], op=ALU.mult)
            S.op('vector', 'tensor_tensor', r=[e1, e2], w=[MT[h]], out=MT[h].ap, in0=e1.ap, in1=e2.ap, op=ALU.add)
            S.op('scalar', 'activation', r=[CON, LG], w=[QDF[h]], out=QDF[h].ap, in_=CON.ap[:, C_I1:C_I1 + 128], func=AF.Exp, scale=lf)
            S.op('scalar', 'activation', r=[CON, LG], w=[QDB[h]], out=QDB[h].ap, in_=CON.ap[:, C_IB:C_IB + 128], func=AF.Exp, scale=lb)
            S.op('scalar', 'activation', r=[CON, LG], w=[KD], out=KD.ap[:, h:h + 1], in_=CON.ap[:, C_KF:C_KF + 1], func=AF.Exp, scale=lf)
            S.op('scalar', 'activation', r=[CON, LG], w=[KD], out=KD.ap[:, 4 + h:5 + h], in_=CON.ap[:, C_KB:C_KB + 1], func=AF.Exp, scale=lb)
        S.op('scalar', 'activation', r=[LG], w=[CD_], out=CD_.ap, in_=LG.ap, func=AF.Exp, scale=128.0)

        nchmax = TS // 128
        SBALL = [pool.alloc(512, BF16) for _ in range(nchmax)]
        SF = pool.alloc(512)
        SB_ = pool.alloc(512)
        SFB = pool.alloc(512, BF16)
        KT_ = [pool.alloc(512, BF16) for _ in range(2)]
        VT_ = [pool.alloc(512, BF16) for _ in range(2)]
        KS = [pool.alloc(512, BF16) for _ in range(2)]
        QT_ = [pool.alloc(512, BF16) for _ in range(2)]
        KTT = [pool.alloc(512, BF16) for _ in range(2)]
        RGT = [pool.alloc(512, BF16) for _ in range(2)]
        QF = [pool.alloc(512, BF16) for _ in range(2)]
        QB = [pool.alloc(512, BF16) for _ in range(2)]
        AM = [pool.alloc(128, BF16) for _ in range(4)]
        OB = pool.alloc(512, BF16)
        OSQ = pool.alloc(512, BF16)
        OF_ = pool.alloc(512)
        MEAN = pool.alloc(512)
        VAR = pool.alloc(512)
        OUT = [pool.alloc(512, BF16) for _ in range(2)]

        seqs = [(0, TS, True, None)] + [(TS + s_ * TP, TP, False, s_) for s_ in range(NPS)]
        for (tok0, Tn, has_ctx, sidx) in seqs:
            nch = Tn // 128
            b0 = tok0 // 128
            if has_ctx:
                S.dma('sync', SF.ap, stin[l * 2 + 0, :, :], w=[SF])
                S.dma('sync', SB_.ap, stin[l * 2 + 1, :, :], w=[SB_])
            else:
                S.op('vector', 'memset', w=[SF], ap=SF.ap, constant=0.0)
                S.op('vector', 'memset', w=[SB_], ap=SB_.ap, constant=0.0)
            for n in range(nch - 1, -1, -1):
                bi = b0 + n
                kt = KT_[n % 2]
                vt = VT_[n % 2]
                S.dma('sync', kt.ap, KTOK[bi, :, :], r=[T(None, B_TOK['k'][bi])], w=[kt])
                S.dma('sync', vt.ap, VTOK[bi, :, :], r=[T(None, B_TOK['v'][bi])], w=[vt])
                S.op('scalar', 'activation', r=[SB_], w=[SBALL[n]], out=SBALL[n].ap, in_=SB_.ap, func=AF.Copy)
                ks = KS[n % 2]
                for h in range(4):
                    S.op('vector', 'tensor_scalar', r=[kt, KD], w=[ks], out=ks.ap[:, h * 128:(h + 1) * 128],
                         in0=kt.ap[:, h * 128:(h + 1) * 128], scalar1=KD.ap[:, 4 + h:5 + h], scalar2=None, op0=ALU.mult)
                pu = PS[n % 2]
                for h in range(4):
                    S.op('tensor', 'matmul', r=[ks, vt], w=[pu], out=pu.ap[:, h * 128:(h + 1) * 128],
                         lhsT=ks.ap[:, h * 128:(h + 1) * 128], rhs=vt.ap[:, h * 128:(h + 1) * 128], start=True, stop=True)
                for h in range(4):
                    S.op('vector', 'scalar_tensor_tensor', r=[SB_, CD_, pu], w=[SB_], out=SB_.ap[:, h * 128:(h + 1) * 128],
                         in0=SB_.ap[:, h * 128:(h + 1) * 128], scalar=CD_.ap[:, 4 + h:5 + h],
                         in1=pu.ap[:, h * 128:(h + 1) * 128], op0=ALU.mult, op1=ALU.add)
            if not has_ctx:
                S.dma('gpsimd', nst[(sidx * L + l) * 2 + 1, :, :], SB_.ap, r=[SB_], w=[T(None)])
            for n in range(nch):
                bi = b0 + n
                c0 = tok0 + n * 128
                kt = KT_[n % 2]
                vt = VT_[n % 2]
                qt = QT_[n % 2]
                ktt = KTT[n % 2]
                rgt = RGT[n % 2]
                S.dma('sync', kt.ap, KTOK[bi, :, :], r=[T(None, B_TOK['k'][bi])], w=[kt])
                S.dma('sync', vt.ap, VTOK[bi, :, :], r=[T(None, B_TOK['v'][bi])], w=[vt])
                S.dma('sync', qt.ap.rearrange("p (h t) -> p h t", h=4),
                      FMS[G_RQ:G_RQ + 4, :, c0:c0 + 128].rearrange("h p t -> p h t"),
                      r=[T(None, B_FMS[G_RQ + h][bi]) for h in range(4)], w=[qt])
                S.dma('sync', ktt.ap.rearrange("p (h t) -> p h t", h=4),
                      FMS[G_RK:G_RK + 4, :, c0:c0 + 128].rearrange("h p t -> p h t"),
                      r=[T(None, B_FMS[G_RK + h][bi]) for h in range(4)], w=[ktt])
                S.dma('sync', rgt.ap.rearrange("p (h t) -> p h t", h=4),
                      FMS[G_RG:G_RG + 4, :, c0:c0 + 128].rearrange("h p t -> p h t"),
                      r=[T(None, B_FMS[G_RG + h][bi]) for h in range(4)], w=[rgt])
                S.op('scalar', 'activation', r=[SF], w=[SFB], out=SFB.ap, in_=SF.ap, func=AF.Copy)
                qf = QF[n % 2]
                qb = QB[n % 2]
                pa = PS[2 + (n % 2)]
                po = PS[4 + (n % 2)]
                for h in range(4):
                    hs = slice(h * 128, (h + 1) * 128)
                    S.op('gpsimd', 'tensor_tensor', r=[qt, QDF[h]], w=[qf], out=qf.ap[:, hs], in0=qt.ap[:, hs], in1=QDF[h].ap, op=ALU.mult)
                    S.op('gpsimd', 'tensor_tensor', r=[qt, QDB[h]], w=[qb], out=qb.ap[:, hs], in0=qt.ap[:, hs], in1=QDB[h].ap, op=ALU.mult)
                for h in range(4):
                    hs = slice(h * 128, (h + 1) * 128)
                    S.op('tensor', 'matmul', r=[ktt, qt], w=[pa], out=pa.ap[:, hs], lhsT=ktt.ap[:, hs], rhs=qt.ap[:, hs],
                         start=True, stop=True)
                for h in range(4):
                    hs = slice(h * 128, (h + 1) * 128)
                    S.op('vector', 'tensor_tensor', r=[pa, MT[h]], w=[AM[h]], out=AM[h].ap, in0=pa.ap[:, hs], in1=MT[h].ap, op=ALU.mult)
                for h in range(4):
                    hs = slice(h * 128, (h + 1) * 128)
                    S.op('tensor', 'matmul', r=[vt, AM[h]], w=[po], out=po.ap[:, hs], lhsT=vt.ap[:, hs], rhs=AM[h].ap, start=True, stop=False)
                    S.op('tensor', 'matmul', r=[SFB, qf], w=[po], out=po.ap[:, hs], lhsT=SFB.ap[:, hs], rhs=qf.ap[:, hs], start=False, stop=False)
                    S.op('tensor', 'matmul', r=[SBALL[n], qb], w=[po], out=po.ap[:, hs], lhsT=SBALL[n].ap[:, hs], rhs=qb.ap[:, hs], start=False, stop=True)
                ks = KS[n % 2]
                for h in range(4):
                    hs = slice(h * 128, (h + 1) * 128)
                    S.op('vector', 'tensor_scalar', r=[kt, KD], w=[ks], out=ks.ap[:, hs], in0=kt.ap[:, hs],
                         scalar1=KD.ap[:, h:h + 1], scalar2=None, op0=ALU.mult)
                pu = PS[n % 2]
                for h in range(4):
                    hs = slice(h * 128, (h + 1) * 128)
                    S.op('tensor', 'matmul', r=[ks, vt], w=[pu], out=pu.ap[:, hs], lhsT=ks.ap[:, hs], rhs=vt.ap[:, hs], start=True, stop=True)
                for h in range(4):
                    hs = slice(h * 128, (h + 1) * 128)
                    S.op('vector', 'scalar_tensor_tensor', r=[SF, CD_, pu], w=[SF], out=SF.ap[:, hs], in0=SF.ap[:, hs],
                         scalar=CD_.ap[:, h:h + 1], in1=pu.ap[:, hs], op0=ALU.mult, op1=ALU.add)
                S.op('scalar', 'activation', r=[po], w=[OB], out=OB.ap, in_=po.ap, func=AF.Copy)
                S.op('scalar', 'activation', r=[po], w=[OSQ], out=OSQ.ap, in_=po.ap, func=AF.Square)
                S.op('vector', 'tensor_copy', r=[po], w=[OF_], out=OF_.ap, in_=po.ap)
                pm = PS[6]
                pv = PS[7]
                S.op('tensor', 'matmul', r=[ONEB, OB], w=[pm], out=pm.ap, lhsT=ONEB.ap, rhs=OB.ap, start=True, stop=True)
                S.op('tensor', 'matmul', r=[ONEB, OSQ], w=[pv], out=pv.ap, lhsT=ONEB.ap, rhs=OSQ.ap, start=True, stop=True)
                S.op('scalar', 'activation', r=[pm], w=[MEAN], out=MEAN.ap, in_=pm.ap, func=AF.Copy, scale=1.0 / 128)
                S.op('vector', 'tensor_tensor', r=[MEAN], w=[VAR], out=VAR.ap, in0=MEAN.ap, in1=MEAN.ap, op=ALU.mult)
                S.op('vector', 'scalar_tensor_tensor', r=[pv, VAR], w=[VAR], out=VAR.ap, in0=pv.ap, scalar=1.0 / 128,
                     in1=VAR.ap, op0=ALU.mult, op1=ALU.subtract)
                S.op('vector', 'tensor_scalar', r=[VAR], w=[VAR], out=VAR.ap, in0=VAR.ap, scalar1=0.0, scalar2=None, op0=ALU.max)
                S.op('scalar', 'activation', r=[VAR, EPSC], w=[VAR], out=VAR.ap, in_=VAR.ap, func=AF.Sqrt, bias=EPSC.ap, scale=1.0)
                S.op('vector', 'reciprocal', r=[VAR], w=[VAR], out=VAR.ap, in_=VAR.ap)
                S.op('vector', 'tensor_tensor', r=[OF_, MEAN], w=[OF_], out=OF_.ap, in0=OF_.ap, in1=MEAN.ap, op=ALU.subtract)
                S.op('vector', 'tensor_tensor', r=[OF_, VAR], w=[OF_], out=OF_.ap, in0=OF_.ap, in1=VAR.ap, op=ALU.mult)
                ot = OUT[n % 2]
                for h in range(4):
                    hs = slice(h * 128, (h + 1) * 128)
                    S.op('vector', 'scalar_tensor_tensor', r=[OF_, SM, rgt], w=[ot], out=ot.ap[:, hs], in0=OF_.ap[:, hs],
                         scalar=SM.ap[:, l * 10 + 2 + h:l * 10 + 3 + h], in1=rgt.ap[:, hs], op0=ALU.mult, op1=ALU.mult)
                S.dma('gpsimd', MIX[0:4, :, c0:c0 + 128].rearrange("h p t -> p h t"),
                      ot.ap.rearrange("p (h t) -> p h t", h=4), r=[ot], w=[T(None, B_MIX[h][bi]) for h in range(4)])
            if not has_ctx:
                S.dma('gpsimd', nst[(sidx * L + l) * 2 + 0, :, :], SF.ap, r=[SF], w=[T(None)])
        S.barrier()
        pool.release(m0)

    def attn_phase(l, diff):
        m0 = pool.mark()
        nkcmax = (TS + PAST) // 128
        KB_ = [pool.alloc(TS + PAST, BF16) for _ in range(2)]
        VB_ = [pool.alloc(nkcmax * 128, BF16) for _ in range(2)]
        QB_ = [pool.alloc(1024, BF16) for _ in range(2)]
        PT = [pool.alloc(512, BF16) for _ in range(4)]
        RC = [pool.alloc(512) for _ in range(2)]
        ACC = [pool.alloc(512) for _ in range(2)]
        O1 = pool.alloc(512)
        O2 = pool.alloc(512)
        OSQ = pool.alloc(512, BF16)
        RSD = pool.alloc(512)
        OUT = [pool.alloc(512, BF16) for _ in range(2)]
        LAM = pool.alloc(4)
        lam_init = 0.8 - 0.6 * math.exp(-0.3 * l)
        if diff:
            DL = pool.alloc(256)
            S.dma('sync', DL.ap, dlam[:, l * 256:(l + 1) * 256], w=[DL])
            PR = pool.alloc(128)
            S.op('vector', 'tensor_tensor', r=[DL], w=[PR], out=PR.ap.rearrange("p (a f) -> p a f", a=2),
                 in0=DL.ap.rearrange("p (a b f) -> p a b f", a=2, b=2)[:, :, 0, :],
                 in1=DL.ap.rearrange("p (a b f) -> p a b f", a=2, b=2)[:, :, 1, :], op=ALU.mult)
            S.op('vector', 'reduce_sum', r=[PR], w=[LAM], out=LAM.ap[:, 0:2], in_=PR.ap.rearrange("p (a f) -> p a f", a=2), axis=AX.X)
            S.op('scalar', 'activation', r=[LAM], w=[LAM], out=LAM.ap[:, 0:2], in_=LAM.ap[:, 0:2], func=AF.Exp)
            S.op('vector', 'tensor_tensor', r=[LAM], w=[LAM], out=LAM.ap[:, 2:3], in0=LAM.ap[:, 1:2], in1=LAM.ap[:, 0:1], op=ALU.subtract)
            S.op('vector', 'tensor_scalar', r=[LAM], w=[LAM], out=LAM.ap[:, 2:3], in0=LAM.ap[:, 2:3], scalar1=-lam_init, scalar2=None, op0=ALU.add)
            DNW = pool.alloc(4)
            S.op('vector', 'tensor_scalar', r=[SM], w=[DNW], out=DNW.ap, in0=SM.ap[:, l * 10 + 6:l * 10 + 10],
                 scalar1=1.0 - lam_init, scalar2=None, op0=ALU.mult)
        nkv = 4 if diff else 2
        scale = (64.0 ** -0.5) if diff else (128.0 ** -0.5)
        seqs = [(0, TS, True, 512)] + [(TS + s_ * TP, TP, False, 256) for s_ in range(NPS)]
        nb_ = 0
        nq_ = 0
        npt = 0
        nout = 0
        for (tok0, Tn, has_ctx, TQ) in seqs:
            nk_own = Tn // 128
            nkc = nk_own + (PAST // 128 if has_ctx else 0)
            b0 = tok0 // 128
            for g in range(nkv):
                kb = KB_[nb_ % 2]
                vb = VB_[nb_ % 2]
                nb_ += 1
                gk = (G_DK if diff else G_GK) + g
                S.dma('sync', kb.ap[:, 0:Tn], FMS[gk, :, tok0:tok0 + Tn],
                      r=[T(None, B_FMS[gk][b]) for b in range(b0, b0 + nk_own)], w=[kb])
                vsrc = DVT if diff else GVT
                S.dma('sync', vb.ap[:, 0:nk_own * 128].rearrange("p (k e) -> p k e", e=128),
                      vsrc[b0:b0 + nk_own, :, g * 128:(g + 1) * 128].rearrange("k p e -> p k e"),
                      r=[T(None, B_TOK['dv' if diff else 'gv'][b]) for b in range(b0, b0 + nk_own)], w=[vb])
                if has_ctx:
                    ck = cdk if diff else cgk
                    cvv = cdv if diff else cgv
                    S.dma('gpsimd', kb.ap[:, Tn:Tn + PAST], ck[l * nkv + g, :, :], w=[kb])
                    for kc in range(PAST // 128):
                        S.dma('gpsimd', vb.ap[:, (nk_own + kc) * 128:(nk_own + kc + 1) * 128],
                              cvv[(l * nkv + g) * 2 + kc, :, :], w=[vb])
                if diff:
                    units = [(g, None)]
                else:
                    units = [(4 * g + 2 * pr, 4 * g + 2 * pr + 1) for pr in range(2)]
                for q0 in range(0, Tn, TQ):
                    qa = tok0 + q0
                    for un in units:
                        qb = QB_[nq_ % 2]
                        nq_ += 1
                        if diff:
                            gq = G_DQ + g
                            S.dma('sync', qb.ap[:, 0:TQ], FMS[gq, :, qa:qa + TQ], r=fmbufs(gq, qa, TQ), w=[qb])
                        else:
                            for k_, hq in enumerate(un):
                                S.dma('sync', qb.ap[:, k_ * 512:k_ * 512 + TQ], FMS[G_GQ + hq, :, qa:qa + TQ],
                                      r=fmbufs(G_GQ + hq, qa, TQ), w=[qb])
                        for kc in range(nkc):
                            for k_ in range(2):
                                ps_s = PS[4 + (npt % 4)]
                                pt = PT[npt % 4]
                                npt += 1
                                if diff:
                                    S.op('tensor', 'matmul', r=[kb, qb], w=[ps_s], out=ps_s.ap[:, 0:TQ],
                                         lhsT=kb.ap[k_ * 64:(k_ + 1) * 64, kc * 128:(kc + 1) * 128],
                                         rhs=qb.ap[k_ * 64:(k_ + 1) * 64, 0:TQ], start=True, stop=True)
                                else:
                                    S.op('tensor', 'matmul', r=[kb, qb], w=[ps_s], out=ps_s.ap[:, 0:TQ],
                                         lhsT=kb.ap[:, kc * 128:(kc + 1) * 128], rhs=qb.ap[:, k_ * 512:k_ * 512 + TQ],
                                         start=True, stop=True)
                                S.op('scalar', 'activation', r=[ps_s], w=[pt], out=pt.ap[:, 0:TQ], in_=ps_s.ap[:, 0:TQ],
                                     func=AF.Exp, scale=scale)
                                S.op('tensor', 'matmul', r=[vb, pt], w=[PS[k_]], out=PS[k_].ap[:, 0:TQ],
                                     lhsT=vb.ap[:, kc * 128:(kc + 1) * 128], rhs=pt.ap[:, 0:TQ],
                                     start=(kc == 0), stop=(kc == nkc - 1))
                                pacc = PS[2 + k_]
                                if kc == 0:
                                    S.op('vector', 'tensor_copy', r=[pt], w=[pacc], out=pacc.ap[:, 0:TQ], in_=pt.ap[:, 0:TQ])
                                else:
                                    S.op('vector', 'tensor_tensor', r=[pt, pacc], w=[pacc], out=pacc.ap[:, 0:TQ],
                                         in0=pacc.ap[:, 0:TQ], in1=pt.ap[:, 0:TQ], op=ALU.add)
                        for k_ in range(2):
                            S.op('scalar', 'activation', r=[PS[2 + k_]], w=[ACC[k_]], out=ACC[k_].ap[:, 0:TQ],
                                 in_=PS[2 + k_].ap[:, 0:TQ], func=AF.Identity)
                            S.op('tensor', 'matmul', r=[CON, ACC[k_]], w=[PS[2 + k_]], out=PS[2 + k_].ap[:, 0:TQ],
                                 lhsT=CON.ap[:, C_ONE:C_ONE + 128], rhs=ACC[k_].ap[:, 0:TQ], start=True, stop=True)
                        for k_ in range(2):
                            S.op('vector', 'reciprocal', r=[PS[2 + k_]], w=[RC[k_]], out=RC[k_].ap[:, 0:TQ], in_=PS[2 + k_].ap[:, 0:TQ])
                        bl = range(qa // 128, (qa + TQ) // 128)
                        if not diff:
                            for k_, hq in enumerate(un):
                                ot = OUT[nout % 2]
                                nout += 1
                                S.op('vector', 'tensor_tensor', r=[PS[k_], RC[k_]], w=[ot], out=ot.ap[:, 0:TQ],
                                     in0=PS[k_].ap[:, 0:TQ], in1=RC[k_].ap[:, 0:TQ], op=ALU.mult)
                                S.dma('gpsimd', MIX[4 + hq, :, qa:qa + TQ], ot.ap[:, 0:TQ], r=[ot],
                                      w=[T(None, B_MIX[4 + hq][b]) for b in bl])
                        else:
                            S.op('vector', 'tensor_tensor', r=[PS[0], RC[0]], w=[O1], out=O1.ap[:, 0:TQ],
                                 in0=PS[0].ap[:, 0:TQ], in1=RC[0].ap[:, 0:TQ], op=ALU.mult)
                            S.op('vector', 'tensor_tensor', r=[PS[1], RC[1]], w=[O2], out=O2.ap[:, 0:TQ],
                                 in0=PS[1].ap[:, 0:TQ], in1=RC[1].ap[:, 0:TQ], op=ALU.mult)
                            S.op('vector', 'scalar_tensor_tensor', r=[O2, LAM, O1], w=[O1], out=O1.ap[:, 0:TQ], in0=O2.ap[:, 0:TQ],
                                 scalar=LAM.ap[:, 2:3], in1=O1.ap[:, 0:TQ], op0=ALU.mult, op1=ALU.add)
                            S.op('scalar', 'activation', r=[O1], w=[OSQ], out=OSQ.ap[:, 0:TQ], in_=O1.ap[:, 0:TQ], func=AF.Square)
                            S.op('tensor', 'matmul', r=[ONEB, OSQ], w=[PS[2]], out=PS[2].ap[:, 0:TQ], lhsT=ONEB.ap,
                                 rhs=OSQ.ap[:, 0:TQ], start=True, stop=True)
                            S.op('scalar', 'activation', r=[PS[2], EPSC], w=[RSD], out=RSD.ap[:, 0:TQ], in_=PS[2].ap[:, 0:TQ],
                                 func=AF.Sqrt, bias=EPSC.ap, scale=1.0 / 128)
                            S.op('vector', 'reciprocal', r=[RSD], w=[RSD], out=RSD.ap[:, 0:TQ], in_=RSD.ap[:, 0:TQ])
                            ot = OUT[nout % 2]
                            nout += 1
                            S.op('vector', 'scalar_tensor_tensor', r=[O1, DNW, RSD], w=[ot], out=ot.ap[:, 0:TQ], in0=O1.ap[:, 0:TQ],
                                 scalar=DNW.ap[:, g:g + 1], in1=RSD.ap[:, 0:TQ], op0=ALU.mult, op1=ALU.mult)
                            S.dma('gpsimd', MIX[12 + g, :, qa:qa + TQ], ot.ap[:, 0:TQ], r=[ot],
                                  w=[T(None, B_MIX[12 + g][b]) for b in bl])
        S.barrier()
        pool.release(m0)

    def wout_phase(l):
        bg_ensure(('wo', l))
        m0 = pool.mark()
        MX = [pool.alloc(512, BF16) for _ in range(32)]
        WB = [pool.alloc(2048, BF16) for _ in range(3)]
        XC = [pool.alloc(512) for _ in range(4)]
        nw = 0
        for t in range(NT):
            cond = tile_cond(t)
            t0 = t * 512
            mx = MX[(t % 2) * 16:(t % 2) * 16 + 16]
            for c in range(16):
                S.dma('sync', mx[c].ap, MIX[c, :, t0:t0 + 512],
                      r=[T(None, B_MIX[c][b]) for b in range(t * 4, t * 4 + 4)], w=[mx[c]])
            gate = der(l, cond, 1, 2)
            for dj in range(16):
                wt = WB[nw % 3]
                xc = XC[nw % 4]
                nw += 1
                S.dma('sync', wt.ap, WOs[l * 16 + dj, :, :], r=[wbuf(('wo', l), dj)], w=[wt])
                S.dma('sync', xc.ap, XT[dj, :, t0:t0 + 512], r=[T(None, B_XT[dj][t])], w=[xc])
                po = PS[dj % 4]
                for c in range(16):
                    S.op('tensor', 'matmul', r=[wt, mx[c]], w=[po], out=po.ap, lhsT=wt.ap[:, c * 128:(c + 1) * 128],
                         rhs=mx[c].ap, start=(c == 0), stop=(c == 15))
                S.op('vector', 'scalar_tensor_tensor', r=[po, gate, xc], w=[xc], out=xc.ap, in0=po.ap,
                     scalar=gate.ap[:, dj:dj + 1], in1=xc.ap, op0=ALU.mult, op1=ALU.add)
                S.dma('gpsimd', XT[dj, :, t0:t0 + 512], xc.ap, r=[xc], w=[T(None, B_XT[dj][t])])
        S.barrier()
        pool.release(m0)

    def final_phase():
        m0 = pool.mark()
        XB = [pool.alloc(512) for _ in range(16)]
        SQ = [pool.alloc(512, BF16) for _ in range(2)]
        RS = pool.alloc(512)
        YT = [pool.alloc(512) for _ in range(2)]
        YO = [pool.alloc(2048) for _ in range(4)]
        ny = 0
        for t in range(NT):
            prologue(t, XB, None, SQ, RS, None, None, None, PS[6])
            for c in range(16):
                yt = YT[c % 2]
                S.op('vector', 'scalar_tensor_tensor', r=[XB[c], FNW, RS], w=[yt], out=yt.ap, in0=XB[c].ap,
                     scalar=FNW.ap[:, c:c + 1], in1=RS.ap, op0=ALU.mult, op1=ALU.mult)
                pb = PS[c % 4]
                for sb in range(4):
                    S.op('tensor', 'transpose', r=[yt, IDF], w=[pb], out=pb.ap[:, sb * 128:(sb + 1) * 128],
                         in_=yt.ap[:, sb * 128:(sb + 1) * 128], identity=IDF.ap)
                for sb in range(4):
                    yo = YO[sb]
                    S.op('vector', 'tensor_copy', r=[pb], w=[yo], out=yo.ap[:, c * 128:(c + 1) * 128], in_=pb.ap[:, sb * 128:(sb + 1) * 128])
            for sb in range(4):
                r0 = t * 512 + sb * 128
                S.dma('sync', y[r0:r0 + 128, :], YO[sb].ap, r=[YO[sb]], w=[T(None)])
        S.barrier()
        pool.release(m0)

    plan = []
    for l in range(L):
        plan += [(ffn_phase, (l, 0)), (inproj_phase, (l,)), (ret_phase, (l,)), (attn_phase, (l, False)),
                 (attn_phase, (l, True)), (wout_phase, (l,)), (ffn_phase, (l, 1))]
    plan.append((final_phase, ()))
    for fn_, a_ in plan[:getattr(cfg, 'nphase', 1000)]:
        fn_(*a_)

    with nc.Block() as block:
        S.emit(block)
    es.close()
    return nc


def _prep_shared(cfg, inp):
    L, NF = cfg.L, cfg.NF
    f = np.float32
    out = {}
    wm = np.asarray(inp['w_mod'], f)
    out['wmod'] = np.ascontiguousarray(
        wm.reshape(L, 16, 128, 72, 2, 128).transpose(0, 3, 2, 4, 1, 5)).reshape(L * 72, 128, 4096)
    del wm
    out['bmod'] = np.ascontiguousarray(np.asarray(inp['b_mod'], f).reshape(L, 144, 128).transpose(2, 0, 1)).reshape(128, L * 144)
    out['normw'] = np.ascontiguousarray(np.asarray(inp['norm_w'], f).reshape(L, 3, 16, 128).transpose(3, 0, 1, 2)).reshape(128, L * 48)
    out['fnw'] = np.ascontiguousarray(np.asarray(inp['final_norm_w'], f).reshape(16, 128).T)
    w1 = np.asarray(inp['ffn_w_in'], f)
    out['w1'] = np.ascontiguousarray(
        w1.reshape(L, 2, 16, 128, 2, NF, 128).transpose(0, 1, 5, 3, 2, 4, 6)).reshape(L * 2 * NF, 128, 4096)
    del w1
    w2 = np.asarray(inp['ffn_w_out'], f)
    out['w2'] = np.ascontiguousarray(
        w2.reshape(L, 2, NF, 128, 16, 128).transpose(0, 1, 4, 3, 2, 5)).reshape(L * 2 * 16, 128, NF * 128)
    del w2
    win = np.asarray(inp['w_in'], f)
    fm, tm = _win_cols()
    wf = np.stack([win[:, :, cols] for cols in fm], 1)
    out['winf'] = np.ascontiguousarray(wf.reshape(L, NFM, 16, 128, 128).transpose(0, 1, 3, 2, 4)).reshape(L * NFM, 128, 2048)
    wt = np.zeros((L, 4, 128, 16, 512), f)
    for bi, cols in enumerate(tm):
        blk = win[:, :, cols].reshape(L, 16, 128, len(cols)).transpose(0, 2, 1, 3)
        wt[:, bi, :, :, 0:len(cols)] = blk
    out['wint'] = wt.reshape(L * 4, 128, 8192)
    wo = np.asarray(inp['w_out'], f)
    out['wout'] = np.ascontiguousarray(wo.reshape(L, 16, 128, 16, 128).transpose(0, 3, 2, 1, 4)).reshape(L * 16, 128, 2048)
    p128 = _perm_deint(128)
    qkn = np.asarray(inp['gqa_qk_norm'], f)[:, :, p128]
    gnw = np.asarray(inp['ret_gn_w'], f)
    dnw = np.asarray(inp['diff_norm_w'], f)
    sm = np.concatenate([qkn, gnw, dnw], 1)
    out['smalls'] = np.ascontiguousarray(sm.transpose(2, 0, 1)).reshape(128, L * 10)
    out['rdec'] = np.ascontiguousarray(np.broadcast_to(np.asarray(inp['ret_decay'], f).reshape(1, L * 8), (128, L * 8)))
    out['dlam'] = np.ascontiguousarray(np.broadcast_to(np.asarray(inp['diff_lambda'], f).reshape(1, L * 256), (128, L * 256)))
    out['rope'] = _rope_tables(cfg.TS)
    out['consts'] = _consts()
    return out


def _prep_core(cfg, inp, b):
    L = cfg.L
    f = np.float32
    p128 = _perm_deint(128)
    p64 = _perm_deint(64)
    d = {}
    xs = np.asarray(inp['x_sample'], f)[b]
    xp = np.asarray(inp['x_prompt'], f)[2 * b:2 * b + 2].reshape(-1, cfg.D)
    d['xin'] = np.concatenate([xs, xp], 0)
    cv = np.stack([np.asarray(inp['c'], f)[b].reshape(16, 128).T, np.asarray(inp['c_ctx'], f).reshape(16, 128).T], 1)
    d['cv'] = np.ascontiguousarray(cv).reshape(128, 32)
    st = np.asarray(inp['state_ret'], f)[b]
    d['stin'] = np.ascontiguousarray(st.transpose(0, 1, 3, 2, 4)).reshape(L * 2, 128, 512)
    gk = np.asarray(inp['cache_gqa_k'], f)[b][:, :, :, p128]
    d['cgk'] = np.ascontiguousarray(gk.transpose(0, 2, 3, 1)).reshape(L * 2, 128, cfg.PAST)
    gv = np.asarray(inp['cache_gqa_v'], f)[b]
    d['cgv'] = np.ascontiguousarray(gv.reshape(L, 2, 128, 2, 128).transpose(0, 3, 1, 2, 4)).reshape(L * 4, 128, 128)
    dk = np.asarray(inp['cache_diff_k'], f)[b]
    dk = dk.reshape(L, cfg.PAST, 4, 2, 64)[..., p64].reshape(L, cfg.PAST, 4, 128)
    d['cdk'] = np.ascontiguousarray(dk.transpose(0, 2, 3, 1)).reshape(L * 4, 128, cfg.PAST)
    dv = np.asarray(inp['cache_diff_v'], f)[b]
    d['cdv'] = np.ascontiguousarray(dv.reshape(L, 2, 128, 4, 128).transpose(0, 3, 1, 2, 4)).reshape(L * 8, 128, 128)
    return d


def run(cfg, inp, n_cores):
    nc = build(cfg)
    shared = _prep_shared(cfg, inp)
    in_maps = []
    for b in range(n_cores):
        m = dict(shared)
        m.update(_prep_core(cfg, inp, b))
        in_maps.append(m)
    res = run_bass_kernel_spmd(nc, in_maps, core_ids=list(range(n_cores)))
    L = cfg.L
    inv128 = np.argsort(_perm_deint(128))
    inv64 = np.argsort(_perm_deint(64))
    ys, yp, st, gk, gv, dk, dv = [], [], [], [], [], [], []
    for b in range(n_cores):
        r = res.results[b]
        yy = np.asarray(r['y'])
        ys.append(yy[:cfg.TS])
        yp.append(yy[cfg.TS:].reshape(2, cfg.TP, cfg.D))
        s_ = np.asarray(r['nst']).reshape(2, L, 2, 128, 4, 128).transpose(0, 1, 2, 4, 3, 5)
        st.append(s_)
        k_ = np.asarray(r['ngk']).reshape(2, L, 256, 2, 128)[..., inv128]
        gk.append(k_)
        gv.append(np.asarray(r['ngv']).reshape(2, L, 256, 2, 128))
        k2 = np.asarray(r['ndk']).reshape(2, L, 256, 4, 2, 64)[..., inv64].reshape(2, L, 256, 4, 128)
        dk.append(k2)
        dv.append(np.asarray(r['ndv']).reshape(2, L, 256, 4, 128))
    cat = lambda xs_: np.ascontiguousarray(np.concatenate(xs_, 0)).astype(np.float32)
    return (cat(yp), np.ascontiguousarray(np.stack(ys, 0)).astype(np.float32), cat(st), cat(gk), cat(gv), cat(dk), cat(dv))


def kernel(**inputs):
    cfg = Cfg()
    return run(cfg, inputs, 8)
```

```python
import math
import numpy as np
import ml_dtypes
import concourse.bass as bass
import concourse.mybir as mybir
from concourse.bass_utils import run_bass_kernel_spmd

F32 = mybir.dt.float32
BF16 = mybir.dt.bfloat16
AF = mybir.ActivationFunctionType
ALU = mybir.AluOpType
AX = mybir.AxisListType
ENGS = ('tensor', 'vector', 'scalar', 'gpsimd', 'sync')
NDS = 40
EPS = 1e-6


class Cfg:
    def __init__(self, TS=4096, DFF=5632, L=2):
        self.D = 2048; self.NCH = 16
        self.DFF = DFF; self.NF = DFF // 128
        self.TS = TS; self.TP = 256; self.NPS = 2; self.PAST = 256
        self.L = L
        self.NTOK = TS + self.NPS * self.TP
        self.NT = self.NTOK // 512
        self.NB = self.NTOK // 128


class Buf:
    __slots__ = ('w', 'r')

    def __init__(self):
        self.w = None
        self.r = {}


class T:
    __slots__ = ('ap', 'b')

    def __init__(self, ap, b=None):
        self.ap = ap
        self.b = b if b is not None else Buf()

    def v(self, ap):
        return T(ap, self.b)


class Sched:
    def __init__(self, nc):
        self.nc = nc
        self.q = {e: [] for e in ENGS}
        self.cnt = {e: 0 for e in ENGS}
        self.sems = {}
        for e in ENGS:
            self.sems['p_' + e] = nc.alloc_semaphore(name='p_' + e)
        for i in range(NDS):
            self.sems[('d', i)] = nc.alloc_semaphore(name='d%d' % i)
        self.duse = [0] * NDS
        self.dpool = {'sync': list(range(0, 24)), 'gpsimd': list(range(24, NDS))}
        self.dnext = {'sync': 0, 'gpsimd': 0}
        self.waited = {e: {} for e in ENGS}

    def _wait(self, eng, deps):
        wd = self.waited[eng]
        for key, val in deps:
            if wd.get(key, 0) >= val:
                continue
            wd[key] = val
            self.q[eng].append(('wait', key, val))

    def _deps(self, eng, r, w):
        deps = []
        for t in r:
            if t.b.w is not None:
                deps.append(t.b.w)
        for t in w:
            if t.b.w is not None:
                deps.append(t.b.w)
            deps.extend(t.b.r.items())
        if eng == 'tensor':
            deps = [d for d in deps if d[0] != 'p_tensor']
        return deps

    def _mark(self, tok, r, w):
        for t in r:
            if t.b.r.get(tok[0], 0) < tok[1]:
                t.b.r[tok[0]] = tok[1]
        for t in w:
            t.b.w = tok
            t.b.r = {}

    def op(self, eng, meth, r=(), w=(), **kw):
        self._wait(eng, self._deps(eng, r, w))
        self.cnt[eng] += 1
        tok = ('p_' + eng, self.cnt[eng])
        self.q[eng].append(('op', meth, kw))
        self._mark(tok, r, w)

    def dma(self, eng, out, in_, r=(), w=(), **kw):
        pl = self.dpool[eng]
        i = pl[self.dnext[eng] % len(pl)]
        self.dnext[eng] += 1
        deps = self._deps(eng, r, w)
        if self.duse[i] > 0:
            deps.append((('d', i), 16 * self.duse[i]))
        self._wait(eng, deps)
        self.duse[i] += 1
        tok = (('d', i), 16 * self.duse[i])
        self.q[eng].append(('dma', out, in_, kw, ('d', i)))
        self._mark(tok, r, w)

    def barrier(self):
        allv = [('p_' + e, self.cnt[e]) for e in ENGS if self.cnt[e] > 0]
        allv += [(('d', i), 16 * self.duse[i]) for i in range(NDS) if self.duse[i] > 0]
        for e in ENGS:
            self._wait(e, allv)

    def emit(self, block):
        nc = self.nc
        sems = self.sems
        q = self.q

        def run(eng, name):
            psem = sems['p_' + name]
            for it in q[name]:
                if it[0] == 'wait':
                    eng.wait_ge(sems[it[1]], it[2])
                elif it[0] == 'op':
                    getattr(eng, it[1])(**it[2]).then_inc(psem, 1)
                else:
                    eng.dma_start(out=it[1], in_=it[2], **it[3]).then_inc(sems[it[4]], 16)

        @block.tensor
        def _(e):
            run(e, 'tensor')

        @block.vector
        def _(e):
            run(e, 'vector')

        @block.scalar
        def _(e):
            run(e, 'scalar')

        @block.gpsimd
        def _(e):
            run(e, 'gpsimd')

        @block.sync
        def _(e):
            run(e, 'sync')


class Pool:
    def __init__(self, ap, ncols):
        self.ap = ap
        self.n = ncols
        self.off = 0

    def mark(self):
        return self.off

    def release(self, m):
        self.off = m

    def alloc(self, cols, dt=F32, parts=128):
        n32 = cols if dt == F32 else (cols + 1) // 2
        assert self.off + n32 <= self.n, ("SBUF pool overflow", self.off, n32, self.n)
        v = self.ap[0:parts, self.off:self.off + n32]
        self.off += n32
        if dt != F32:
            v = v.bitcast(dt)
        return T(v)


def _perm_deint(n):
    return np.concatenate([np.arange(0, n, 2), np.arange(1, n, 2)])


def _win_cols():
    sizes = [512, 512, 512, 512, 1024, 256, 256, 512, 512, 512]
    st = np.concatenate([[0], np.cumsum(sizes)])
    rq, rk, rv, rg, gq, gk, gv, dq, dk, dv = [np.arange(st[i], st[i + 1]) for i in range(10)]
    p128 = _perm_deint(128)
    p64 = _perm_deint(64)
    fm = []
    for h in range(4):
        fm.append(rq[h * 128:(h + 1) * 128])
    for h in range(4):
        fm.append(rk[h * 128:(h + 1) * 128])
    for h in range(4):
        fm.append(rg[h * 128:(h + 1) * 128])
    for h in range(8):
        fm.append(gq[h * 128:(h + 1) * 128][p128])
    for h in range(2):
        fm.append(gk[h * 128:(h + 1) * 128][p128])
    for h in range(4):
        c = dq[h * 128:(h + 1) * 128]
        fm.append(np.concatenate([c[0:64][p64], c[64:128][p64]]))
    for h in range(4):
        c = dk[h * 128:(h + 1) * 128]
        fm.append(np.concatenate([c[0:64][p64], c[64:128][p64]]))
    tm = [rk, rv, gv, dv]
    return fm, tm


G_RQ, G_RK, G_RG, G_GQ, G_GK, G_DQ, G_DK, NFM = 0, 4, 8, 12, 20, 22, 26, 30


def _rope_tables(TS):
    rows = TS // 64

    def tab(d):
        axis = d // 2
        inv = 1.0 / (10000.0 ** (np.arange(0, axis, 2, dtype=np.float32) / axis))
        r = np.repeat(np.arange(rows, dtype=np.float32), 64)
        cl = np.tile(np.arange(64, dtype=np.float32), rows)
        ang = np.concatenate([r[:, None] * inv, cl[:, None] * inv], axis=-1).astype(np.float32)
        return np.cos(ang).T.astype(np.float32), np.sin(ang).T.astype(np.float32)

    cg, sg = tab(128)
    cd, sd = tab(64)
    CG = np.concatenate([cg, cg], 0)
    SG = np.concatenate([-sg, sg], 0)
    CD = np.concatenate([cd, cd, cd, cd], 0)
    SD = np.concatenate([-sd, sd, -sd, sd], 0)
    return np.ascontiguousarray(np.stack([CG, SG, CD, SD], 0)).astype(np.float32)


C_ID, C_ONE, C_PG, C_PD, C_POS, C_NEG, C_MF, C_MB, C_I1, C_IB, C_KF, C_KB, NCONST = \
    0, 128, 256, 384, 512, 640, 768, 896, 1024, 1152, 1280, 1281, 1282


def _consts():
    c = np.zeros((128, NCONST), np.float32)
    c[:, C_ID:C_ID + 128] = np.eye(128)
    c[:, C_ONE:C_ONE + 128] = 1.0
    k = np.arange(128)
    pg = np.zeros((128, 128), np.float32)
    pg[(k + 64) % 128, k] = 1.0
    c[:, C_PG:C_PG + 128] = pg
    pd = np.zeros((128, 128), np.float32)
    for m in range(128):
        blk = (m // 64) * 64
        pd[blk + ((m % 64) + 32) % 64, m] = 1.0
    c[:, C_PD:C_PD + 128] = pd
    j = np.arange(128)[:, None].astype(np.float32)
    i = np.arange(128)[None, :].astype(np.float32)
    dif = i - j
    c[:, C_POS:C_POS + 128] = np.maximum(dif, 0)
    c[:, C_NEG:C_NEG + 128] = np.maximum(-dif, 0)
    c[:, C_MF:C_MF + 128] = (dif >= 0)
    c[:, C_MB:C_MB + 128] = (dif <= 0)
    c[:, C_I1:C_I1 + 128] = i + 1.0
    c[:, C_IB:C_IB + 128] = 128.0 - i
    c[:, C_KF] = 127.0 - np.arange(128)
    c[:, C_KB] = np.arange(128)
    return c


def build(cfg):
    nc = bass.Bass("TRN2", target_bir_lowering=False)
    D, NCH, NF, L = cfg.D, cfg.NCH, cfg.NF, cfg.L
    TS, TP, NPS, PAST = cfg.TS, cfg.TP, cfg.NPS, cfg.PAST
    NTOK, NT, NB = cfg.NTOK, cfg.NT, cfg.NB

    def din(name, shape, dt=F32):
        return nc.dram_tensor(name, list(shape), dt, kind="ExternalInput").ap()

    def dout(name, shape):
        return nc.dram_tensor(name, list(shape), F32, kind="ExternalOutput").ap()

    def dscr(name, shape, dt):
        return nc.dram_tensor(name, list(shape), dt, kind="Internal").ap()

    xin = din("xin", [NTOK, D])
    cv_in = din("cv", [128, 32])
    wmod = din("wmod", [L * 72, 128, 4096])
    bmod = din("bmod", [128, L * 144])
    normw = din("normw", [128, L * 48])
    fnw = din("fnw", [128, 16])
    w1 = din("w1", [L * 2 * NF, 128, 4096])
    w2 = din("w2", [L * 2 * 16, 128, NF * 128])
    winf = din("winf", [L * NFM, 128, 2048])
    wint = din("wint", [L * 4, 128, 16 * 512])
    wout = din("wout", [L * 16, 128, 2048])
    smalls = din("smalls", [128, L * 10])
    rdec = din("rdec", [128, L * 8])
    dlam = din("dlam", [128, L * 256])
    stin = din("stin", [L * 2, 128, 512])
    cgk = din("cgk", [L * 2, 128, PAST])
    cgv = din("cgv", [L * 2 * 2, 128, 128])
    cdk = din("cdk", [L * 4, 128, PAST])
    cdv = din("cdv", [L * 4 * 2, 128, 128])
    rope = din("rope", [4, 128, TS])
    consts = din("consts", [128, NCONST])

    y = dout("y", [NTOK, D])
    nst = dout("nst", [NPS * L * 2, 128, 512])
    ngk = dout("ngk", [NPS * L, 256, 256])
    ngv = dout("ngv", [NPS * L, 256, 256])
    ndk = dout("ndk", [NPS * L, 256, 512])
    ndv = dout("ndv", [NPS * L, 256, 512])

    XT = dscr("XT", [16, 128, NTOK], F32)
    W1s = dscr("W1s", [L * 2 * NF, 128, 4096], BF16)
    W2s = dscr("W2s", [L * 2 * 16, 128, NF * 128], BF16)
    WFs = dscr("WFs", [L * NFM, 128, 2048], BF16)
    WTs = dscr("WTs", [L * 4, 128, 16 * 512], BF16)
    WOs = dscr("WOs", [L * 16, 128, 2048], BF16)
    FMS = dscr("FMS", [NFM, 128, NTOK], BF16)
    KTOK = dscr("KTOK", [NB, 128, 512], BF16)
    VTOK = dscr("VTOK", [NB, 128, 512], BF16)
    GVT = dscr("GVT", [NB, 128, 256], BF16)
    DVT = dscr("DVT", [NB, 128, 512], BF16)
    MIX = dscr("MIX", [16, 128, NTOK], BF16)
    B_XT = [[Buf() for _ in range(NT)] for _ in range(16)]
    B_W = {}
    B_FMS = [[Buf() for _ in range(NB)] for _ in range(NFM)]
    B_TOK = {n: [Buf() for _ in range(NB)] for n in ('k', 'v', 'gv', 'dv')}
    B_MIX = [[Buf() for _ in range(NB)] for _ in range(16)]
    RO = T(None)

    S = Sched(nc)
    NPOOL = 44000
    from contextlib import ExitStack
    es = ExitStack()
    pool_t = es.enter_context(nc.sbuf_tensor("pool", [128, NPOOL], F32))
    ps_t = es.enter_context(nc.psum_tensor("ps", [128, 8 * 512], F32))
    pool = Pool(pool_t[:], NPOOL)
    PS = [T(ps_t[:, b * 512:(b + 1) * 512]) for b in range(8)]

    def fmbufs(g, t0, n):
        return [T(None, B_FMS[g][b]) for b in range(t0 // 128, (t0 + n) // 128)]

    CON = pool.alloc(NCONST)
    S.dma('sync', CON.ap, consts[:, :], w=[CON])
    IDF = CON.v(CON.ap[:, C_ID:C_ID + 128])
    CB = pool.alloc(512, BF16)
    S.op('vector', 'tensor_copy', r=[CON], w=[CB], out=CB.ap, in_=CON.ap[:, 0:512])
    IDB = CB.v(CB.ap[:, 0:128]); ONEB = CB.v(CB.ap[:, 128:256])
    PGB = CB.v(CB.ap[:, 256:384]); PDB = CB.v(CB.ap[:, 384:512])
    EPSC = pool.alloc(1)
    S.op('vector', 'memset', w=[EPSC], ap=EPSC.ap, constant=EPS)
    SM = pool.alloc(L * 10)
    S.dma('sync', SM.ap, smalls[:, :], w=[SM])
    FNW = pool.alloc(16)
    S.dma('sync', FNW.ap, fnw[:, :], w=[FNW])
    DER = pool.alloc(L * 2 * 9 * 16)

    def der(l, cond, i, kind):
        o = (((l * 2 + cond) * 3 + i) * 3 + kind) * 16
        return DER.v(DER.ap[:, o:o + 16])

    m0 = pool.mark()
    xl = [pool.alloc(2048) for _ in range(2)]
    xs = [pool.alloc(2048) for _ in range(2)]
    for tb in range(NB):
        a = xl[tb % 2]
        S.dma('sync', a.ap, xin[tb * 128:(tb + 1) * 128, :], w=[a])
        st = xs[tb % 2]
        for q4 in range(4):
            pb = PS[(tb * 4 + q4) % 8]
            for k in range(4):
                c = q4 * 4 + k
                S.op('tensor', 'transpose', r=[a, IDF], w=[pb], out=pb.ap[:, k * 128:(k + 1) * 128],
                     in_=a.ap[:, c * 128:(c + 1) * 128], identity=IDF.ap)
            eng = 'vector' if q4 % 2 == 0 else 'scalar'
            if eng == 'vector':
                S.op('vector', 'tensor_copy', r=[pb], w=[st], out=st.ap[:, q4 * 512:(q4 + 1) * 512], in_=pb.ap)
            else:
                S.op('scalar', 'activation', r=[pb], w=[st], out=st.ap[:, q4 * 512:(q4 + 1) * 512], in_=pb.ap,
                     func=AF.Copy)
        wb = [T(None, B_XT[c][tb // 4]) for c in range(16)]
        S.dma('sync', XT[:, :, tb * 128:(tb + 1) * 128].rearrange("c p t -> p c t"),
              st.ap.rearrange("p (c t) -> p c t", c=16), r=[st], w=wb)
    S.barrier()
    pool.release(m0)

    m0 = pool.mark()
    CV = pool.alloc(32)
    S.dma('sync', CV.ap, cv_in[:, :], w=[CV])
    SC_ = pool.alloc(32)
    S.op('scalar', 'activation', r=[CV], w=[SC_], out=SC_.ap, in_=CV.ap, func=AF.Silu)
    SCr = SC_.ap.rearrange("p (k c) -> p c k", k=2)
    SCc = pool.alloc(32)
    S.op('vector', 'tensor_copy', r=[SC_], w=[SCc], out=SCc.ap.rearrange("p (c k) -> p c k", k=2), in_=SCr)
    BM = pool.alloc(L * 144)
    S.dma('sync', BM.ap, bmod[:, :], w=[BM])
    NW = pool.alloc(L * 48)
    S.dma('sync', NW.ap, normw[:, :], w=[NW])
    MODT = pool.alloc(L * 2 * 144)
    wmb = [pool.alloc(4096) for _ in range(3)]
    for l in range(L):
        mp = PS[l % 2]
        for g in range(72):
            wt = wmb[(l * 72 + g) % 3]
            S.dma('sync', wt.ap, wmod[l * 72 + g, :, :], w=[wt])
            for mm in range(2):
                m = 2 * g + mm
                for c in range(16):
                    S.op('tensor', 'matmul', r=[wt, SCc], w=[mp], out=mp.ap[:, 2 * m:2 * m + 2],
                         lhsT=wt.ap[:, (mm * 16 + c) * 128:(mm * 16 + c + 1) * 128],
                         rhs=SCc.ap[:, 2 * c:2 * c + 2], start=(c == 0), stop=(c == 15))
        for cond in range(2):
            o = (l * 2 + cond) * 144
            S.op('vector', 'tensor_tensor', r=[mp, BM], w=[MODT],
                 out=MODT.ap[:, o:o + 144],
                 in0=mp.ap[:, 0:288].rearrange("p (m k) -> p m k", k=2)[:, :, cond],
                 in1=BM.ap[:, l * 144:(l + 1) * 144], op=ALU.add)
        for cond in range(2):
            o = (l * 2 + cond) * 144
            for i in range(3):
                S.op('vector', 'scalar_tensor_tensor', r=[MODT, NW], w=[DER],
                     out=der(l, cond, i, 0).ap, in0=MODT.ap[:, o + (3 * i + 1) * 16:o + (3 * i + 2) * 16],
                     scalar=1.0, in1=NW.ap[:, l * 48 + i * 16:l * 48 + (i + 1) * 16], op0=ALU.add, op1=ALU.mult)
                S.op('vector', 'tensor_copy', r=[MODT], w=[DER],
                     out=der(l, cond, i, 1).ap, in_=MODT.ap[:, o + (3 * i) * 16:o + (3 * i + 1) * 16])
                S.op('vector', 'tensor_scalar', r=[MODT], w=[DER],
                     out=der(l, cond, i, 2).ap, in0=MODT.ap[:, o + (3 * i + 2) * 16:o + (3 * i + 3) * 16],
                     scalar1=(1.0 if i == 1 else 0.5), scalar2=None, op0=ALU.mult)
    S.barrier()
    pool.release(m0)

    CW = 2048
    stg = [pool.alloc(CW, BF16) for _ in range(3)]
    chunks = []

    def add_rows(src, dst, n, width, key):
        for r_ in range(n):
            B_W[(key, r_)] = Buf()
            for c0 in range(0, width, CW):
                cw = min(CW, width - c0)
                chunks.append((src[r_, :, c0:c0 + cw], dst[r_, :, c0:c0 + cw], cw, (key, r_)))

    order = []
    for l in range(L):
        order.append((w1[l * 2 * NF:(l * 2 + 1) * NF], W1s[l * 2 * NF:(l * 2 + 1) * NF], NF, 4096, ('w1', l, 0)))
        order.append((w2[l * 32:l * 32 + 16], W2s[l * 32:l * 32 + 16], 16, NF * 128, ('w2', l, 0)))
        order.append((winf[l * NFM:(l + 1) * NFM], WFs[l * NFM:(l + 1) * NFM], NFM, 2048, ('wf', l)))
        order.append((wint[l * 4:(l + 1) * 4], WTs[l * 4:(l + 1) * 4], 4, 8192, ('wt', l)))
        order.append((wout[l * 16:(l + 1) * 16], WOs[l * 16:(l + 1) * 16], 16, 2048, ('wo', l)))
        order.append((w1[(l * 2 + 1) * NF:(l * 2 + 2) * NF], W1s[(l * 2 + 1) * NF:(l * 2 + 2) * NF], NF, 4096, ('w1', l, 1)))
        order.append((w2[l * 32 + 16:l * 32 + 32], W2s[l * 32 + 16:l * 32 + 32], 16, NF * 128, ('w2', l, 1)))
    for o_ in order:
        add_rows(*o_)
    bgs = {'i': 0, 'pend': None, 'emitted': set()}

    def bg_step(n=1):
        for _ in range(n):
            i = bgs['i']
            if i < len(chunks):
                src_, dst_, cw, kr = chunks[i]
                s_ = stg[i % 3]
                S.dma('gpsimd', s_.ap[:, 0:cw], src_, w=[s_], max_dma_last_dim=8192)
                bgs['i'] = i + 1
            else:
                src_ = None
            p_ = bgs['pend']
            if p_ is not None:
                S.dma('gpsimd', p_[1], p_[0].ap[:, 0:p_[2]], r=[p_[0]], w=[T(None, B_W[p_[3]])])
                bgs['emitted'].add(p_[3][0])
                bgs['pend'] = None
            if src_ is not None:
                bgs['pend'] = (s_, dst_, cw, kr)

    def bg_ensure(key):
        last = max(i for i, c_ in enumerate(chunks) if c_[3][0] == key)
        while bgs['i'] <= last or (bgs['pend'] is not None and bgs['pend'][3][0] == key):
            bg_step()

    bg_ensure(('w1', 0, 0))
    bg_ensure(('w2', 0, 0))

    def wbuf(key, r_):
        return T(None, B_W[(key, r_)])

    def prologue(t, XB, HT, SQ, RS, TMP, sc, sh, psb):
        t0 = t * 512
        for c in range(16):
            S.dma('sync', XB[c].ap, XT[c, :, t0:t0 + 512], r=[T(None, B_XT[c][t])], w=[XB[c]])
        for c in range(16):
            sq = SQ[c % 2]
            S.op('scalar', 'activation', r=[XB[c]], w=[sq], out=sq.ap, in_=XB[c].ap, func=AF.Square)
            S.op('tensor', 'matmul', r=[ONEB, sq], w=[psb], out=psb.ap, lhsT=ONEB.ap, rhs=sq.ap,
                 start=(c == 0), stop=(c == 15))
        S.op('scalar', 'activation', r=[psb, EPSC], w=[RS], out=RS.ap, in_=psb.ap, func=AF.Sqrt,
             bias=EPSC.ap, scale=1.0 / D)
        if HT is None:
            S.op('vector', 'reciprocal', r=[RS], w=[RS], out=RS.ap, in_=RS.ap)
            return
        RP = PS[7]
        S.op('vector', 'reciprocal', r=[RS], w=[RP], out=RP.ap, in_=RS.ap)
        for c in range(16):
            tm = TMP[c % 2]
            S.op('vector', 'scalar_tensor_tensor', r=[XB[c], RP, sc], w=[tm], out=tm.ap, in0=XB[c].ap,
                 scalar=sc.ap[:, c:c + 1], in1=RP.ap, op0=ALU.mult, op1=ALU.mult)
            S.op('scalar', 'activation', r=[tm, sh], w=[HT[c]], out=HT[c].ap, in_=tm.ap, func=AF.Identity,
                 bias=sh.ap[:, c:c + 1], scale=1.0)

    def tile_cond(t):
        return 0 if t * 512 < TS else 1

    def ffn_phase(l, i):
        bg_ensure(('w1', l, i))
        bg_ensure(('w2', l, i))
        m0 = pool.mark()
        HT = [pool.alloc(512, BF16) for _ in range(16)]
        AT = [pool.alloc(512, BF16) for _ in range(NF)]
        W1B = [pool.alloc(4096, BF16) for _ in range(3)]
        W2B = [pool.alloc(NF * 128, BF16) for _ in range(3)]
        SQ = [pool.alloc(512, BF16) for _ in range(2)]
        RS = pool.alloc(512)
        TMP = [pool.alloc(512) for _ in range(2)]
        SG = [pool.alloc(512) for _ in range(2)]
        XS = [pool.alloc(512) for _ in range(4)]
        XC = [pool.alloc(512) for _ in range(4)]
        ii = 0 if i == 0 else 2
        st_ = {'n1': 0, 'n2': 0, 'nx': 0, 'xa': {}, 'xb': {}}
        psb = PS[6]
        RP = PS[7]

        def load_x(t, c, which):
            xs = XS[st_['nx'] % 4]
            st_['nx'] += 1
            S.dma('sync', xs.ap, XT[c, :, t * 512:t * 512 + 512], r=[T(None, B_XT[c][t])], w=[xs])
            st_[which][c] = xs

        def comp_a(t, c):
            xs = st_['xa'].pop(c)
            sq = SQ[c % 2]
            S.op('scalar', 'activation', r=[xs], w=[sq], out=sq.ap, in_=xs.ap, func=AF.Square)
            S.op('tensor', 'matmul', r=[ONEB, sq], w=[psb], out=psb.ap, lhsT=ONEB.ap, rhs=sq.ap,
                 start=(c == 0), stop=(c == 15))

        def pro_mid():
            S.op('scalar', 'activation', r=[psb, EPSC], w=[RS], out=RS.ap, in_=psb.ap, func=AF.Sqrt,
                 bias=EPSC.ap, scale=1.0 / D)
            S.op('vector', 'reciprocal', r=[RS], w=[RP], out=RP.ap, in_=RS.ap)

        def comp_b(t, c):
            cond = tile_cond(t)
            sc = der(l, cond, ii, 0)
            sh = der(l, cond, ii, 1)
            xs = st_['xb'].pop(c)
            tm = TMP[c % 2]
            S.op('vector', 'scalar_tensor_tensor', r=[xs, RP, sc], w=[tm], out=tm.ap, in0=xs.ap,
                 scalar=sc.ap[:, c:c + 1], in1=RP.ap, op0=ALU.mult, op1=ALU.mult)
            S.op('scalar', 'activation', r=[tm, sh], w=[HT[c]], out=HT[c].ap, in_=tm.ap, func=AF.Identity,
                 bias=sh.ap[:, c:c + 1], scale=1.0)

        for c0 in range(0, 16, 2):
            load_x(0, c0, 'xa'); load_x(0, c0 + 1, 'xa')
            comp_a(0, c0); comp_a(0, c0 + 1)
        pro_mid()
        for c0 in range(0, 16, 2):
            load_x(0, c0, 'xb'); load_x(0, c0 + 1, 'xb')
            comp_b(0, c0); comp_b(0, c0 + 1)

        for t in range(NT):
            cond = tile_cond(t)
            t0 = t * 512
            nxt = t + 1 if t + 1 < NT else None
            if t > 0:
                comp_b(t, 14); comp_b(t, 15)
            for j in range(NF):
                wt = W1B[st_['n1'] % 3]
                st_['n1'] += 1
                S.dma('sync', wt.ap, W1s[(l * 2 + i) * NF + j, :, :], r=[wbuf(('w1', l, i), j)], w=[wt])
                pg = PS[(j % 2) * 2]
                pu = PS[(j % 2) * 2 + 1]
                for c in range(16):
                    S.op('tensor', 'matmul', r=[wt, HT[c]], w=[pg], out=pg.ap,
                         lhsT=wt.ap[:, c * 256:c * 256 + 128], rhs=HT[c].ap, start=(c == 0), stop=(c == 15))
                for c in range(16):
                    S.op('tensor', 'matmul', r=[wt, HT[c]], w=[pu], out=pu.ap,
                         lhsT=wt.ap[:, c * 256 + 128:c * 256 + 256], rhs=HT[c].ap, start=(c == 0), stop=(c == 15))
                bg_step()
                sg = SG[j % 2]
                S.op('scalar', 'activation', r=[pg], w=[sg], out=sg.ap, in_=pg.ap, func=AF.Silu)
                S.op('vector', 'tensor_tensor', r=[sg, pu], w=[AT[j]], out=AT[j].ap, in0=sg.ap, in1=pu.ap, op=ALU.mult)
            gate = der(l, cond, ii, 2)
            for dj in range(16):
                wt = W2B[st_['n2'] % 3]
                xc = XC[st_['n2'] % 4]
                st_['n2'] += 1
                S.dma('sync', wt.ap, W2s[(l * 2 + i) * 16 + dj, :, :], r=[wbuf(('w2', l, i), dj)], w=[wt])
                S.dma('sync', xc.ap, XT[dj, :, t0:t0 + 512], r=[T(None, B_XT[dj][t])], w=[xc])
                bg_step()
                po = PS[4 + dj % 2]
                for c in range(NF):
                    S.op('tensor', 'matmul', r=[wt, AT[c]], w=[po], out=po.ap,
                         lhsT=wt.ap[:, c * 128:(c + 1) * 128], rhs=AT[c].ap, start=(c == 0), stop=(c == NF - 1))
                S.op('vector', 'scalar_tensor_tensor', r=[po, gate, xc], w=[xc], out=xc.ap, in0=po.ap,
                     scalar=gate.ap[:, dj:dj + 1], in1=xc.ap, op0=ALU.mult, op1=ALU.add)
                S.dma('gpsimd', XT[dj, :, t0:t0 + 512], xc.ap, r=[xc], w=[T(None, B_XT[dj][t])])
                if nxt is not None:
                    if dj < 8:
                        if dj >= 1:
                            comp_a(nxt, 2 * dj - 2); comp_a(nxt, 2 * dj - 1)
                        load_x(nxt, 2 * dj, 'xa'); load_x(nxt, 2 * dj + 1, 'xa')
                    elif dj == 8:
                        comp_a(nxt, 14); comp_a(nxt, 15)
                        pro_mid()
                        load_x(nxt, 0, 'xb'); load_x(nxt, 1, 'xb')
                    else:
                        comp_b(nxt, 2 * (dj - 9)); comp_b(nxt, 2 * (dj - 9) + 1)
                        load_x(nxt, 2 * (dj - 8), 'xb'); load_x(nxt, 2 * (dj - 8) + 1, 'xb')
        S.barrier()
        pool.release(m0)

    def inproj_phase(l):
        bg_ensure(('wf', l))
        bg_ensure(('wt', l))
        m0 = pool.mark()
        XB = [pool.alloc(512) for _ in range(16)]
        HT = [pool.alloc(512, BF16) for _ in range(16)]
        WFB = [pool.alloc(2048, BF16) for _ in range(3)]
        WTB = [pool.alloc(8192, BF16) for _ in range(2)]
        SQ = [pool.alloc(512, BF16) for _ in range(2)]
        RS = pool.alloc(512)
        TMP = [pool.alloc(512) for _ in range(2)]
        RT = [pool.alloc(512) for _ in range(4)]
        STG = [pool.alloc(512, BF16) for _ in range(4)]
        QN = [pool.alloc(512, BF16) for _ in range(2)]
        R1 = [pool.alloc(512) for _ in range(2)]
        R2 = [pool.alloc(512) for _ in range(2)]
        RQ = [pool.alloc(512) for _ in range(2)]
        TST = [pool.alloc(512, BF16) for _ in range(2)]
        OF = [pool.alloc(512) for _ in range(2)]
        OT = [pool.alloc(512) for _ in range(2)]
        nw = 0
        nst_ = 0
        nq = 0
        for t in range(NT):
            cond = tile_cond(t)
            t0 = t * 512
            sample = (cond == 0)
            prologue(t, XB, HT, SQ, RS, TMP, der(l, cond, 1, 0), der(l, cond, 1, 1), PS[6])
            if sample:
                for k in range(4):
                    S.dma('sync', RT[k].ap, rope[k, :, t0:t0 + 512], w=[RT[k]])
            dbg = getattr(cfg, 'dbg', ())
            for g in range(NFM):
                if 'fmx' in dbg and g >= G_GQ:
                    continue
                wt = WFB[nw % 3]
                nw += 1
                S.dma('sync', wt.ap, WFs[l * NFM + g, :, :], r=[wbuf(('wf', l), g)], w=[wt])
                pa = PS[(g % 2) * 2]
                for c in range(16):
                    S.op('tensor', 'matmul', r=[wt, HT[c]], w=[pa], out=pa.ap,
                         lhsT=wt.ap[:, c * 128:(c + 1) * 128], rhs=HT[c].ap, start=(c == 0), stop=(c == 15))
                dstb = fmbufs(g, t0, 512)
                dst = FMS[g, :, t0:t0 + 512]
                st = STG[nst_ % 4]
                nst_ += 1
                if g < G_RG and g < G_RK:
                    S.op('scalar', 'activation', r=[pa], w=[st], out=st.ap, in_=pa.ap, func=AF.Copy)
                    S.dma('gpsimd', dst, st.ap, r=[st], w=dstb)
                elif g < G_RG:
                    S.op('scalar', 'activation', r=[pa], w=[st], out=st.ap, in_=pa.ap, func=AF.Copy,
                         scale=128.0 ** -0.5)
                    S.dma('gpsimd', dst, st.ap, r=[st], w=dstb)
                elif g < G_GQ:
                    S.op('scalar', 'activation', r=[pa], w=[st], out=st.ap, in_=pa.ap, func=AF.Silu)
                    S.dma('gpsimd', dst, st.ap, r=[st], w=dstb)
                else:
                    isg = g < G_DQ
                    k_ = nq % 2
                    nq += 1
                    pb = PS[(g % 2) * 2 + 1]
                    if isg:
                        sq = SQ[k_]
                        S.op('scalar', 'activation', r=[pa], w=[sq], out=sq.ap, in_=pa.ap, func=AF.Square)
                        S.op('tensor', 'matmul', r=[ONEB, sq], w=[pb], out=pb.ap, lhsT=ONEB.ap, rhs=sq.ap,
                             start=True, stop=True)
                        rq_ = RQ[k_]
                        S.op('scalar', 'activation', r=[pb, EPSC], w=[rq_], out=rq_.ap, in_=pb.ap, func=AF.Sqrt,
                             bias=EPSC.ap, scale=1.0 / 128)
                        S.op('vector', 'reciprocal', r=[rq_], w=[rq_], out=rq_.ap, in_=rq_.ap)
                        wcol = l * 10 + (0 if g < G_GK else 1)
                        qf = OF[k_]
                        S.op('vector', 'scalar_tensor_tensor', r=[pa, SM, rq_], w=[qf], out=qf.ap, in0=pa.ap,
                             scalar=SM.ap[:, wcol:wcol + 1], in1=rq_.ap, op0=ALU.mult, op1=ALU.mult)
                    else:
                        qf = OF[k_]
                        S.op('vector', 'tensor_copy', r=[pa], w=[qf], out=qf.ap, in_=pa.ap)
                    if sample:
                        qn = QN[k_]
                        S.op('scalar', 'activation', r=[qf], w=[qn], out=qn.ap, in_=qf.ap, func=AF.Copy)
                        pm = PGB if isg else PDB
                        S.op('tensor', 'matmul', r=[pm, qn], w=[pb], out=pb.ap, lhsT=pm.ap, rhs=qn.ap,
                             start=True, stop=True)
                        ct, stb = (RT[0], RT[1]) if isg else (RT[2], RT[3])
                        r1 = R1[k_]
                        r2 = R2[k_]
                        S.op('gpsimd', 'tensor_tensor', r=[qf, ct], w=[r1], out=r1.ap, in0=qf.ap, in1=ct.ap, op=ALU.mult)
                        S.op('vector', 'tensor_tensor', r=[pb, stb], w=[r2], out=r2.ap, in0=pb.ap, in1=stb.ap, op=ALU.mult)
                        S.op('gpsimd', 'tensor_tensor', r=[r1, r2], w=[st], out=st.ap, in0=r1.ap, in1=r2.ap, op=ALU.add)
                        S.dma('gpsimd', dst, st.ap, r=[st], w=dstb)
                    else:
                        S.op('scalar', 'activation', r=[qf], w=[st], out=st.ap, in_=qf.ap, func=AF.Copy)
                        S.dma('gpsimd', dst, st.ap, r=[st], w=dstb)
                        if (G_GK <= g < G_DQ) or g >= G_DK:
                            for sb in range(4):
                                S.op('tensor', 'transpose', r=[qf, IDF], w=[pb], out=pb.ap[:, sb * 128:(sb + 1) * 128],
                                     in_=qf.ap[:, sb * 128:(sb + 1) * 128], identity=IDF.ap)
                            ot = OT[k_]
                            S.op('vector', 'tensor_copy', r=[pb], w=[ot], out=ot.ap, in_=pb.ap)
                            for s_ in range(NPS):
                                if g < G_DQ:
                                    h = g - G_GK
                                    o_ = ngk[s_ * L + l, :, h * 128:(h + 1) * 128]
                                else:
                                    h = g - G_DK
                                    o_ = ndk[s_ * L + l, :, h * 128:(h + 1) * 128]
                                S.dma('gpsimd', o_.rearrange("(b p) f -> p b f", p=128),
                                      ot.ap[:, s_ * 256:(s_ + 1) * 256].rearrange("p (b f) -> p b f", b=2),
                                      r=[ot], w=[T(None)])
            for blk in range(4):
                if 'tm' in dbg:
                    continue
                wt = WTB[blk % 2]
                width = 256 if blk == 2 else 512
                S.dma('sync', wt.ap, WTs[l * 4 + blk, :, :], r=[wbuf(('wt', l), blk)], w=[wt])
                for sb in range(4):
                    pa = PS[4 + (sb % 2)]
                    for c in range(16):
                        S.op('tensor', 'matmul', r=[wt, HT[c]], w=[pa], out=pa.ap[:, 0:width],
                             lhsT=HT[c].ap[:, sb * 128:(sb + 1) * 128], rhs=wt.ap[:, c * 512:c * 512 + width],
                             start=(c == 0), stop=(c == 15))
                    st = TST[sb % 2]
                    bi = t * 4 + sb
                    if blk == 0:
                        S.op('scalar', 'activation', r=[pa], w=[st], out=st.ap, in_=pa.ap, func=AF.Copy,
                             scale=128.0 ** -0.5)
                        S.dma('gpsimd', KTOK[bi, :, :], st.ap, r=[st], w=[T(None, B_TOK['k'][bi])])
                    elif blk == 1:
                        S.op('vector', 'tensor_copy', r=[pa], w=[st], out=st.ap, in_=pa.ap)
                        S.dma('gpsimd', VTOK[bi, :, :], st.ap, r=[st], w=[T(None, B_TOK['v'][bi])])
                    else:
                        S.op('vector', 'tensor_copy', r=[pa], w=[st], out=st.ap[:, 0:width], in_=pa.ap[:, 0:width])
                        if blk == 2:
                            S.dma('gpsimd', GVT[bi, :, :], st.ap[:, 0:256], r=[st], w=[T(None, B_TOK['gv'][bi])])
                        else:
                            S.dma('gpsimd', DVT[bi, :, :], st.ap, r=[st], w=[T(None, B_TOK['dv'][bi])])
                        if not sample and 'po' not in dbg:
                            of = OF[sb % 2]
                            of = OT[sb % 2]
                            S.op('vector', 'tensor_copy', r=[pa], w=[of], out=of.ap[:, 0:width], in_=pa.ap[:, 0:width])
                            s_ = sb // 2
                            rr = (sb % 2) * 128
                            od = ngv if blk == 2 else ndv
                            S.dma('gpsimd', od[s_ * L + l, rr:rr + 128, :], of.ap[:, 0:width], r=[of], w=[T(None)])
        S.barrier()
        pool.release(m0)

    def ret_phase(l):
        m0 = pool.mark()
        RD = pool.alloc(8)
        S.dma('sync', RD.ap, rdec[:, l * 8:(l + 1) * 8], w=[RD])
        LG = pool.alloc(8)
        S.op('scalar', 'activation', r=[RD], w=[LG], out=LG.ap, in_=RD.ap, func=AF.Exp, scale=-1.0)
        S.op('vector', 'tensor_scalar', r=[LG], w=[LG], out=LG.ap, in0=LG.ap, scalar1=1.0, scalar2=None, op0=ALU.add)
        S.op('scalar', 'activation', r=[LG], w=[LG], out=LG.ap, in_=LG.ap, func=AF.Ln)
        S.op('vector', 'tensor_scalar', r=[LG], w=[LG], out=LG.ap, in0=LG.ap, scalar1=-1.0, scalar2=None, op0=ALU.mult)
        MT = [pool.alloc(128) for _ in range(4)]
        QDF = [pool.alloc(128) for _ in range(4)]
        QDB = [pool.alloc(128) for _ in range(4)]
        KD = pool.alloc(8)
        CD_ = pool.alloc(8)
        e1 = pool.alloc(128)
        e2 = pool.alloc(128)
        for h in range(4):
            lf = LG.ap[:, h:h + 1]
            lb = LG.ap[:, 4 + h:5 + h]
            S.op('scalar', 'activation', r=[CON, LG], w=[e1], out=e1.ap, in_=CON.ap[:, C_POS:C_POS + 128], func=AF.Exp, scale=lf)
            S.op('vector', 'tensor_tensor', r=[e1, CON], w=[e1], out=e1.ap, in0=e1.ap, in1=CON.ap[:, C_MF:C_MF + 128], op=ALU.mult)
            S.op('scalar', 'activation', r=[CON, LG], w=[e2], out=e2.ap, in_=CON.ap[:, C_NEG:C_NEG + 128], func=AF.Exp, scale=lb)
            S.op('vector', 'tensor_tensor', r=[e2, CON], w=[e2], out=e2.ap, in0=e2.ap, in1=CON.ap[:, C_MB:C_MB + 128], op=ALU.mult)
            S.op('vector', 'tensor_tensor', r=[e1, e2], w=[MT[h]], out=MT[h].ap, in0=e1.ap, in1=e2.ap, op=ALU.add)
            S.op('scalar', 'activation', r=[CON, LG], w=[QDF[h]], out=QDF[h].ap, in_=CON.ap[:, C_I1:C_I1 + 128], func=AF.Exp, scale=lf)
            S.op('scalar', 'activation', r=[CON, LG], w=[QDB[h]], out=QDB[h].ap, in_=CON.ap[:, C_IB:C_IB + 128], func=AF.Exp, scale=lb)
            S.op('scalar', 'activation', r=[CON, LG], w=[KD], out=KD.ap[:, h:h + 1], in_=CON.ap[:, C_KF:C_KF + 1], func=AF.Exp, scale=lf)
            S.op('scalar', 'activation', r=[CON, LG], w=[KD], out=KD.ap[:, 4 + h:5 + h], in_=CON.ap[:, C_KB:C_KB + 1], func=AF.Exp, scale=lb)
        S.op('scalar', 'activation', r=[LG], w=[CD_], out=CD_.ap, in_=LG.ap, func=AF.Exp, scale=128.0)

        nchmax = TS // 128
        SBALL = [pool.alloc(512, BF16) for _ in range(nchmax)]
        SF = pool.alloc(512)
        SB_ = pool.alloc(512)
        SFB = pool.alloc(512, BF16)
        KT_ = [pool.alloc(512, BF16) for _ in range(2)]
        VT_ = [pool.alloc(512, BF16) for _ in range(2)]
        KS = [pool.alloc(512, BF16) for _ in range(2)]
        QT_ = [pool.alloc(512, BF16) for _ in range(2)]
        KTT = [pool.alloc(512, BF16) for _ in range(2)]
        RGT = [pool.alloc(512, BF16) for _ in range(2)]
        QF = [pool.alloc(512, BF16) for _ in range(2)]
        QB = [pool.alloc(512, BF16) for _ in range(2)]
        AM = [pool.alloc(128, BF16) for _ in range(4)]
        OB = [pool.alloc(512, BF16) for _ in range(2)]
        OSQ = [pool.alloc(512, BF16) for _ in range(2)]
        OF_ = [pool.alloc(512) for _ in range(2)]
        MEAN = pool.alloc(512)
        VAR = pool.alloc(512)
        OUT = [pool.alloc(512, BF16) for _ in range(2)]

        seqs = [(0, TS, True, None)] + [(TS + s_ * TP, TP, False, s_) for s_ in range(NPS)]
        for (tok0, Tn, has_ctx, sidx) in seqs:
            nch = Tn // 128
            b0 = tok0 // 128
            if has_ctx:
                S.dma('sync', SF.ap, stin[l * 2 + 0, :, :], w=[SF])
                S.dma('sync', SB_.ap, stin[l * 2 + 1, :, :], w=[SB_])
            else:
                S.op('vector', 'memset', w=[SF], ap=SF.ap, constant=0.0)
                S.op('vector', 'memset', w=[SB_], ap=SB_.ap, constant=0.0)
            for n in range(nch - 1, -1, -1):
                bi = b0 + n
                kt = KT_[n % 2]
                vt = VT_[n % 2]
                S.dma('sync', kt.ap, KTOK[bi, :, :], r=[T(None, B_TOK['k'][bi])], w=[kt])
                S.dma('sync', vt.ap, VTOK[bi, :, :], r=[T(None, B_TOK['v'][bi])], w=[vt])
                S.op('scalar', 'activation', r=[SB_], w=[SBALL[n]], out=SBALL[n].ap, in_=SB_.ap, func=AF.Copy)
                ks = KS[n % 2]
                for h in range(4):
                    S.op('vector', 'tensor_scalar', r=[kt, KD], w=[ks], out=ks.ap[:, h * 128:(h + 1) * 128],
                         in0=kt.ap[:, h * 128:(h + 1) * 128], scalar1=KD.ap[:, 4 + h:5 + h], scalar2=None, op0=ALU.mult)
                pu = PS[n % 2]
                for h in range(4):
                    S.op('tensor', 'matmul', r=[ks, vt], w=[pu], out=pu.ap[:, h * 128:(h + 1) * 128],
                         lhsT=ks.ap[:, h * 128:(h + 1) * 128], rhs=vt.ap[:, h * 128:(h + 1) * 128], start=True, stop=True)
                for h in range(4):
                    S.op('vector', 'scalar_tensor_tensor', r=[SB_, CD_, pu], w=[SB_], out=SB_.ap[:, h * 128:(h + 1) * 128],
                         in0=SB_.ap[:, h * 128:(h + 1) * 128], scalar=CD_.ap[:, 4 + h:5 + h],
                         in1=pu.ap[:, h * 128:(h + 1) * 128], op0=ALU.mult, op1=ALU.add)
            if not has_ctx:
                S.dma('gpsimd', nst[(sidx * L + l) * 2 + 1, :, :], SB_.ap, r=[SB_], w=[T(None)])
            def part1(n):
                bi = b0 + n
                c0 = tok0 + n * 128
                kt = KT_[n % 2]
                vt = VT_[n % 2]
                qt = QT_[n % 2]
                ktt = KTT[n % 2]
                rgt = RGT[n % 2]
                S.dma('sync', kt.ap, KTOK[bi, :, :], r=[T(None, B_TOK['k'][bi])], w=[kt])
                S.dma('sync', vt.ap, VTOK[bi, :, :], r=[T(None, B_TOK['v'][bi])], w=[vt])
                S.dma('sync', qt.ap.rearrange("p (h t) -> p h t", h=4),
                      FMS[G_RQ:G_RQ + 4, :, c0:c0 + 128].rearrange("h p t -> p h t"),
                      r=[T(None, B_FMS[G_RQ + h][bi]) for h in range(4)], w=[qt])
                S.dma('sync', ktt.ap.rearrange("p (h t) -> p h t", h=4),
                      FMS[G_RK:G_RK + 4, :, c0:c0 + 128].rearrange("h p t -> p h t"),
                      r=[T(None, B_FMS[G_RK + h][bi]) for h in range(4)], w=[ktt])
                S.dma('sync', rgt.ap.rearrange("p (h t) -> p h t", h=4),
                      FMS[G_RG:G_RG + 4, :, c0:c0 + 128].rearrange("h p t -> p h t"),
                      r=[T(None, B_FMS[G_RG + h][bi]) for h in range(4)], w=[rgt])
                S.op('scalar', 'activation', r=[SF], w=[SFB], out=SFB.ap, in_=SF.ap, func=AF.Copy)
                qf = QF[n % 2]
                qb = QB[n % 2]
                pa = PS[2 + (n % 2)]
                po = PS[4 + (n % 2)]
                for h in range(4):
                    hs = slice(h * 128, (h + 1) * 128)
                    S.op('gpsimd', 'tensor_tensor', r=[qt, QDF[h]], w=[qf], out=qf.ap[:, hs], in0=qt.ap[:, hs], in1=QDF[h].ap, op=ALU.mult)
                    S.op('gpsimd', 'tensor_tensor', r=[qt, QDB[h]], w=[qb], out=qb.ap[:, hs], in0=qt.ap[:, hs], in1=QDB[h].ap, op=ALU.mult)
                for h in range(4):
                    hs = slice(h * 128, (h + 1) * 128)
                    S.op('tensor', 'matmul', r=[ktt, qt], w=[pa], out=pa.ap[:, hs], lhsT=ktt.ap[:, hs], rhs=qt.ap[:, hs],
                         start=True, stop=True)
                for h in range(4):
                    hs = slice(h * 128, (h + 1) * 128)
                    S.op('vector', 'tensor_tensor', r=[pa, MT[h]], w=[AM[h]], out=AM[h].ap, in0=pa.ap[:, hs], in1=MT[h].ap, op=ALU.mult)
                for h in range(4):
                    hs = slice(h * 128, (h + 1) * 128)
                    S.op('tensor', 'matmul', r=[vt, AM[h]], w=[po], out=po.ap[:, hs], lhsT=vt.ap[:, hs], rhs=AM[h].ap, start=True, stop=False)
                    S.op('tensor', 'matmul', r=[SFB, qf], w=[po], out=po.ap[:, hs], lhsT=SFB.ap[:, hs], rhs=qf.ap[:, hs], start=False, stop=False)
                    S.op('tensor', 'matmul', r=[SBALL[n], qb], w=[po], out=po.ap[:, hs], lhsT=SBALL[n].ap[:, hs], rhs=qb.ap[:, hs], start=False, stop=True)
                ks = KS[n % 2]
                for h in range(4):
                    hs = slice(h * 128, (h + 1) * 128)
                    S.op('vector', 'tensor_scalar', r=[kt, KD], w=[ks], out=ks.ap[:, hs], in0=kt.ap[:, hs],
                         scalar1=KD.ap[:, h:h + 1], scalar2=None, op0=ALU.mult)
                pu = PS[n % 2]
                for h in range(4):
                    hs = slice(h * 128, (h + 1) * 128)
                    S.op('tensor', 'matmul', r=[ks, vt], w=[pu], out=pu.ap[:, hs], lhsT=ks.ap[:, hs], rhs=vt.ap[:, hs], start=True, stop=True)
                for h in range(4):
                    hs = slice(h * 128, (h + 1) * 128)
                    S.op('vector', 'scalar_tensor_tensor', r=[SF, CD_, pu], w=[SF], out=SF.ap[:, hs], in0=SF.ap[:, hs],
                         scalar=CD_.ap[:, h:h + 1], in1=pu.ap[:, hs], op0=ALU.mult, op1=ALU.add)
                S.op('scalar', 'activation', r=[po], w=[OB[n % 2]], out=OB[n % 2].ap, in_=po.ap, func=AF.Copy)
                S.op('scalar', 'activation', r=[po], w=[OSQ[n % 2]], out=OSQ[n % 2].ap, in_=po.ap, func=AF.Square)
                S.op('vector', 'tensor_copy', r=[po], w=[OF_[n % 2]], out=OF_[n % 2].ap, in_=po.ap)

            def part2(n):
                bi = b0 + n
                c0 = tok0 + n * 128
                rgt = RGT[n % 2]
                pm = PS[6]
                pv = PS[7]
                S.op('tensor', 'matmul', r=[ONEB, OB[n % 2]], w=[pm], out=pm.ap, lhsT=ONEB.ap, rhs=OB[n % 2].ap, start=True, stop=True)
                S.op('tensor', 'matmul', r=[ONEB, OSQ[n % 2]], w=[pv], out=pv.ap, lhsT=ONEB.ap, rhs=OSQ[n % 2].ap, start=True, stop=True)
                S.op('scalar', 'activation', r=[pm], w=[MEAN], out=MEAN.ap, in_=pm.ap, func=AF.Copy, scale=1.0 / 128)
                S.op('vector', 'tensor_tensor', r=[MEAN], w=[VAR], out=VAR.ap, in0=MEAN.ap, in1=MEAN.ap, op=ALU.mult)
                S.op('vector', 'scalar_tensor_tensor', r=[pv, VAR], w=[VAR], out=VAR.ap, in0=pv.ap, scalar=1.0 / 128,
                     in1=VAR.ap, op0=ALU.mult, op1=ALU.subtract)
                S.op('vector', 'tensor_scalar', r=[VAR], w=[VAR], out=VAR.ap, in0=VAR.ap, scalar1=0.0, scalar2=None, op0=ALU.max)
                S.op('scalar', 'activation', r=[VAR, EPSC], w=[VAR], out=VAR.ap, in_=VAR.ap, func=AF.Sqrt, bias=EPSC.ap, scale=1.0)
                S.op('vector', 'reciprocal', r=[VAR], w=[VAR], out=VAR.ap, in_=VAR.ap)
                S.op('vector', 'tensor_tensor', r=[OF_[n % 2], MEAN], w=[OF_[n % 2]], out=OF_[n % 2].ap, in0=OF_[n % 2].ap, in1=MEAN.ap, op=ALU.subtract)
                S.op('vector', 'tensor_tensor', r=[OF_[n % 2], VAR], w=[OF_[n % 2]], out=OF_[n % 2].ap, in0=OF_[n % 2].ap, in1=VAR.ap, op=ALU.mult)
                ot = OUT[n % 2]
                for h in range(4):
                    hs = slice(h * 128, (h + 1) * 128)
                    S.op('vector', 'scalar_tensor_tensor', r=[OF_[n % 2], SM, rgt], w=[ot], out=ot.ap[:, hs], in0=OF_[n % 2].ap[:, hs],
                         scalar=SM.ap[:, l * 10 + 2 + h:l * 10 + 3 + h], in1=rgt.ap[:, hs], op0=ALU.mult, op1=ALU.mult)
                S.dma('gpsimd', MIX[0:4, :, c0:c0 + 128].rearrange("h p t -> p h t"),
                      ot.ap.rearrange("p (h t) -> p h t", h=4), r=[ot], w=[T(None, B_MIX[h][bi]) for h in range(4)])

            part1(0)
            for n in range(nch):
                if n + 1 < nch:
                    part1(n + 1)
                part2(n)
            if not has_ctx:
                S.dma('gpsimd', nst[(sidx * L + l) * 2 + 0, :, :], SF.ap, r=[SF], w=[T(None)])
        S.barrier()
        pool.release(m0)

    def attn_phase(l, diff):
        m0 = pool.mark()
        nkcmax = (TS + PAST) // 128
        KB_ = [pool.alloc(TS + PAST, BF16) for _ in range(2)]
        VB_ = [pool.alloc(nkcmax * 128, BF16) for _ in range(2)]
        QB_ = [pool.alloc(1024, BF16) for _ in range(2)]
        PT = [pool.alloc(512, BF16) for _ in range(4)]
        RC = [pool.alloc(512) for _ in range(2)]
        ACC = [pool.alloc(512) for _ in range(2)]
        O1 = pool.alloc(512)
        O2 = pool.alloc(512)
        OSQ = pool.alloc(512, BF16)
        RSD = pool.alloc(512)
        OUT = [pool.alloc(512, BF16) for _ in range(2)]
        LAM = pool.alloc(4)
        lam_init = 0.8 - 0.6 * math.exp(-0.3 * l)
        if diff:
            DL = pool.alloc(256)
            S.dma('sync', DL.ap, dlam[:, l * 256:(l + 1) * 256], w=[DL])
            PR = pool.alloc(128)
            S.op('vector', 'tensor_tensor', r=[DL], w=[PR], out=PR.ap.rearrange("p (a f) -> p a f", a=2),
                 in0=DL.ap.rearrange("p (a b f) -> p a b f", a=2, b=2)[:, :, 0, :],
                 in1=DL.ap.rearrange("p (a b f) -> p a b f", a=2, b=2)[:, :, 1, :], op=ALU.mult)
            S.op('vector', 'reduce_sum', r=[PR], w=[LAM], out=LAM.ap[:, 0:2], in_=PR.ap.rearrange("p (a f) -> p a f", a=2), axis=AX.X)
            S.op('scalar', 'activation', r=[LAM], w=[LAM], out=LAM.ap[:, 0:2], in_=LAM.ap[:, 0:2], func=AF.Exp)
            S.op('vector', 'tensor_tensor', r=[LAM], w=[LAM], out=LAM.ap[:, 2:3], in0=LAM.ap[:, 1:2], in1=LAM.ap[:, 0:1], op=ALU.subtract)
            S.op('vector', 'tensor_scalar', r=[LAM], w=[LAM], out=LAM.ap[:, 2:3], in0=LAM.ap[:, 2:3], scalar1=-lam_init, scalar2=None, op0=ALU.add)
            DNW = pool.alloc(4)
            S.op('vector', 'tensor_scalar', r=[SM], w=[DNW], out=DNW.ap, in0=SM.ap[:, l * 10 + 6:l * 10 + 10],
                 scalar1=1.0 - lam_init, scalar2=None, op0=ALU.mult)
        nkv = 4 if diff else 2
        scale = (64.0 ** -0.5) if diff else (128.0 ** -0.5)
        seqs = [(0, TS, True, 512)] + [(TS + s_ * TP, TP, False, 256) for s_ in range(NPS)]
        nb_ = 0
        nq_ = 0
        npt = 0
        nout = 0
        for (tok0, Tn, has_ctx, TQ) in seqs:
            nk_own = Tn // 128
            nkc = nk_own + (PAST // 128 if has_ctx else 0)
            b0 = tok0 // 128
            for g in range(nkv):
                kb = KB_[nb_ % 2]
                vb = VB_[nb_ % 2]
                nb_ += 1
                gk = (G_DK if diff else G_GK) + g
                S.dma('sync', kb.ap[:, 0:Tn], FMS[gk, :, tok0:tok0 + Tn],
                      r=[T(None, B_FMS[gk][b]) for b in range(b0, b0 + nk_own)], w=[kb])
                vsrc = DVT if diff else GVT
                S.dma('sync', vb.ap[:, 0:nk_own * 128].rearrange("p (k e) -> p k e", e=128),
                      vsrc[b0:b0 + nk_own, :, g * 128:(g + 1) * 128].rearrange("k p e -> p k e"),
                      r=[T(None, B_TOK['dv' if diff else 'gv'][b]) for b in range(b0, b0 + nk_own)], w=[vb])
                if has_ctx:
                    ck = cdk if diff else cgk
                    cvv = cdv if diff else cgv
                    S.dma('gpsimd', kb.ap[:, Tn:Tn + PAST], ck[l * nkv + g, :, :], w=[kb])
                    for kc in range(PAST // 128):
                        S.dma('gpsimd', vb.ap[:, (nk_own + kc) * 128:(nk_own + kc + 1) * 128],
                              cvv[(l * nkv + g) * 2 + kc, :, :], w=[vb])
                if diff:
                    units = [(g, None)]
                else:
                    units = [(4 * g + 2 * pr, 4 * g + 2 * pr + 1) for pr in range(2)]
                for q0 in range(0, Tn, TQ):
                    qa = tok0 + q0
                    for un in units:
                        qb = QB_[nq_ % 2]
                        nq_ += 1
                        if diff:
                            gq = G_DQ + g
                            S.dma('sync', qb.ap[:, 0:TQ], FMS[gq, :, qa:qa + TQ], r=fmbufs(gq, qa, TQ), w=[qb])
                        else:
                            for k_, hq in enumerate(un):
                                S.dma('sync', qb.ap[:, k_ * 512:k_ * 512 + TQ], FMS[G_GQ + hq, :, qa:qa + TQ],
                                      r=fmbufs(G_GQ + hq, qa, TQ), w=[qb])
                        for kc in range(nkc):
                            for k_ in range(2):
                                ps_s = PS[4 + (npt % 4)]
                                pt = PT[npt % 4]
                                npt += 1
                                if diff:
                                    S.op('tensor', 'matmul', r=[kb, qb], w=[ps_s], out=ps_s.ap[:, 0:TQ],
                                         lhsT=kb.ap[k_ * 64:(k_ + 1) * 64, kc * 128:(kc + 1) * 128],
                                         rhs=qb.ap[k_ * 64:(k_ + 1) * 64, 0:TQ], start=True, stop=True)
                                else:
                                    S.op('tensor', 'matmul', r=[kb, qb], w=[ps_s], out=ps_s.ap[:, 0:TQ],
                                         lhsT=kb.ap[:, kc * 128:(kc + 1) * 128], rhs=qb.ap[:, k_ * 512:k_ * 512 + TQ],
                                         start=True, stop=True)
                                S.op('scalar', 'activation', r=[ps_s], w=[pt], out=pt.ap[:, 0:TQ], in_=ps_s.ap[:, 0:TQ],
                                     func=AF.Exp, scale=scale)
                                S.op('tensor', 'matmul', r=[vb, pt], w=[PS[k_]], out=PS[k_].ap[:, 0:TQ],
                                     lhsT=vb.ap[:, kc * 128:(kc + 1) * 128], rhs=pt.ap[:, 0:TQ],
                                     start=(kc == 0), stop=(kc == nkc - 1))
                                pacc = PS[2 + k_]
                                if kc == 0:
                                    S.op('vector', 'tensor_copy', r=[pt], w=[pacc], out=pacc.ap[:, 0:TQ], in_=pt.ap[:, 0:TQ])
                                else:
                                    S.op('vector', 'tensor_tensor', r=[pt, pacc], w=[pacc], out=pacc.ap[:, 0:TQ],
                                         in0=pacc.ap[:, 0:TQ], in1=pt.ap[:, 0:TQ], op=ALU.add)
                        for k_ in range(2):
                            S.op('scalar', 'activation', r=[PS[2 + k_]], w=[ACC[k_]], out=ACC[k_].ap[:, 0:TQ],
                                 in_=PS[2 + k_].ap[:, 0:TQ], func=AF.Identity)
                            S.op('tensor', 'matmul', r=[CON, ACC[k_]], w=[PS[2 + k_]], out=PS[2 + k_].ap[:, 0:TQ],
                                 lhsT=CON.ap[:, C_ONE:C_ONE + 128], rhs=ACC[k_].ap[:, 0:TQ], start=True, stop=True)
                        for k_ in range(2):
                            S.op('vector', 'reciprocal', r=[PS[2 + k_]], w=[RC[k_]], out=RC[k_].ap[:, 0:TQ], in_=PS[2 + k_].ap[:, 0:TQ])
                        bl = range(qa // 128, (qa + TQ) // 128)
                        if not diff:
                            for k_, hq in enumerate(un):
                                ot = OUT[nout % 2]
                                nout += 1
                                S.op('vector', 'tensor_tensor', r=[PS[k_], RC[k_]], w=[ot], out=ot.ap[:, 0:TQ],
                                     in0=PS[k_].ap[:, 0:TQ], in1=RC[k_].ap[:, 0:TQ], op=ALU.mult)
                                S.dma('gpsimd', MIX[4 + hq, :, qa:qa + TQ], ot.ap[:, 0:TQ], r=[ot],
                                      w=[T(None, B_MIX[4 + hq][b]) for b in bl])
                        else:
                            S.op('vector', 'tensor_tensor', r=[PS[0], RC[0]], w=[O1], out=O1.ap[:, 0:TQ],
                                 in0=PS[0].ap[:, 0:TQ], in1=RC[0].ap[:, 0:TQ], op=ALU.mult)
                            S.op('vector', 'tensor_tensor', r=[PS[1], RC[1]], w=[O2], out=O2.ap[:, 0:TQ],
                                 in0=PS[1].ap[:, 0:TQ], in1=RC[1].ap[:, 0:TQ], op=ALU.mult)
                            S.op('vector', 'scalar_tensor_tensor', r=[O2, LAM, O1], w=[O1], out=O1.ap[:, 0:TQ], in0=O2.ap[:, 0:TQ],
                                 scalar=LAM.ap[:, 2:3], in1=O1.ap[:, 0:TQ], op0=ALU.mult, op1=ALU.add)
                            S.op('scalar', 'activation', r=[O1], w=[OSQ], out=OSQ.ap[:, 0:TQ], in_=O1.ap[:, 0:TQ], func=AF.Square)
                            S.op('tensor', 'matmul', r=[ONEB, OSQ], w=[PS[2]], out=PS[2].ap[:, 0:TQ], lhsT=ONEB.ap,
                                 rhs=OSQ.ap[:, 0:TQ], start=True, stop=True)
                            S.op('scalar', 'activation', r=[PS[2], EPSC], w=[RSD], out=RSD.ap[:, 0:TQ], in_=PS[2].ap[:, 0:TQ],
                                 func=AF.Sqrt, bias=EPSC.ap, scale=1.0 / 128)
                            S.op('vector', 'reciprocal', r=[RSD], w=[RSD], out=RSD.ap[:, 0:TQ], in_=RSD.ap[:, 0:TQ])
                            ot = OUT[nout % 2]
                            nout += 1
                            S.op('vector', 'scalar_tensor_tensor', r=[O1, DNW, RSD], w=[ot], out=ot.ap[:, 0:TQ], in0=O1.ap[:, 0:TQ],
                                 scalar=DNW.ap[:, g:g + 1], in1=RSD.ap[:, 0:TQ], op0=ALU.mult, op1=ALU.mult)
                            S.dma('gpsimd', MIX[12 + g, :, qa:qa + TQ], ot.ap[:, 0:TQ], r=[ot],
                                  w=[T(None, B_MIX[12 + g][b]) for b in bl])
        S.barrier()
        pool.release(m0)

    def wout_phase(l):
        bg_ensure(('wo', l))
        m0 = pool.mark()
        MX = [pool.alloc(512, BF16) for _ in range(32)]
        WB = [pool.alloc(2048, BF16) for _ in range(3)]
        XC = [pool.alloc(512) for _ in range(4)]
        nw = 0
        for t in range(NT):
            cond = tile_cond(t)
            t0 = t * 512
            mx = MX[(t % 2) * 16:(t % 2) * 16 + 16]
            for c in range(16):
                S.dma('sync', mx[c].ap, MIX[c, :, t0:t0 + 512],
                      r=[T(None, B_MIX[c][b]) for b in range(t * 4, t * 4 + 4)], w=[mx[c]])
            gate = der(l, cond, 1, 2)
            for dj in range(16):
                wt = WB[nw % 3]
                xc = XC[nw % 4]
                nw += 1
                S.dma('sync', wt.ap, WOs[l * 16 + dj, :, :], r=[wbuf(('wo', l), dj)], w=[wt])
                S.dma('sync', xc.ap, XT[dj, :, t0:t0 + 512], r=[T(None, B_XT[dj][t])], w=[xc])
                po = PS[dj % 4]
                for c in range(16):
                    S.op('tensor', 'matmul', r=[wt, mx[c]], w=[po], out=po.ap, lhsT=wt.ap[:, c * 128:(c + 1) * 128],
                         rhs=mx[c].ap, start=(c == 0), stop=(c == 15))
                S.op('vector', 'scalar_tensor_tensor', r=[po, gate, xc], w=[xc], out=xc.ap, in0=po.ap,
                     scalar=gate.ap[:, dj:dj + 1], in1=xc.ap, op0=ALU.mult, op1=ALU.add)
                S.dma('gpsimd', XT[dj, :, t0:t0 + 512], xc.ap, r=[xc], w=[T(None, B_XT[dj][t])])
        S.barrier()
        pool.release(m0)

    def final_phase():
        m0 = pool.mark()
        XB = [pool.alloc(512) for _ in range(16)]
        SQ = [pool.alloc(512, BF16) for _ in range(2)]
        RS = pool.alloc(512)
        YT = [pool.alloc(512) for _ in range(2)]
        YO = [pool.alloc(2048) for _ in range(4)]
        ny = 0
        for t in range(NT):
            prologue(t, XB, None, SQ, RS, None, None, None, PS[6])
            for c in range(16):
                yt = YT[c % 2]
                S.op('vector', 'scalar_tensor_tensor', r=[XB[c], FNW, RS], w=[yt], out=yt.ap, in0=XB[c].ap,
                     scalar=FNW.ap[:, c:c + 1], in1=RS.ap, op0=ALU.mult, op1=ALU.mult)
                pb = PS[c % 4]
                for sb in range(4):
                    S.op('tensor', 'transpose', r=[yt, IDF], w=[pb], out=pb.ap[:, sb * 128:(sb + 1) * 128],
                         in_=yt.ap[:, sb * 128:(sb + 1) * 128], identity=IDF.ap)
                for sb in range(4):
                    yo = YO[sb]
                    S.op('vector', 'tensor_copy', r=[pb], w=[yo], out=yo.ap[:, c * 128:(c + 1) * 128], in_=pb.ap[:, sb * 128:(sb + 1) * 128])
            for sb in range(4):
                r0 = t * 512 + sb * 128
                S.dma('sync', y[r0:r0 + 128, :], YO[sb].ap, r=[YO[sb]], w=[T(None)])
        S.barrier()
        pool.release(m0)

    plan = []
    for l in range(L):
        plan += [(ffn_phase, (l, 0)), (inproj_phase, (l,)), (ret_phase, (l,)), (attn_phase, (l, False)),
                 (attn_phase, (l, True)), (wout_phase, (l,)), (ffn_phase, (l, 1))]
    plan.append((final_phase, ()))
    for fn_, a_ in plan[:getattr(cfg, 'nphase', 1000)]:
        fn_(*a_)

    with nc.Block() as block:
        S.emit(block)
    es.close()
    return nc


def _prep_shared(cfg, inp):
    L, NF = cfg.L, cfg.NF
    f = np.float32
    out = {}
    wm = np.asarray(inp['w_mod'], f)
    out['wmod'] = np.ascontiguousarray(
        wm.reshape(L, 16, 128, 72, 2, 128).transpose(0, 3, 2, 4, 1, 5)).reshape(L * 72, 128, 4096)
    del wm
    out['bmod'] = np.ascontiguousarray(np.asarray(inp['b_mod'], f).reshape(L, 144, 128).transpose(2, 0, 1)).reshape(128, L * 144)
    out['normw'] = np.ascontiguousarray(np.asarray(inp['norm_w'], f).reshape(L, 3, 16, 128).transpose(3, 0, 1, 2)).reshape(128, L * 48)
    out['fnw'] = np.ascontiguousarray(np.asarray(inp['final_norm_w'], f).reshape(16, 128).T)
    w1 = np.asarray(inp['ffn_w_in'], f)
    out['w1'] = np.ascontiguousarray(
        w1.reshape(L, 2, 16, 128, 2, NF, 128).transpose(0, 1, 5, 3, 2, 4, 6)).reshape(L * 2 * NF, 128, 4096)
    del w1
    w2 = np.asarray(inp['ffn_w_out'], f)
    out['w2'] = np.ascontiguousarray(
        w2.reshape(L, 2, NF, 128, 16, 128).transpose(0, 1, 4, 3, 2, 5)).reshape(L * 2 * 16, 128, NF * 128)
    del w2
    win = np.asarray(inp['w_in'], f)
    fm, tm = _win_cols()
    wf = np.stack([win[:, :, cols] for cols in fm], 1)
    out['winf'] = np.ascontiguousarray(wf.reshape(L, NFM, 16, 128, 128).transpose(0, 1, 3, 2, 4)).reshape(L * NFM, 128, 2048)
    wt = np.zeros((L, 4, 128, 16, 512), f)
    for bi, cols in enumerate(tm):
        blk = win[:, :, cols].reshape(L, 16, 128, len(cols)).transpose(0, 2, 1, 3)
        wt[:, bi, :, :, 0:len(cols)] = blk
    out['wint'] = wt.reshape(L * 4, 128, 8192)
    wo = np.asarray(inp['w_out'], f)
    out['wout'] = np.ascontiguousarray(wo.reshape(L, 16, 128, 16, 128).transpose(0, 3, 2, 1, 4)).reshape(L * 16, 128, 2048)
    p128 = _perm_deint(128)
    qkn = np.asarray(inp['gqa_qk_norm'], f)[:, :, p128]
    gnw = np.asarray(inp['ret_gn_w'], f)
    dnw = np.asarray(inp['diff_norm_w'], f)
    sm = np.concatenate([qkn, gnw, dnw], 1)
    out['smalls'] = np.ascontiguousarray(sm.transpose(2, 0, 1)).reshape(128, L * 10)
    out['rdec'] = np.ascontiguousarray(np.broadcast_to(np.asarray(inp['ret_decay'], f).reshape(1, L * 8), (128, L * 8)))
    out['dlam'] = np.ascontiguousarray(np.broadcast_to(np.asarray(inp['diff_lambda'], f).reshape(1, L * 256), (128, L * 256)))
    out['rope'] = _rope_tables(cfg.TS)
    out['consts'] = _consts()
    return out


def _prep_core(cfg, inp, b):
    L = cfg.L
    f = np.float32
    p128 = _perm_deint(128)
    p64 = _perm_deint(64)
    d = {}
    xs = np.asarray(inp['x_sample'], f)[b]
    xp = np.asarray(inp['x_prompt'], f)[2 * b:2 * b + 2].reshape(-1, cfg.D)
    d['xin'] = np.concatenate([xs, xp], 0)
    cv = np.stack([np.asarray(inp['c'], f)[b].reshape(16, 128).T, np.asarray(inp['c_ctx'], f).reshape(16, 128).T], 1)
    d['cv'] = np.ascontiguousarray(cv).reshape(128, 32)
    st = np.asarray(inp['state_ret'], f)[b]
    d['stin'] = np.ascontiguousarray(st.transpose(0, 1, 3, 2, 4)).reshape(L * 2, 128, 512)
    gk = np.asarray(inp['cache_gqa_k'], f)[b][:, :, :, p128]
    d['cgk'] = np.ascontiguousarray(gk.transpose(0, 2, 3, 1)).reshape(L * 2, 128, cfg.PAST)
    gv = np.asarray(inp['cache_gqa_v'], f)[b]
    d['cgv'] = np.ascontiguousarray(gv.reshape(L, 2, 128, 2, 128).transpose(0, 3, 1, 2, 4)).reshape(L * 4, 128, 128)
    dk = np.asarray(inp['cache_diff_k'], f)[b]
    dk = dk.reshape(L, cfg.PAST, 4, 2, 64)[..., p64].reshape(L, cfg.PAST, 4, 128)
    d['cdk'] = np.ascontiguousarray(dk.transpose(0, 2, 3, 1)).reshape(L * 4, 128, cfg.PAST)
    dv = np.asarray(inp['cache_diff_v'], f)[b]
    d['cdv'] = np.ascontiguousarray(dv.reshape(L, 2, 128, 4, 128).transpose(0, 3, 1, 2, 4)).reshape(L * 8, 128, 128)
    return d


def run(cfg, inp, n_cores):
    nc = build(cfg)
    shared = _prep_shared(cfg, inp)
    in_maps = []
    for b in range(n_cores):
        m = dict(shared)
        m.update(_prep_core(cfg, inp, b))
        in_maps.append(m)
    res = run_bass_kernel_spmd(nc, in_maps, core_ids=list(range(n_cores)))
    L = cfg.L
    inv128 = np.argsort(_perm_deint(128))
    inv64 = np.argsort(_perm_deint(64))
    ys, yp, st, gk, gv, dk, dv = [], [], [], [], [], [], []
    for b in range(n_cores):
        r = res.results[b]
        yy = np.asarray(r['y'])
        ys.append(yy[:cfg.TS])
        yp.append(yy[cfg.TS:].reshape(2, cfg.TP, cfg.D))
        s_ = np.asarray(r['nst']).reshape(2, L, 2, 128, 4, 128).transpose(0, 1, 2, 4, 3, 5)
        st.append(s_)
        k_ = np.asarray(r['ngk']).reshape(2, L, 256, 2, 128)[..., inv128]
        gk.append(k_)
        gv.append(np.asarray(r['ngv']).reshape(2, L, 256, 2, 128))
        k2 = np.asarray(r['ndk']).reshape(2, L, 256, 4, 2, 64)[..., inv64].reshape(2, L, 256, 4, 128)
        dk.append(k2)
        dv.append(np.asarray(r['ndv']).reshape(2, L, 256, 4, 128))
    cat = lambda xs_: np.ascontiguousarray(np.concatenate(xs_, 0)).astype(np.float32)
    return (cat(yp), np.ascontiguousarray(np.stack(ys, 0)).astype(np.float32), cat(st), cat(gk), cat(gv), cat(dk), cat(dv))


def kernel(**inputs):
    cfg = Cfg()
    return run(cfg, inputs, 8)
```

```python
import math
import numpy as np
import ml_dtypes
import concourse.bass as bass
import concourse.mybir as mybir
from concourse.bass_utils import run_bass_kernel_spmd

F32 = mybir.dt.float32
BF16 = mybir.dt.bfloat16
AF = mybir.ActivationFunctionType
ALU = mybir.AluOpType
AX = mybir.AxisListType
ENGS = ('tensor', 'vector', 'scalar', 'gpsimd', 'sync')
NDS = 40
EPS = 1e-6


class Cfg:
    def __init__(self, TS=4096, DFF=5632, L=2):
        self.D = 2048; self.NCH = 16
        self.DFF = DFF; self.NF = DFF // 128
        self.TS = TS; self.TP = 256; self.NPS = 2; self.PAST = 256
        self.L = L
        self.NTOK = TS + self.NPS * self.TP
        self.NT = self.NTOK // 512
        self.NB = self.NTOK // 128


class Buf:
    __slots__ = ('w', 'r')

    def __init__(self):
        self.w = None
        self.r = {}


class T:
    __slots__ = ('ap', 'b')

    def __init__(self, ap, b=None):
        self.ap = ap
        self.b = b if b is not None else Buf()

    def v(self, ap):
        return T(ap, self.b)


class Sched:
    def __init__(self, nc):
        self.nc = nc
        self.q = {e: [] for e in ENGS}
        self.cnt = {e: 0 for e in ENGS}
        self.sems = {}
        for e in ENGS:
            self.sems['p_' + e] = nc.alloc_semaphore(name='p_' + e)
        for i in range(NDS):
            self.sems[('d', i)] = nc.alloc_semaphore(name='d%d' % i)
        self.duse = [0] * NDS
        self.dpool = {'sync': list(range(0, 24)), 'gpsimd': list(range(24, NDS))}
        self.dnext = {'sync': 0, 'gpsimd': 0}
        self.waited = {e: {} for e in ENGS}

    def _wait(self, eng, deps):
        wd = self.waited[eng]
        for key, val in deps:
            if wd.get(key, 0) >= val:
                continue
            wd[key] = val
            self.q[eng].append(('wait', key, val))

    def _deps(self, eng, r, w):
        deps = []
        for t in r:
            if t.b.w is not None:
                deps.append(t.b.w)
        for t in w:
            if t.b.w is not None:
                deps.append(t.b.w)
            deps.extend(t.b.r.items())
        if eng == 'tensor':
            deps = [d for d in deps if d[0] != 'p_tensor']
        return deps

    def _mark(self, tok, r, w):
        for t in r:
            if t.b.r.get(tok[0], 0) < tok[1]:
                t.b.r[tok[0]] = tok[1]
        for t in w:
            t.b.w = tok
            t.b.r = {}

    def op(self, eng, meth, r=(), w=(), **kw):
        self._wait(eng, self._deps(eng, r, w))
        self.cnt[eng] += 1
        tok = ('p_' + eng, self.cnt[eng])
        self.q[eng].append(('op', meth, kw))
        self._mark(tok, r, w)

    def dma(self, eng, out, in_, r=(), w=(), **kw):
        pl = self.dpool[eng]
        i = pl[self.dnext[eng] % len(pl)]
        self.dnext[eng] += 1
        deps = self._deps(eng, r, w)
        if self.duse[i] > 0:
            deps.append((('d', i), 16 * self.duse[i]))
        self._wait(eng, deps)
        self.duse[i] += 1
        tok = (('d', i), 16 * self.duse[i])
        self.q[eng].append(('dma', out, in_, kw, ('d', i)))
        self._mark(tok, r, w)

    def barrier(self):
        allv = [('p_' + e, self.cnt[e]) for e in ENGS if self.cnt[e] > 0]
        allv += [(('d', i), 16 * self.duse[i]) for i in range(NDS) if self.duse[i] > 0]
        for e in ENGS:
            self._wait(e, allv)

    def emit(self, block):
        nc = self.nc
        sems = self.sems
        q = self.q

        def run(eng, name):
            psem = sems['p_' + name]
            for it in q[name]:
                if it[0] == 'wait':
                    eng.wait_ge(sems[it[1]], it[2])
                elif it[0] == 'op':
                    getattr(eng, it[1])(**it[2]).then_inc(psem, 1)
                else:
                    eng.dma_start(out=it[1], in_=it[2], **it[3]).then_inc(sems[it[4]], 16)

        @block.tensor
        def _(e):
            run(e, 'tensor')

        @block.vector
        def _(e):
            run(e, 'vector')

        @block.scalar
        def _(e):
            run(e, 'scalar')

        @block.gpsimd
        def _(e):
            run(e, 'gpsimd')

        @block.sync
        def _(e):
            run(e, 'sync')


class Pool:
    def __init__(self, ap, ncols):
        self.ap = ap
        self.n = ncols
        self.off = 0

    def mark(self):
        return self.off

    def release(self, m):
        self.off = m

    def alloc(self, cols, dt=F32, parts=128):
        n32 = cols if dt == F32 else (cols + 1) // 2
        assert self.off + n32 <= self.n, ("SBUF pool overflow", self.off, n32, self.n)
        v = self.ap[0:parts, self.off:self.off + n32]
        self.off += n32
        if dt != F32:
            v = v.bitcast(dt)
        return T(v)


def _perm_deint(n):
    return np.concatenate([np.arange(0, n, 2), np.arange(1, n, 2)])


def _win_cols():
    sizes = [512, 512, 512, 512, 1024, 256, 256, 512, 512, 512]
    st = np.concatenate([[0], np.cumsum(sizes)])
    rq, rk, rv, rg, gq, gk, gv, dq, dk, dv = [np.arange(st[i], st[i + 1]) for i in range(10)]
    p128 = _perm_deint(128)
    p64 = _perm_deint(64)
    fm = []
    for h in range(4):
        fm.append(rq[h * 128:(h + 1) * 128])
    for h in range(4):
        fm.append(rk[h * 128:(h + 1) * 128])
    for h in range(4):
        fm.append(rg[h * 128:(h + 1) * 128])
    for h in range(8):
        fm.append(gq[h * 128:(h + 1) * 128][p128])
    for h in range(2):
        fm.append(gk[h * 128:(h + 1) * 128][p128])
    for h in range(4):
        c = dq[h * 128:(h + 1) * 128]
        fm.append(np.concatenate([c[0:64][p64], c[64:128][p64]]))
    for h in range(4):
        c = dk[h * 128:(h + 1) * 128]
        fm.append(np.concatenate([c[0:64][p64], c[64:128][p64]]))
    tm = [rk, rv, gv, dv]
    return fm, tm


G_RQ, G_RK, G_RG, G_GQ, G_GK, G_DQ, G_DK, NFM = 0, 4, 8, 12, 20, 22, 26, 30


def _rope_tables(TS):
    rows = TS // 64

    def tab(d):
        axis = d // 2
        inv = 1.0 / (10000.0 ** (np.arange(0, axis, 2, dtype=np.float32) / axis))
        r = np.repeat(np.arange(rows, dtype=np.float32), 64)
        cl = np.tile(np.arange(64, dtype=np.float32), rows)
        ang = np.concatenate([r[:, None] * inv, cl[:, None] * inv], axis=-1).astype(np.float32)
        return np.cos(ang).T.astype(np.float32), np.sin(ang).T.astype(np.float32)

    cg, sg = tab(128)
    cd, sd = tab(64)
    CG = np.concatenate([cg, cg], 0)
    SG = np.concatenate([-sg, sg], 0)
    CD = np.concatenate([cd, cd, cd, cd], 0)
    SD = np.concatenate([-sd, sd, -sd, sd], 0)
    return np.ascontiguousarray(np.stack([CG, SG, CD, SD], 0)).astype(np.float32)


C_ID, C_ONE, C_PG, C_PD, C_POS, C_NEG, C_MF, C_MB, C_I1, C_IB, C_KF, C_KB, NCONST = \
    0, 128, 256, 384, 512, 640, 768, 896, 1024, 1152, 1280, 1281, 1282


def _consts():
    c = np.zeros((128, NCONST), np.float32)
    c[:, C_ID:C_ID + 128] = np.eye(128)
    c[:, C_ONE:C_ONE + 128] = 1.0
    k = np.arange(128)
    pg = np.zeros((128, 128), np.float32)
    pg[(k + 64) % 128, k] = 1.0
    c[:, C_PG:C_PG + 128] = pg
    pd = np.zeros((128, 128), np.float32)
    for m in range(128):
        blk = (m // 64) * 64
        pd[blk + ((m % 64) + 32) % 64, m] = 1.0
    c[:, C_PD:C_PD + 128] = pd
    j = np.arange(128)[:, None].astype(np.float32)
    i = np.arange(128)[None, :].astype(np.float32)
    dif = i - j
    c[:, C_POS:C_POS + 128] = np.maximum(dif, 0)
    c[:, C_NEG:C_NEG + 128] = np.maximum(-dif, 0)
    c[:, C_MF:C_MF + 128] = (dif >= 0)
    c[:, C_MB:C_MB + 128] = (dif <= 0)
    c[:, C_I1:C_I1 + 128] = i + 1.0
    c[:, C_IB:C_IB + 128] = 128.0 - i
    c[:, C_KF] = 127.0 - np.arange(128)
    c[:, C_KB] = np.arange(128)
    return c


def build(cfg):
    nc = bass.Bass("TRN2", target_bir_lowering=False)
    D, NCH, NF, L = cfg.D, cfg.NCH, cfg.NF, cfg.L
    TS, TP, NPS, PAST = cfg.TS, cfg.TP, cfg.NPS, cfg.PAST
    NTOK, NT, NB = cfg.NTOK, cfg.NT, cfg.NB

    def din(name, shape, dt=F32):
        return nc.dram_tensor(name, list(shape), dt, kind="ExternalInput").ap()

    def dout(name, shape):
        return nc.dram_tensor(name, list(shape), F32, kind="ExternalOutput").ap()

    def dscr(name, shape, dt):
        return nc.dram_tensor(name, list(shape), dt, kind="Internal").ap()

    xin = din("xin", [NTOK, D])
    cv_in = din("cv", [128, 32])
    wmod = din("wmod", [L * 72, 128, 4096])
    bmod = din("bmod", [128, L * 144])
    normw = din("normw", [128, L * 48])
    fnw = din("fnw", [128, 16])
    w1 = din("w1", [L * 2 * NF, 128, 4096])
    w2 = din("w2", [L * 2 * 16, 128, NF * 128])
    winf = din("winf", [L * NFM, 128, 2048])
    wint = din("wint", [L * 4, 128, 16 * 512])
    wout = din("wout", [L * 16, 128, 2048])
    smalls = din("smalls", [128, L * 10])
    rdec = din("rdec", [128, L * 8])
    dlam = din("dlam", [128, L * 256])
    stin = din("stin", [L * 2, 128, 512])
    cgk = din("cgk", [L * 2, 128, PAST])
    cgv = din("cgv", [L * 2 * 2, 128, 128])
    cdk = din("cdk", [L * 4, 128, PAST])
    cdv = din("cdv", [L * 4 * 2, 128, 128])
    rope = din("rope", [4, 128, TS])
    consts = din("consts", [128, NCONST])

    y = dout("y", [NTOK, D])
    nst = dout("nst", [NPS * L * 2, 128, 512])
    ngk = dout("ngk", [NPS * L, 256, 256])
    ngv = dout("ngv", [NPS * L, 256, 256])
    ndk = dout("ndk", [NPS * L, 256, 512])
    ndv = dout("ndv", [NPS * L, 256, 512])

    XT = dscr("XT", [16, 128, NTOK], F32)
    W1s = dscr("W1s", [L * 2 * NF, 128, 4096], BF16)
    W2s = dscr("W2s", [L * 2 * 16, 128, NF * 128], BF16)
    WFs = dscr("WFs", [L * NFM, 128, 2048], BF16)
    WTs = dscr("WTs", [L * 4, 128, 16 * 512], BF16)
    WOs = dscr("WOs", [L * 16, 128, 2048], BF16)
    FMS = dscr("FMS", [NFM, 128, NTOK], BF16)
    KTOK = dscr("KTOK", [NB, 128, 512], BF16)
    VTOK = dscr("VTOK", [NB, 128, 512], BF16)
    GVT = dscr("GVT", [NB, 128, 256], BF16)
    DVT = dscr("DVT", [NB, 128, 512], BF16)
    MIX = dscr("MIX", [16, 128, NTOK], BF16)
    B_XT = [[Buf() for _ in range(NT)] for _ in range(16)]
    B_W = {}
    B_FMS = [[Buf() for _ in range(NB)] for _ in range(NFM)]
    B_TOK = {n: [Buf() for _ in range(NB)] for n in ('k', 'v', 'gv', 'dv')}
    B_MIX = [[Buf() for _ in range(NB)] for _ in range(16)]
    RO = T(None)

    S = Sched(nc)
    NPOOL = 44000
    from contextlib import ExitStack
    es = ExitStack()
    pool_t = es.enter_context(nc.sbuf_tensor("pool", [128, NPOOL], F32))
    ps_t = es.enter_context(nc.psum_tensor("ps", [128, 8 * 512], F32))
    pool = Pool(pool_t[:], NPOOL)
    PS = [T(ps_t[:, b * 512:(b + 1) * 512]) for b in range(8)]

    def fmbufs(g, t0, n):
        return [T(None, B_FMS[g][b]) for b in range(t0 // 128, (t0 + n) // 128)]

    CON = pool.alloc(NCONST)
    S.dma('sync', CON.ap, consts[:, :], w=[CON])
    IDF = CON.v(CON.ap[:, C_ID:C_ID + 128])
    CB = pool.alloc(512, BF16)
    S.op('vector', 'tensor_copy', r=[CON], w=[CB], out=CB.ap, in_=CON.ap[:, 0:512])
    IDB = CB.v(CB.ap[:, 0:128]); ONEB = CB.v(CB.ap[:, 128:256])
    PGB = CB.v(CB.ap[:, 256:384]); PDB = CB.v(CB.ap[:, 384:512])
    EPSC = pool.alloc(1)
    S.op('vector', 'memset', w=[EPSC], ap=EPSC.ap, constant=EPS)
    SM = pool.alloc(L * 10)
    S.dma('sync', SM.ap, smalls[:, :], w=[SM])
    FNW = pool.alloc(16)
    S.dma('sync', FNW.ap, fnw[:, :], w=[FNW])
    DER = pool.alloc(L * 2 * 9 * 16)

    def der(l, cond, i, kind):
        o = (((l * 2 + cond) * 3 + i) * 3 + kind) * 16
        return DER.v(DER.ap[:, o:o + 16])

    m0 = pool.mark()
    xl = [pool.alloc(2048) for _ in range(2)]
    xs = [pool.alloc(2048) for _ in range(2)]
    for tb in range(NB):
        a = xl[tb % 2]
        S.dma('sync', a.ap, xin[tb * 128:(tb + 1) * 128, :], w=[a])
        st = xs[tb % 2]
        for q4 in range(4):
            pb = PS[(tb * 4 + q4) % 8]
            for k in range(4):
                c = q4 * 4 + k
                S.op('tensor', 'transpose', r=[a, IDF], w=[pb], out=pb.ap[:, k * 128:(k + 1) * 128],
                     in_=a.ap[:, c * 128:(c + 1) * 128], identity=IDF.ap)
            eng = 'vector' if q4 % 2 == 0 else 'scalar'
            if eng == 'vector':
                S.op('vector', 'tensor_copy', r=[pb], w=[st], out=st.ap[:, q4 * 512:(q4 + 1) * 512], in_=pb.ap)
            else:
                S.op('scalar', 'activation', r=[pb], w=[st], out=st.ap[:, q4 * 512:(q4 + 1) * 512], in_=pb.ap,
                     func=AF.Copy)
        wb = [T(None, B_XT[c][tb // 4]) for c in range(16)]
        S.dma('sync', XT[:, :, tb * 128:(tb + 1) * 128].rearrange("c p t -> p c t"),
              st.ap.rearrange("p (c t) -> p c t", c=16), r=[st], w=wb)
    S.barrier()
    pool.release(m0)

    m0 = pool.mark()
    CV = pool.alloc(32)
    S.dma('sync', CV.ap, cv_in[:, :], w=[CV])
    SC_ = pool.alloc(32)
    S.op('scalar', 'activation', r=[CV], w=[SC_], out=SC_.ap, in_=CV.ap, func=AF.Silu)
    SCr = SC_.ap.rearrange("p (k c) -> p c k", k=2)
    SCc = pool.alloc(32)
    S.op('vector', 'tensor_copy', r=[SC_], w=[SCc], out=SCc.ap.rearrange("p (c k) -> p c k", k=2), in_=SCr)
    BM = pool.alloc(L * 144)
    S.dma('sync', BM.ap, bmod[:, :], w=[BM])
    NW = pool.alloc(L * 48)
    S.dma('sync', NW.ap, normw[:, :], w=[NW])
    MODT = pool.alloc(L * 2 * 144)
    wmb = [pool.alloc(4096) for _ in range(3)]
    for l in range(L):
        mp = PS[l % 2]
        for g in range(72):
            wt = wmb[(l * 72 + g) % 3]
            S.dma('sync', wt.ap, wmod[l * 72 + g, :, :], w=[wt])
            for mm in range(2):
                m = 2 * g + mm
                for c in range(16):
                    S.op('tensor', 'matmul', r=[wt, SCc], w=[mp], out=mp.ap[:, 2 * m:2 * m + 2],
                         lhsT=wt.ap[:, (mm * 16 + c) * 128:(mm * 16 + c + 1) * 128],
                         rhs=SCc.ap[:, 2 * c:2 * c + 2], start=(c == 0), stop=(c == 15))
        for cond in range(2):
            o = (l * 2 + cond) * 144
            S.op('vector', 'tensor_tensor', r=[mp, BM], w=[MODT],
                 out=MODT.ap[:, o:o + 144],
                 in0=mp.ap[:, 0:288].rearrange("p (m k) -> p m k", k=2)[:, :, cond],
                 in1=BM.ap[:, l * 144:(l + 1) * 144], op=ALU.add)
        for cond in range(2):
            o = (l * 2 + cond) * 144
            for i in range(3):
                S.op('vector', 'scalar_tensor_tensor', r=[MODT, NW], w=[DER],
                     out=der(l, cond, i, 0).ap, in0=MODT.ap[:, o + (3 * i + 1) * 16:o + (3 * i + 2) * 16],
                     scalar=1.0, in1=NW.ap[:, l * 48 + i * 16:l * 48 + (i + 1) * 16], op0=ALU.add, op1=ALU.mult)
                S.op('vector', 'tensor_copy', r=[MODT], w=[DER],
                     out=der(l, cond, i, 1).ap, in_=MODT.ap[:, o + (3 * i) * 16:o + (3 * i + 1) * 16])
                S.op('vector', 'tensor_scalar', r=[MODT], w=[DER],
                     out=der(l, cond, i, 2).ap, in0=MODT.ap[:, o + (3 * i + 2) * 16:o + (3 * i + 3) * 16],
                     scalar1=(1.0 if i == 1 else 0.5), scalar2=None, op0=ALU.mult)
    S.barrier()
    pool.release(m0)

    CW = 2048
    stg = [pool.alloc(CW, BF16) for _ in range(3)]
    chunks = []

    def add_rows(src, dst, n, width, key):
        for r_ in range(n):
            B_W[(key, r_)] = Buf()
            for c0 in range(0, width, CW):
                cw = min(CW, width - c0)
                chunks.append((src[r_, :, c0:c0 + cw], dst[r_, :, c0:c0 + cw], cw, (key, r_)))

    order = []
    for l in range(L):
        order.append((w1[l * 2 * NF:(l * 2 + 1) * NF], W1s[l * 2 * NF:(l * 2 + 1) * NF], NF, 4096, ('w1', l, 0)))
        order.append((w2[l * 32:l * 32 + 16], W2s[l * 32:l * 32 + 16], 16, NF * 128, ('w2', l, 0)))
        order.append((winf[l * NFM:(l + 1) * NFM], WFs[l * NFM:(l + 1) * NFM], NFM, 2048, ('wf', l)))
        order.append((wint[l * 4:(l + 1) * 4], WTs[l * 4:(l + 1) * 4], 4, 8192, ('wt', l)))
        order.append((wout[l * 16:(l + 1) * 16], WOs[l * 16:(l + 1) * 16], 16, 2048, ('wo', l)))
        order.append((w1[(l * 2 + 1) * NF:(l * 2 + 2) * NF], W1s[(l * 2 + 1) * NF:(l * 2 + 2) * NF], NF, 4096, ('w1', l, 1)))
        order.append((w2[l * 32 + 16:l * 32 + 32], W2s[l * 32 + 16:l * 32 + 32], 16, NF * 128, ('w2', l, 1)))
    for o_ in order:
        add_rows(*o_)
    bgs = {'i': 0, 'pend': None, 'emitted': set()}

    def bg_step(n=1):
        for _ in range(n):
            i = bgs['i']
            if i < len(chunks):
                src_, dst_, cw, kr = chunks[i]
                s_ = stg[i % 3]
                S.dma('gpsimd', s_.ap[:, 0:cw], src_, w=[s_], max_dma_last_dim=8192)
                bgs['i'] = i + 1
            else:
                src_ = None
            p_ = bgs['pend']
            if p_ is not None:
                S.dma('gpsimd', p_[1], p_[0].ap[:, 0:p_[2]], r=[p_[0]], w=[T(None, B_W[p_[3]])])
                bgs['emitted'].add(p_[3][0])
                bgs['pend'] = None
            if src_ is not None:
                bgs['pend'] = (s_, dst_, cw, kr)

    def bg_ensure(key):
        last = max(i for i, c_ in enumerate(chunks) if c_[3][0] == key)
        while bgs['i'] <= last or (bgs['pend'] is not None and bgs['pend'][3][0] == key):
            bg_step()

    bg_ensure(('w1', 0, 0))
    bg_ensure(('w2', 0, 0))

    def wbuf(key, r_):
        return T(None, B_W[(key, r_)])

    def prologue(t, XB, HT, SQ, RS, TMP, sc, sh, psb):
        t0 = t * 512
        for c in range(16):
            S.dma('sync', XB[c].ap, XT[c, :, t0:t0 + 512], r=[T(None, B_XT[c][t])], w=[XB[c]])
        for c in range(16):
            sq = SQ[c % 2]
            S.op('scalar', 'activation', r=[XB[c]], w=[sq], out=sq.ap, in_=XB[c].ap, func=AF.Square)
            S.op('tensor', 'matmul', r=[ONEB, sq], w=[psb], out=psb.ap, lhsT=ONEB.ap, rhs=sq.ap,
                 start=(c == 0), stop=(c == 15))
        S.op('scalar', 'activation', r=[psb, EPSC], w=[RS], out=RS.ap, in_=psb.ap, func=AF.Sqrt,
             bias=EPSC.ap, scale=1.0 / D)
        if HT is None:
            S.op('vector', 'reciprocal', r=[RS], w=[RS], out=RS.ap, in_=RS.ap)
            return
        RP = PS[7]
        S.op('vector', 'reciprocal', r=[RS], w=[RP], out=RP.ap, in_=RS.ap)
        for c in range(16):
            tm = TMP[c % 2]
            S.op('vector', 'scalar_tensor_tensor', r=[XB[c], RP, sc], w=[tm], out=tm.ap, in0=XB[c].ap,
                 scalar=sc.ap[:, c:c + 1], in1=RP.ap, op0=ALU.mult, op1=ALU.mult)
            S.op('scalar', 'activation', r=[tm, sh], w=[HT[c]], out=HT[c].ap, in_=tm.ap, func=AF.Identity,
                 bias=sh.ap[:, c:c + 1], scale=1.0)

    def tile_cond(t):
        return 0 if t * 512 < TS else 1

    def ffn_phase(l, i):
        bg_ensure(('w1', l, i))
        bg_ensure(('w2', l, i))
        m0 = pool.mark()
        HT = [pool.alloc(512, BF16) for _ in range(16)]
        AT = [pool.alloc(512, BF16) for _ in range(NF)]
        W1B = [pool.alloc(4096, BF16) for _ in range(3)]
        W2B = [pool.alloc(NF * 128, BF16) for _ in range(3)]
        SQ = [pool.alloc(512, BF16) for _ in range(2)]
        RS = pool.alloc(512)
        TMP = [pool.alloc(512) for _ in range(2)]
        SG = [pool.alloc(512) for _ in range(2)]
        XS = [pool.alloc(512) for _ in range(4)]
        XC = [pool.alloc(512) for _ in range(4)]
        ii = 0 if i == 0 else 2
        st_ = {'n1': 0, 'n2': 0, 'nx': 0, 'xa': {}, 'xb': {}}
        psb = PS[6]
        RP = PS[7]

        def load_x(t, c, which):
            xs = XS[st_['nx'] % 4]
            st_['nx'] += 1
            S.dma('sync', xs.ap, XT[c, :, t * 512:t * 512 + 512], r=[T(None, B_XT[c][t])], w=[xs])
            st_[which][c] = xs

        def comp_a(t, c):
            xs = st_['xa'].pop(c)
            sq = SQ[c % 2]
            S.op('scalar', 'activation', r=[xs], w=[sq], out=sq.ap, in_=xs.ap, func=AF.Square)
            S.op('tensor', 'matmul', r=[ONEB, sq], w=[psb], out=psb.ap, lhsT=ONEB.ap, rhs=sq.ap,
                 start=(c == 0), stop=(c == 15))

        def pro_mid():
            S.op('scalar', 'activation', r=[psb, EPSC], w=[RS], out=RS.ap, in_=psb.ap, func=AF.Sqrt,
                 bias=EPSC.ap, scale=1.0 / D)
            S.op('vector', 'reciprocal', r=[RS], w=[RP], out=RP.ap, in_=RS.ap)

        def comp_b(t, c):
            cond = tile_cond(t)
            sc = der(l, cond, ii, 0)
            sh = der(l, cond, ii, 1)
            xs = st_['xb'].pop(c)
            tm = TMP[c % 2]
            S.op('vector', 'scalar_tensor_tensor', r=[xs, RP, sc], w=[tm], out=tm.ap, in0=xs.ap,
                 scalar=sc.ap[:, c:c + 1], in1=RP.ap, op0=ALU.mult, op1=ALU.mult)
            S.op('scalar', 'activation', r=[tm, sh], w=[HT[c]], out=HT[c].ap, in_=tm.ap, func=AF.Identity,
                 bias=sh.ap[:, c:c + 1], scale=1.0)

        for c0 in range(0, 16, 2):
            load_x(0, c0, 'xa'); load_x(0, c0 + 1, 'xa')
            comp_a(0, c0); comp_a(0, c0 + 1)
        pro_mid()
        for c0 in range(0, 16, 2):
            load_x(0, c0, 'xb'); load_x(0, c0 + 1, 'xb')
            comp_b(0, c0); comp_b(0, c0 + 1)

        for t in range(NT):
            cond = tile_cond(t)
            t0 = t * 512
            nxt = t + 1 if t + 1 < NT else None
            if t > 0:
                comp_b(t, 14); comp_b(t, 15)
            for j in range(NF):
                wt = W1B[st_['n1'] % 3]
                st_['n1'] += 1
                S.dma('sync', wt.ap, W1s[(l * 2 + i) * NF + j, :, :], r=[wbuf(('w1', l, i), j)], w=[wt])
                pg = PS[(j % 2) * 2]
                pu = PS[(j % 2) * 2 + 1]
                for c in range(16):
                    S.op('tensor', 'matmul', r=[wt, HT[c]], w=[pg], out=pg.ap,
                         lhsT=wt.ap[:, c * 256:c * 256 + 128], rhs=HT[c].ap, start=(c == 0), stop=(c == 15))
                for c in range(16):
                    S.op('tensor', 'matmul', r=[wt, HT[c]], w=[pu], out=pu.ap,
                         lhsT=wt.ap[:, c * 256 + 128:c * 256 + 256], rhs=HT[c].ap, start=(c == 0), stop=(c == 15))
                st_['bgc'] = st_.get('bgc', 0) + 1
                if st_['bgc'] % 3 == 0:
                    bg_step()
                sg = SG[j % 2]
                S.op('scalar', 'activation', r=[pg], w=[sg], out=sg.ap, in_=pg.ap, func=AF.Silu)
                S.op('vector', 'tensor_tensor', r=[sg, pu], w=[AT[j]], out=AT[j].ap, in0=sg.ap, in1=pu.ap, op=ALU.mult)
            gate = der(l, cond, ii, 2)
            for dj in range(16):
                wt = W2B[st_['n2'] % 3]
                xc = XC[st_['n2'] % 4]
                st_['n2'] += 1
                S.dma('sync', wt.ap, W2s[(l * 2 + i) * 16 + dj, :, :], r=[wbuf(('w2', l, i), dj)], w=[wt])
                S.dma('sync', xc.ap, XT[dj, :, t0:t0 + 512], r=[T(None, B_XT[dj][t])], w=[xc])
                st_['bgc'] = st_.get('bgc', 0) + 1
                if st_['bgc'] % 3 == 0:
                    bg_step()
                po = PS[4 + dj % 2]
                for c in range(NF):
                    S.op('tensor', 'matmul', r=[wt, AT[c]], w=[po], out=po.ap,
                         lhsT=wt.ap[:, c * 128:(c + 1) * 128], rhs=AT[c].ap, start=(c == 0), stop=(c == NF - 1))
                S.op('vector', 'scalar_tensor_tensor', r=[po, gate, xc], w=[xc], out=xc.ap, in0=po.ap,
                     scalar=gate.ap[:, dj:dj + 1], in1=xc.ap, op0=ALU.mult, op1=ALU.add)
                S.dma('gpsimd', XT[dj, :, t0:t0 + 512], xc.ap, r=[xc], w=[T(None, B_XT[dj][t])])
                if nxt is not None:
                    if dj < 8:
                        if dj >= 1:
                            comp_a(nxt, 2 * dj - 2); comp_a(nxt, 2 * dj - 1)
                        load_x(nxt, 2 * dj, 'xa'); load_x(nxt, 2 * dj + 1, 'xa')
                    elif dj == 8:
                        comp_a(nxt, 14); comp_a(nxt, 15)
                        pro_mid()
                        load_x(nxt, 0, 'xb'); load_x(nxt, 1, 'xb')
                    else:
                        comp_b(nxt, 2 * (dj - 9)); comp_b(nxt, 2 * (dj - 9) + 1)
                        load_x(nxt, 2 * (dj - 8), 'xb'); load_x(nxt, 2 * (dj - 8) + 1, 'xb')
        S.barrier()
        pool.release(m0)

    def inproj_phase(l):
        bg_ensure(('wf', l))
        bg_ensure(('wt', l))
        m0 = pool.mark()
        XB = [pool.alloc(512) for _ in range(16)]
        HT = [pool.alloc(512, BF16) for _ in range(16)]
        WFB = [pool.alloc(2048, BF16) for _ in range(3)]
        WTB = [pool.alloc(8192, BF16) for _ in range(2)]
        SQ = [pool.alloc(512, BF16) for _ in range(2)]
        RS = pool.alloc(512)
        TMP = [pool.alloc(512) for _ in range(2)]
        RT = [pool.alloc(512) for _ in range(4)]
        STG = [pool.alloc(512, BF16) for _ in range(4)]
        QN = [pool.alloc(512, BF16) for _ in range(2)]
        R1 = [pool.alloc(512) for _ in range(2)]
        R2 = [pool.alloc(512) for _ in range(2)]
        RQ = [pool.alloc(512) for _ in range(2)]
        TST = [pool.alloc(512, BF16) for _ in range(2)]
        OF = [pool.alloc(512) for _ in range(2)]
        OT = [pool.alloc(512) for _ in range(2)]
        nw = 0
        nst_ = 0
        nq = 0
        for t in range(NT):
            cond = tile_cond(t)
            t0 = t * 512
            sample = (cond == 0)
            prologue(t, XB, HT, SQ, RS, TMP, der(l, cond, 1, 0), der(l, cond, 1, 1), PS[6])
            if sample:
                for k in range(4):
                    S.dma('sync', RT[k].ap, rope[k, :, t0:t0 + 512], w=[RT[k]])
            dbg = getattr(cfg, 'dbg', ())
            for g in range(NFM):
                if 'fmx' in dbg and g >= G_GQ:
                    continue
                wt = WFB[nw % 3]
                nw += 1
                S.dma('sync', wt.ap, WFs[l * NFM + g, :, :], r=[wbuf(('wf', l), g)], w=[wt])
                pa = PS[(g % 2) * 2]
                for c in range(16):
                    S.op('tensor', 'matmul', r=[wt, HT[c]], w=[pa], out=pa.ap,
                         lhsT=wt.ap[:, c * 128:(c + 1) * 128], rhs=HT[c].ap, start=(c == 0), stop=(c == 15))
                if g % 4 == 0:
                    bg_step()
                dstb = fmbufs(g, t0, 512)
                dst = FMS[g, :, t0:t0 + 512]
                st = STG[nst_ % 4]
                nst_ += 1
                if g < G_RG and g < G_RK:
                    S.op('scalar', 'activation', r=[pa], w=[st], out=st.ap, in_=pa.ap, func=AF.Copy)
                    S.dma('gpsimd', dst, st.ap, r=[st], w=dstb)
                elif g < G_RG:
                    S.op('scalar', 'activation', r=[pa], w=[st], out=st.ap, in_=pa.ap, func=AF.Copy,
                         scale=128.0 ** -0.5)
                    S.dma('gpsimd', dst, st.ap, r=[st], w=dstb)
                elif g < G_GQ:
                    S.op('scalar', 'activation', r=[pa], w=[st], out=st.ap, in_=pa.ap, func=AF.Silu)
                    S.dma('gpsimd', dst, st.ap, r=[st], w=dstb)
                else:
                    isg = g < G_DQ
                    k_ = nq % 2
                    nq += 1
                    pb = PS[(g % 2) * 2 + 1]
                    if isg:
                        sq = SQ[k_]
                        S.op('scalar', 'activation', r=[pa], w=[sq], out=sq.ap, in_=pa.ap, func=AF.Square)
                        S.op('tensor', 'matmul', r=[ONEB, sq], w=[pb], out=pb.ap, lhsT=ONEB.ap, rhs=sq.ap,
                             start=True, stop=True)
                        rq_ = RQ[k_]
                        S.op('scalar', 'activation', r=[pb, EPSC], w=[rq_], out=rq_.ap, in_=pb.ap, func=AF.Sqrt,
                             bias=EPSC.ap, scale=1.0 / 128)
                        S.op('vector', 'reciprocal', r=[rq_], w=[rq_], out=rq_.ap, in_=rq_.ap)
                        wcol = l * 10 + (0 if g < G_GK else 1)
                        qf = OF[k_]
                        S.op('vector', 'scalar_tensor_tensor', r=[pa, SM, rq_], w=[qf], out=qf.ap, in0=pa.ap,
                             scalar=SM.ap[:, wcol:wcol + 1], in1=rq_.ap, op0=ALU.mult, op1=ALU.mult)
                    else:
                        qf = OF[k_]
                        S.op('vector', 'tensor_copy', r=[pa], w=[qf], out=qf.ap, in_=pa.ap)
                    if sample:
                        qn = QN[k_]
                        S.op('scalar', 'activation', r=[qf], w=[qn], out=qn.ap, in_=qf.ap, func=AF.Copy)
                        pm = PGB if isg else PDB
                        S.op('tensor', 'matmul', r=[pm, qn], w=[pb], out=pb.ap, lhsT=pm.ap, rhs=qn.ap,
                             start=True, stop=True)
                        ct, stb = (RT[0], RT[1]) if isg else (RT[2], RT[3])
                        r1 = R1[k_]
                        r2 = R2[k_]
                        S.op('gpsimd', 'tensor_tensor', r=[qf, ct], w=[r1], out=r1.ap, in0=qf.ap, in1=ct.ap, op=ALU.mult)
                        S.op('vector', 'tensor_tensor', r=[pb, stb], w=[r2], out=r2.ap, in0=pb.ap, in1=stb.ap, op=ALU.mult)
                        S.op('gpsimd', 'tensor_tensor', r=[r1, r2], w=[st], out=st.ap, in0=r1.ap, in1=r2.ap, op=ALU.add)
                        S.dma('gpsimd', dst, st.ap, r=[st], w=dstb)
                    else:
                        S.op('scalar', 'activation', r=[qf], w=[st], out=st.ap, in_=qf.ap, func=AF.Copy)
                        S.dma('gpsimd', dst, st.ap, r=[st], w=dstb)
                        if (G_GK <= g < G_DQ) or g >= G_DK:
                            for sb in range(4):
                                S.op('tensor', 'transpose', r=[qf, IDF], w=[pb], out=pb.ap[:, sb * 128:(sb + 1) * 128],
                                     in_=qf.ap[:, sb * 128:(sb + 1) * 128], identity=IDF.ap)
                            ot = OT[k_]
                            S.op('vector', 'tensor_copy', r=[pb], w=[ot], out=ot.ap, in_=pb.ap)
                            for s_ in range(NPS):
                                if g < G_DQ:
                                    h = g - G_GK
                                    o_ = ngk[s_ * L + l, :, h * 128:(h + 1) * 128]
                                else:
                                    h = g - G_DK
                                    o_ = ndk[s_ * L + l, :, h * 128:(h + 1) * 128]
                                S.dma('gpsimd', o_.rearrange("(b p) f -> p b f", p=128),
                                      ot.ap[:, s_ * 256:(s_ + 1) * 256].rearrange("p (b f) -> p b f", b=2),
                                      r=[ot], w=[T(None)])
            for blk in range(4):
                if 'tm' in dbg:
                    continue
                wt = WTB[blk % 2]
                width = 256 if blk == 2 else 512
                S.dma('sync', wt.ap, WTs[l * 4 + blk, :, :], r=[wbuf(('wt', l), blk)], w=[wt])
                for sb in range(4):
                    pa = PS[4 + (sb % 2)]
                    for c in range(16):
                        S.op('tensor', 'matmul', r=[wt, HT[c]], w=[pa], out=pa.ap[:, 0:width],
                             lhsT=HT[c].ap[:, sb * 128:(sb + 1) * 128], rhs=wt.ap[:, c * 512:c * 512 + width],
                             start=(c == 0), stop=(c == 15))
                    st = TST[sb % 2]
                    bi = t * 4 + sb
                    if blk == 0:
                        S.op('scalar', 'activation', r=[pa], w=[st], out=st.ap, in_=pa.ap, func=AF.Copy,
                             scale=128.0 ** -0.5)
                        S.dma('gpsimd', KTOK[bi, :, :], st.ap, r=[st], w=[T(None, B_TOK['k'][bi])])
                    elif blk == 1:
                        S.op('vector', 'tensor_copy', r=[pa], w=[st], out=st.ap, in_=pa.ap)
                        S.dma('gpsimd', VTOK[bi, :, :], st.ap, r=[st], w=[T(None, B_TOK['v'][bi])])
                    else:
                        S.op('vector', 'tensor_copy', r=[pa], w=[st], out=st.ap[:, 0:width], in_=pa.ap[:, 0:width])
                        if blk == 2:
                            S.dma('gpsimd', GVT[bi, :, :], st.ap[:, 0:256], r=[st], w=[T(None, B_TOK['gv'][bi])])
                        else:
                            S.dma('gpsimd', DVT[bi, :, :], st.ap, r=[st], w=[T(None, B_TOK['dv'][bi])])
                        if not sample and 'po' not in dbg:
                            of = OF[sb % 2]
                            of = OT[sb % 2]
                            S.op('vector', 'tensor_copy', r=[pa], w=[of], out=of.ap[:, 0:width], in_=pa.ap[:, 0:width])
                            s_ = sb // 2
                            rr = (sb % 2) * 128
                            od = ngv if blk == 2 else ndv
                            S.dma('gpsimd', od[s_ * L + l, rr:rr + 128, :], of.ap[:, 0:width], r=[of], w=[T(None)])
        S.barrier()
        pool.release(m0)

    def ret_phase(l):
        m0 = pool.mark()
        RD = pool.alloc(8)
        S.dma('sync', RD.ap, rdec[:, l * 8:(l + 1) * 8], w=[RD])
        LG = pool.alloc(8)
        S.op('scalar', 'activation', r=[RD], w=[LG], out=LG.ap, in_=RD.ap, func=AF.Exp, scale=-1.0)
        S.op('vector', 'tensor_scalar', r=[LG], w=[LG], out=LG.ap, in0=LG.ap, scalar1=1.0, scalar2=None, op0=ALU.add)
        S.op('scalar', 'activation', r=[LG], w=[LG], out=LG.ap, in_=LG.ap, func=AF.Ln)
        S.op('vector', 'tensor_scalar', r=[LG], w=[LG], out=LG.ap, in0=LG.ap, scalar1=-1.0, scalar2=None, op0=ALU.mult)
        MT = [pool.alloc(128) for _ in range(4)]
        QDF = [pool.alloc(128) for _ in range(4)]
        QDB = [pool.alloc(128) for _ in range(4)]
        KD = pool.alloc(8)
        CD_ = pool.alloc(8)
        e1 = pool.alloc(128)
        e2 = pool.alloc(128)
        for h in range(4):
            lf = LG.ap[:, h:h + 1]
            lb = LG.ap[:, 4 + h:5 + h]
            S.op('scalar', 'activation', r=[CON, LG], w=[e1], out=e1.ap, in_=CON.ap[:, C_POS:C_POS + 128], func=AF.Exp, scale=lf)
            S.op('vector', 'tensor_tensor', r=[e1, CON], w=[e1], out=e1.ap, in0=e1.ap, in1=CON.ap[:, C_MF:C_MF + 128], op=ALU.mult)
            S.op('scalar', 'activation', r=[CON, LG], w=[e2], out=e2.ap, in_=CON.ap[:, C_NEG:C_NEG + 128], func=AF.Exp, scale=lb)
            S.op('vector', 'tensor_tensor', r=[e2, CON], w=[e2], out=e2.ap, in0=e2.ap, in1=CON.ap[:, C_MB:C_MB + 128], op=ALU.mult)
            S.op('vector', 'tensor_tensor', r=[e1, e2], w=[MT[h]], out=MT[h].ap, in0=e1.ap, in1=e2.ap, op=ALU.add)
            S.op('scalar', 'activation', r=[CON, LG], w=[QDF[h]], out=QDF[h].ap, in_=CON.ap[:, C_I1:C_I1 + 128], func=AF.Exp, scale=lf)
            S.op('scalar', 'activation', r=[CON, LG], w=[QDB[h]], out=QDB[h].ap, in_=CON.ap[:, C_IB:C_IB + 128], func=AF.Exp, scale=lb)
            S.op('scalar', 'activation', r=[CON, LG], w=[KD], out=KD.ap[:, h:h + 1], in_=CON.ap[:, C_KF:C_KF + 1], func=AF.Exp, scale=lf)
            S.op('scalar', 'activation', r=[CON, LG], w=[KD], out=KD.ap[:, 4 + h:5 + h], in_=CON.ap[:, C_KB:C_KB + 1], func=AF.Exp, scale=lb)
        S.op('scalar', 'activation', r=[LG], w=[CD_], out=CD_.ap, in_=LG.ap, func=AF.Exp, scale=128.0)

        nchmax = TS // 128
        SBALL = [pool.alloc(512, BF16) for _ in range(nchmax)]
        SF = pool.alloc(512)
        SB_ = pool.alloc(512)
        SFB = pool.alloc(512, BF16)
        KT_ = [pool.alloc(512, BF16) for _ in range(2)]
        VT_ = [pool.alloc(512, BF16) for _ in range(2)]
        KS = [pool.alloc(512, BF16) for _ in range(2)]
        QT_ = [pool.alloc(512, BF16) for _ in range(2)]
        KTT = [pool.alloc(512, BF16) for _ in range(2)]
        RGT = [pool.alloc(512, BF16) for _ in range(2)]
        QF = [pool.alloc(512, BF16) for _ in range(2)]
        QB = [pool.alloc(512, BF16) for _ in range(2)]
        AM = [pool.alloc(128, BF16) for _ in range(4)]
        OB = pool.alloc(512, BF16)
        OSQ = pool.alloc(512, BF16)
        OF_ = pool.alloc(512)
        MEAN = pool.alloc(512)
        VAR = pool.alloc(512)
        OUT = [pool.alloc(512, BF16) for _ in range(2)]

        seqs = [(0, TS, True, None)] + [(TS + s_ * TP, TP, False, s_) for s_ in range(NPS)]
        for (tok0, Tn, has_ctx, sidx) in seqs:
            nch = Tn // 128
            b0 = tok0 // 128
            if has_ctx:
                S.dma('sync', SF.ap, stin[l * 2 + 0, :, :], w=[SF])
                S.dma('sync', SB_.ap, stin[l * 2 + 1, :, :], w=[SB_])
            else:
                S.op('vector', 'memset', w=[SF], ap=SF.ap, constant=0.0)
                S.op('vector', 'memset', w=[SB_], ap=SB_.ap, constant=0.0)
            for n in range(nch - 1, -1, -1):
                bi = b0 + n
                kt = KT_[n % 2]
                vt = VT_[n % 2]
                S.dma('sync', kt.ap, KTOK[bi, :, :], r=[T(None, B_TOK['k'][bi])], w=[kt])
                S.dma('sync', vt.ap, VTOK[bi, :, :], r=[T(None, B_TOK['v'][bi])], w=[vt])
                S.op('scalar', 'activation', r=[SB_], w=[SBALL[n]], out=SBALL[n].ap, in_=SB_.ap, func=AF.Copy)
                ks = KS[n % 2]
                for h in range(4):
                    S.op('vector', 'tensor_scalar', r=[kt, KD], w=[ks], out=ks.ap[:, h * 128:(h + 1) * 128],
                         in0=kt.ap[:, h * 128:(h + 1) * 128], scalar1=KD.ap[:, 4 + h:5 + h], scalar2=None, op0=ALU.mult)
                pu = PS[n % 2]
                for h in range(4):
                    S.op('tensor', 'matmul', r=[ks, vt], w=[pu], out=pu.ap[:, h * 128:(h + 1) * 128],
                         lhsT=ks.ap[:, h * 128:(h + 1) * 128], rhs=vt.ap[:, h * 128:(h + 1) * 128], start=True, stop=True)
                for h in range(4):
                    S.op('vector', 'scalar_tensor_tensor', r=[SB_, CD_, pu], w=[SB_], out=SB_.ap[:, h * 128:(h + 1) * 128],
                         in0=SB_.ap[:, h * 128:(h + 1) * 128], scalar=CD_.ap[:, 4 + h:5 + h],
                         in1=pu.ap[:, h * 128:(h + 1) * 128], op0=ALU.mult, op1=ALU.add)
            if not has_ctx:
                S.dma('gpsimd', nst[(sidx * L + l) * 2 + 1, :, :], SB_.ap, r=[SB_], w=[T(None)])
            for n in range(nch):
                bi = b0 + n
                c0 = tok0 + n * 128
                kt = KT_[n % 2]
                vt = VT_[n % 2]
                qt = QT_[n % 2]
                ktt = KTT[n % 2]
                rgt = RGT[n % 2]
                S.dma('sync', kt.ap, KTOK[bi, :, :], r=[T(None, B_TOK['k'][bi])], w=[kt])
                S.dma('sync', vt.ap, VTOK[bi, :, :], r=[T(None, B_TOK['v'][bi])], w=[vt])
                S.dma('sync', qt.ap.rearrange("p (h t) -> p h t", h=4),
                      FMS[G_RQ:G_RQ + 4, :, c0:c0 + 128].rearrange("h p t -> p h t"),
                      r=[T(None, B_FMS[G_RQ + h][bi]) for h in range(4)], w=[qt])
                S.dma('sync', ktt.ap.rearrange("p (h t) -> p h t", h=4),
                      FMS[G_RK:G_RK + 4, :, c0:c0 + 128].rearrange("h p t -> p h t"),
                      r=[T(None, B_FMS[G_RK + h][bi]) for h in range(4)], w=[ktt])
                S.dma('sync', rgt.ap.rearrange("p (h t) -> p h t", h=4),
                      FMS[G_RG:G_RG + 4, :, c0:c0 + 128].rearrange("h p t -> p h t"),
                      r=[T(None, B_FMS[G_RG + h][bi]) for h in range(4)], w=[rgt])
                S.op('scalar', 'activation', r=[SF], w=[SFB], out=SFB.ap, in_=SF.ap, func=AF.Copy)
                qf = QF[n % 2]
                qb = QB[n % 2]
                pa = PS[2 + (n % 2)]
                po = PS[4 + (n % 2)]
                for h in range(4):
                    hs = slice(h * 128, (h + 1) * 128)
                    S.op('gpsimd', 'tensor_tensor', r=[qt, QDF[h]], w=[qf], out=qf.ap[:, hs], in0=qt.ap[:, hs], in1=QDF[h].ap, op=ALU.mult)
                    S.op('gpsimd', 'tensor_tensor', r=[qt, QDB[h]], w=[qb], out=qb.ap[:, hs], in0=qt.ap[:, hs], in1=QDB[h].ap, op=ALU.mult)
                for h in range(4):
                    hs = slice(h * 128, (h + 1) * 128)
                    S.op('tensor', 'matmul', r=[ktt, qt], w=[pa], out=pa.ap[:, hs], lhsT=ktt.ap[:, hs], rhs=qt.ap[:, hs],
                         start=True, stop=True)
                for h in range(4):
                    hs = slice(h * 128, (h + 1) * 128)
                    S.op('vector', 'tensor_tensor', r=[pa, MT[h]], w=[AM[h]], out=AM[h].ap, in0=pa.ap[:, hs], in1=MT[h].ap, op=ALU.mult)
                for h in range(4):
                    hs = slice(h * 128, (h + 1) * 128)
                    S.op('tensor', 'matmul', r=[vt, AM[h]], w=[po], out=po.ap[:, hs], lhsT=vt.ap[:, hs], rhs=AM[h].ap, start=True, stop=False)
                    S.op('tensor', 'matmul', r=[SFB, qf], w=[po], out=po.ap[:, hs], lhsT=SFB.ap[:, hs], rhs=qf.ap[:, hs], start=False, stop=False)
                    S.op('tensor', 'matmul', r=[SBALL[n], qb], w=[po], out=po.ap[:, hs], lhsT=SBALL[n].ap[:, hs], rhs=qb.ap[:, hs], start=False, stop=True)
                ks = KS[n % 2]
                for h in range(4):
                    hs = slice(h * 128, (h + 1) * 128)
                    S.op('vector', 'tensor_scalar', r=[kt, KD], w=[ks], out=ks.ap[:, hs], in0=kt.ap[:, hs],
                         scalar1=KD.ap[:, h:h + 1], scalar2=None, op0=ALU.mult)
                pu = PS[n % 2]
                for h in range(4):
                    hs = slice(h * 128, (h + 1) * 128)
                    S.op('tensor', 'matmul', r=[ks, vt], w=[pu], out=pu.ap[:, hs], lhsT=ks.ap[:, hs], rhs=vt.ap[:, hs], start=True, stop=True)
                for h in range(4):
                    hs = slice(h * 128, (h + 1) * 128)
                    S.op('vector', 'scalar_tensor_tensor', r=[SF, CD_, pu], w=[SF], out=SF.ap[:, hs], in0=SF.ap[:, hs],
                         scalar=CD_.ap[:, h:h + 1], in1=pu.ap[:, hs], op0=ALU.mult, op1=ALU.add)
                S.op('scalar', 'activation', r=[po], w=[OB], out=OB.ap, in_=po.ap, func=AF.Copy)
                S.op('scalar', 'activation', r=[po], w=[OSQ], out=OSQ.ap, in_=po.ap, func=AF.Square)
                S.op('vector', 'tensor_copy', r=[po], w=[OF_], out=OF_.ap, in_=po.ap)
                pm = PS[6]
                pv = PS[7]
                S.op('tensor', 'matmul', r=[ONEB, OB], w=[pm], out=pm.ap, lhsT=ONEB.ap, rhs=OB.ap, start=True, stop=True)
                S.op('tensor', 'matmul', r=[ONEB, OSQ], w=[pv], out=pv.ap, lhsT=ONEB.ap, rhs=OSQ.ap, start=True, stop=True)
                S.op('scalar', 'activation', r=[pm], w=[MEAN], out=MEAN.ap, in_=pm.ap, func=AF.Copy, scale=1.0 / 128)
                S.op('vector', 'tensor_tensor', r=[MEAN], w=[VAR], out=VAR.ap, in0=MEAN.ap, in1=MEAN.ap, op=ALU.mult)
                S.op('vector', 'scalar_tensor_tensor', r=[pv, VAR], w=[VAR], out=VAR.ap, in0=pv.ap, scalar=1.0 / 128,
                     in1=VAR.ap, op0=ALU.mult, op1=ALU.subtract)
                S.op('vector', 'tensor_scalar', r=[VAR], w=[VAR], out=VAR.ap, in0=VAR.ap, scalar1=0.0, scalar2=None, op0=ALU.max)
                S.op('scalar', 'activation', r=[VAR, EPSC], w=[VAR], out=VAR.ap, in_=VAR.ap, func=AF.Sqrt, bias=EPSC.ap, scale=1.0)
                S.op('vector', 'reciprocal', r=[VAR], w=[VAR], out=VAR.ap, in_=VAR.ap)
                S.op('vector', 'tensor_tensor', r=[OF_, MEAN], w=[OF_], out=OF_.ap, in0=OF_.ap, in1=MEAN.ap, op=ALU.subtract)
                S.op('vector', 'tensor_tensor', r=[OF_, VAR], w=[OF_], out=OF_.ap, in0=OF_.ap, in1=VAR.ap, op=ALU.mult)
                ot = OUT[n % 2]
                for h in range(4):
                    hs = slice(h * 128, (h + 1) * 128)
                    S.op('vector', 'scalar_tensor_tensor', r=[OF_, SM, rgt], w=[ot], out=ot.ap[:, hs], in0=OF_.ap[:, hs],
                         scalar=SM.ap[:, l * 10 + 2 + h:l * 10 + 3 + h], in1=rgt.ap[:, hs], op0=ALU.mult, op1=ALU.mult)
                S.dma('gpsimd', MIX[0:4, :, c0:c0 + 128].rearrange("h p t -> p h t"),
                      ot.ap.rearrange("p (h t) -> p h t", h=4), r=[ot], w=[T(None, B_MIX[h][bi]) for h in range(4)])
            if not has_ctx:
                S.dma('gpsimd', nst[(sidx * L + l) * 2 + 0, :, :], SF.ap, r=[SF], w=[T(None)])
        S.barrier()
        pool.release(m0)

    def attn_phase(l, diff):
        m0 = pool.mark()
        nkcmax = (TS + PAST) // 128
        KB_ = [pool.alloc(TS + PAST, BF16) for _ in range(2)]
        VB_ = [pool.alloc(nkcmax * 128, BF16) for _ in range(2)]
        QB_ = [pool.alloc(1024, BF16) for _ in range(2)]
        PT = [pool.alloc(512, BF16) for _ in range(4)]
        RC = [pool.alloc(512) for _ in range(2)]
        ACC = [pool.alloc(512) for _ in range(2)]
        O1 = pool.alloc(512)
        O2 = pool.alloc(512)
        OSQ = pool.alloc(512, BF16)
        RSD = pool.alloc(512)
        OUT = [pool.alloc(512, BF16) for _ in range(2)]
        LAM = pool.alloc(4)
        lam_init = 0.8 - 0.6 * math.exp(-0.3 * l)
        if diff:
            DL = pool.alloc(256)
            S.dma('sync', DL.ap, dlam[:, l * 256:(l + 1) * 256], w=[DL])
            PR = pool.alloc(128)
            S.op('vector', 'tensor_tensor', r=[DL], w=[PR], out=PR.ap.rearrange("p (a f) -> p a f", a=2),
                 in0=DL.ap.rearrange("p (a b f) -> p a b f", a=2, b=2)[:, :, 0, :],
                 in1=DL.ap.rearrange("p (a b f) -> p a b f", a=2, b=2)[:, :, 1, :], op=ALU.mult)
            S.op('vector', 'reduce_sum', r=[PR], w=[LAM], out=LAM.ap[:, 0:2], in_=PR.ap.rearrange("p (a f) -> p a f", a=2), axis=AX.X)
            S.op('scalar', 'activation', r=[LAM], w=[LAM], out=LAM.ap[:, 0:2], in_=LAM.ap[:, 0:2], func=AF.Exp)
            S.op('vector', 'tensor_tensor', r=[LAM], w=[LAM], out=LAM.ap[:, 2:3], in0=LAM.ap[:, 1:2], in1=LAM.ap[:, 0:1], op=ALU.subtract)
            S.op('vector', 'tensor_scalar', r=[LAM], w=[LAM], out=LAM.ap[:, 2:3], in0=LAM.ap[:, 2:3], scalar1=-lam_init, scalar2=None, op0=ALU.add)
            DNW = pool.alloc(4)
            S.op('vector', 'tensor_scalar', r=[SM], w=[DNW], out=DNW.ap, in0=SM.ap[:, l * 10 + 6:l * 10 + 10],
                 scalar1=1.0 - lam_init, scalar2=None, op0=ALU.mult)
        nkv = 4 if diff else 2
        scale = (64.0 ** -0.5) if diff else (128.0 ** -0.5)
        seqs = [(0, TS, True, 512)] + [(TS + s_ * TP, TP, False, 256) for s_ in range(NPS)]
        nb_ = 0
        nq_ = 0
        npt = 0
        nout = 0
        for (tok0, Tn, has_ctx, TQ) in seqs:
            nk_own = Tn // 128
            nkc = nk_own + (PAST // 128 if has_ctx else 0)
            b0 = tok0 // 128
            for g in range(nkv):
                kb = KB_[nb_ % 2]
                vb = VB_[nb_ % 2]
                nb_ += 1
                gk = (G_DK if diff else G_GK) + g
                S.dma('sync', kb.ap[:, 0:Tn], FMS[gk, :, tok0:tok0 + Tn],
                      r=[T(None, B_FMS[gk][b]) for b in range(b0, b0 + nk_own)], w=[kb])
                vsrc = DVT if diff else GVT
                S.dma('sync', vb.ap[:, 0:nk_own * 128].rearrange("p (k e) -> p k e", e=128),
                      vsrc[b0:b0 + nk_own, :, g * 128:(g + 1) * 128].rearrange("k p e -> p k e"),
                      r=[T(None, B_TOK['dv' if diff else 'gv'][b]) for b in range(b0, b0 + nk_own)], w=[vb])
                if has_ctx:
                    ck = cdk if diff else cgk
                    cvv = cdv if diff else cgv
                    S.dma('gpsimd', kb.ap[:, Tn:Tn + PAST], ck[l * nkv + g, :, :], w=[kb])
                    for kc in range(PAST // 128):
                        S.dma('gpsimd', vb.ap[:, (nk_own + kc) * 128:(nk_own + kc + 1) * 128],
                              cvv[(l * nkv + g) * 2 + kc, :, :], w=[vb])
                if diff:
                    units = [(g, None)]
                else:
                    units = [(4 * g + 2 * pr, 4 * g + 2 * pr + 1) for pr in range(2)]
                for q0 in range(0, Tn, TQ):
                    qa = tok0 + q0
                    for un in units:
                        qb = QB_[nq_ % 2]
                        nq_ += 1
                        if diff:
                            gq = G_DQ + g
                            S.dma('sync', qb.ap[:, 0:TQ], FMS[gq, :, qa:qa + TQ], r=fmbufs(gq, qa, TQ), w=[qb])
                        else:
                            for k_, hq in enumerate(un):
                                S.dma('sync', qb.ap[:, k_ * 512:k_ * 512 + TQ], FMS[G_GQ + hq, :, qa:qa + TQ],
                                      r=fmbufs(G_GQ + hq, qa, TQ), w=[qb])
                        for kc in range(nkc):
                            for k_ in range(2):
                                ps_s = PS[4 + (npt % 4)]
                                pt = PT[npt % 4]
                                npt += 1
                                if diff:
                                    S.op('tensor', 'matmul', r=[kb, qb], w=[ps_s], out=ps_s.ap[:, 0:TQ],
                                         lhsT=kb.ap[k_ * 64:(k_ + 1) * 64, kc * 128:(kc + 1) * 128],
                                         rhs=qb.ap[k_ * 64:(k_ + 1) * 64, 0:TQ], start=True, stop=True)
                                else:
                                    S.op('tensor', 'matmul', r=[kb, qb], w=[ps_s], out=ps_s.ap[:, 0:TQ],
                                         lhsT=kb.ap[:, kc * 128:(kc + 1) * 128], rhs=qb.ap[:, k_ * 512:k_ * 512 + TQ],
                                         start=True, stop=True)
                                S.op('scalar', 'activation', r=[ps_s], w=[pt], out=pt.ap[:, 0:TQ], in_=ps_s.ap[:, 0:TQ],
                                     func=AF.Exp, scale=scale)
                                S.op('tensor', 'matmul', r=[vb, pt], w=[PS[k_]], out=PS[k_].ap[:, 0:TQ],
                                     lhsT=vb.ap[:, kc * 128:(kc + 1) * 128], rhs=pt.ap[:, 0:TQ],
                                     start=(kc == 0), stop=(kc == nkc - 1))
                                pacc = PS[2 + k_]
                                if kc == 0:
                                    S.op('vector', 'tensor_copy', r=[pt], w=[pacc], out=pacc.ap[:, 0:TQ], in_=pt.ap[:, 0:TQ])
                                else:
                                    S.op('vector', 'tensor_tensor', r=[pt, pacc], w=[pacc], out=pacc.ap[:, 0:TQ],
                                         in0=pacc.ap[:, 0:TQ], in1=pt.ap[:, 0:TQ], op=ALU.add)
                        for k_ in range(2):
                            S.op('scalar', 'activation', r=[PS[2 + k_]], w=[ACC[k_]], out=ACC[k_].ap[:, 0:TQ],
                                 in_=PS[2 + k_].ap[:, 0:TQ], func=AF.Identity)
                            S.op('tensor', 'matmul', r=[CON, ACC[k_]], w=[PS[2 + k_]], out=PS[2 + k_].ap[:, 0:TQ],
                                 lhsT=CON.ap[:, C_ONE:C_ONE + 128], rhs=ACC[k_].ap[:, 0:TQ], start=True, stop=True)
                        for k_ in range(2):
                            S.op('vector', 'reciprocal', r=[PS[2 + k_]], w=[RC[k_]], out=RC[k_].ap[:, 0:TQ], in_=PS[2 + k_].ap[:, 0:TQ])
                        bl = range(qa // 128, (qa + TQ) // 128)
                        if not diff:
                            for k_, hq in enumerate(un):
                                ot = OUT[nout % 2]
                                nout += 1
                                S.op('vector', 'tensor_tensor', r=[PS[k_], RC[k_]], w=[ot], out=ot.ap[:, 0:TQ],
                                     in0=PS[k_].ap[:, 0:TQ], in1=RC[k_].ap[:, 0:TQ], op=ALU.mult)
                                S.dma('gpsimd', MIX[4 + hq, :, qa:qa + TQ], ot.ap[:, 0:TQ], r=[ot],
                                      w=[T(None, B_MIX[4 + hq][b]) for b in bl])
                        else:
                            S.op('vector', 'tensor_tensor', r=[PS[0], RC[0]], w=[O1], out=O1.ap[:, 0:TQ],
                                 in0=PS[0].ap[:, 0:TQ], in1=RC[0].ap[:, 0:TQ], op=ALU.mult)
                            S.op('vector', 'tensor_tensor', r=[PS[1], RC[1]], w=[O2], out=O2.ap[:, 0:TQ],
                                 in0=PS[1].ap[:, 0:TQ], in1=RC[1].ap[:, 0:TQ], op=ALU.mult)
                            S.op('vector', 'scalar_tensor_tensor', r=[O2, LAM, O1], w=[O1], out=O1.ap[:, 0:TQ], in0=O2.ap[:, 0:TQ],
                                 scalar=LAM.ap[:, 2:3], in1=O1.ap[:, 0:TQ], op0=ALU.mult, op1=ALU.add)
                            S.op('scalar', 'activation', r=[O1], w=[OSQ], out=OSQ.ap[:, 0:TQ], in_=O1.ap[:, 0:TQ], func=AF.Square)
                            S.op('tensor', 'matmul', r=[ONEB, OSQ], w=[PS[2]], out=PS[2].ap[:, 0:TQ], lhsT=ONEB.ap,
                                 rhs=OSQ.ap[:, 0:TQ], start=True, stop=True)
                            S.op('scalar', 'activation', r=[PS[2], EPSC], w=[RSD], out=RSD.ap[:, 0:TQ], in_=PS[2].ap[:, 0:TQ],
                                 func=AF.Sqrt, bias=EPSC.ap, scale=1.0 / 128)
                            S.op('vector', 'reciprocal', r=[RSD], w=[RSD], out=RSD.ap[:, 0:TQ], in_=RSD.ap[:, 0:TQ])
                            ot = OUT[nout % 2]
                            nout += 1
                            S.op('vector', 'scalar_tensor_tensor', r=[O1, DNW, RSD], w=[ot], out=ot.ap[:, 0:TQ], in0=O1.ap[:, 0:TQ],
                                 scalar=DNW.ap[:, g:g + 1], in1=RSD.ap[:, 0:TQ], op0=ALU.mult, op1=ALU.mult)
                            S.dma('gpsimd', MIX[12 + g, :, qa:qa + TQ], ot.ap[:, 0:TQ], r=[ot],
                                  w=[T(None, B_MIX[12 + g][b]) for b in bl])
        S.barrier()
        pool.release(m0)

    def wout_phase(l):
        bg_ensure(('wo', l))
        m0 = pool.mark()
        MX = [pool.alloc(512, BF16) for _ in range(32)]
        WB = [pool.alloc(2048, BF16) for _ in range(3)]
        XC = [pool.alloc(512) for _ in range(4)]
        nw = 0
        for t in range(NT):
            cond = tile_cond(t)
            t0 = t * 512
            mx = MX[(t % 2) * 16:(t % 2) * 16 + 16]
            for c in range(16):
                S.dma('sync', mx[c].ap, MIX[c, :, t0:t0 + 512],
                      r=[T(None, B_MIX[c][b]) for b in range(t * 4, t * 4 + 4)], w=[mx[c]])
            gate = der(l, cond, 1, 2)
            for dj in range(16):
                wt = WB[nw % 3]
                xc = XC[nw % 4]
                nw += 1
                S.dma('sync', wt.ap, WOs[l * 16 + dj, :, :], r=[wbuf(('wo', l), dj)], w=[wt])
                S.dma('sync', xc.ap, XT[dj, :, t0:t0 + 512], r=[T(None, B_XT[dj][t])], w=[xc])
                po = PS[dj % 4]
                for c in range(16):
                    S.op('tensor', 'matmul', r=[wt, mx[c]], w=[po], out=po.ap, lhsT=wt.ap[:, c * 128:(c + 1) * 128],
                         rhs=mx[c].ap, start=(c == 0), stop=(c == 15))
                S.op('vector', 'scalar_tensor_tensor', r=[po, gate, xc], w=[xc], out=xc.ap, in0=po.ap,
                     scalar=gate.ap[:, dj:dj + 1], in1=xc.ap, op0=ALU.mult, op1=ALU.add)
                S.dma('gpsimd', XT[dj, :, t0:t0 + 512], xc.ap, r=[xc], w=[T(None, B_XT[dj][t])])
        S.barrier()
        pool.release(m0)

    def final_phase():
        m0 = pool.mark()
        XB = [pool.alloc(512) for _ in range(16)]
        SQ = [pool.alloc(512, BF16) for _ in range(2)]
        RS = pool.alloc(512)
        YT = [pool.alloc(512) for _ in range(2)]
        YO = [pool.alloc(2048) for _ in range(4)]
        ny = 0
        for t in range(NT):
            prologue(t, XB, None, SQ, RS, None, None, None, PS[6])
            for c in range(16):
                yt = YT[c % 2]
                S.op('vector', 'scalar_tensor_tensor', r=[XB[c], FNW, RS], w=[yt], out=yt.ap, in0=XB[c].ap,
                     scalar=FNW.ap[:, c:c + 1], in1=RS.ap, op0=ALU.mult, op1=ALU.mult)
                pb = PS[c % 4]
                for sb in range(4):
                    S.op('tensor', 'transpose', r=[yt, IDF], w=[pb], out=pb.ap[:, sb * 128:(sb + 1) * 128],
                         in_=yt.ap[:, sb * 128:(sb + 1) * 128], identity=IDF.ap)
                for sb in range(4):
                    yo = YO[sb]
                    S.op('vector', 'tensor_copy', r=[pb], w=[yo], out=yo.ap[:, c * 128:(c + 1) * 128], in_=pb.ap[:, sb * 128:(sb + 1) * 128])
            for sb in range(4):
                r0 = t * 512 + sb * 128
                S.dma('sync', y[r0:r0 + 128, :], YO[sb].ap, r=[YO[sb]], w=[T(None)])
        S.barrier()
        pool.release(m0)

    plan = []
    for l in range(L):
        plan += [(ffn_phase, (l, 0)), (inproj_phase, (l,)), (ret_phase, (l,)), (attn_phase, (l, False)),
                 (attn_phase, (l, True)), (wout_phase, (l,)), (ffn_phase, (l, 1))]
    plan.append((final_phase, ()))
    for fn_, a_ in plan[:getattr(cfg, 'nphase', 1000)]:
        fn_(*a_)

    with nc.Block() as block:
        S.emit(block)
    es.close()
    return nc


def _prep_shared(cfg, inp):
    L, NF = cfg.L, cfg.NF
    f = np.float32
    out = {}
    wm = np.asarray(inp['w_mod'], f)
    out['wmod'] = np.ascontiguousarray(
        wm.reshape(L, 16, 128, 72, 2, 128).transpose(0, 3, 2, 4, 1, 5)).reshape(L * 72, 128, 4096)
    del wm
    out['bmod'] = np.ascontiguousarray(np.asarray(inp['b_mod'], f).reshape(L, 144, 128).transpose(2, 0, 1)).reshape(128, L * 144)
    out['normw'] = np.ascontiguousarray(np.asarray(inp['norm_w'], f).reshape(L, 3, 16, 128).transpose(3, 0, 1, 2)).reshape(128, L * 48)
    out['fnw'] = np.ascontiguousarray(np.asarray(inp['final_norm_w'], f).reshape(16, 128).T)
    w1 = np.asarray(inp['ffn_w_in'], f)
    out['w1'] = np.ascontiguousarray(
        w1.reshape(L, 2, 16, 128, 2, NF, 128).transpose(0, 1, 5, 3, 2, 4, 6)).reshape(L * 2 * NF, 128, 4096)
    del w1
    w2 = np.asarray(inp['ffn_w_out'], f)
    out['w2'] = np.ascontiguousarray(
        w2.reshape(L, 2, NF, 128, 16, 128).transpose(0, 1, 4, 3, 2, 5)).reshape(L * 2 * 16, 128, NF * 128)
    del w2
    win = np.asarray(inp['w_in'], f)
    fm, tm = _win_cols()
    wf = np.stack([win[:, :, cols] for cols in fm], 1)
    out['winf'] = np.ascontiguousarray(wf.reshape(L, NFM, 16, 128, 128).transpose(0, 1, 3, 2, 4)).reshape(L * NFM, 128, 2048)
    wt = np.zeros((L, 4, 128, 16, 512), f)
    for bi, cols in enumerate(tm):
        blk = win[:, :, cols].reshape(L, 16, 128, len(cols)).transpose(0, 2, 1, 3)
        wt[:, bi, :, :, 0:len(cols)] = blk
    out['wint'] = wt.reshape(L * 4, 128, 8192)
    wo = np.asarray(inp['w_out'], f)
    out['wout'] = np.ascontiguousarray(wo.reshape(L, 16, 128, 16, 128).transpose(0, 3, 2, 1, 4)).reshape(L * 16, 128, 2048)
    p128 = _perm_deint(128)
    qkn = np.asarray(inp['gqa_qk_norm'], f)[:, :, p128]
    gnw = np.asarray(inp['ret_gn_w'], f)
    dnw = np.asarray(inp['diff_norm_w'], f)
    sm = np.concatenate([qkn, gnw, dnw], 1)
    out['smalls'] = np.ascontiguousarray(sm.transpose(2, 0, 1)).reshape(128, L * 10)
    out['rdec'] = np.ascontiguousarray(np.broadcast_to(np.asarray(inp['ret_decay'], f).reshape(1, L * 8), (128, L * 8)))
    out['dlam'] = np.ascontiguousarray(np.broadcast_to(np.asarray(inp['diff_lambda'], f).reshape(1, L * 256), (128, L * 256)))
    out['rope'] = _rope_tables(cfg.TS)
    out['consts'] = _consts()
    return out


def _prep_core(cfg, inp, b):
    L = cfg.L
    f = np.float32
    p128 = _perm_deint(128)
    p64 = _perm_deint(64)
    d = {}
    xs = np.asarray(inp['x_sample'], f)[b]
    xp = np.asarray(inp['x_prompt'], f)[2 * b:2 * b + 2].reshape(-1, cfg.D)
    d['xin'] = np.concatenate([xs, xp], 0)
    cv = np.stack([np.asarray(inp['c'], f)[b].reshape(16, 128).T, np.asarray(inp['c_ctx'], f).reshape(16, 128).T], 1)
    d['cv'] = np.ascontiguousarray(cv).reshape(128, 32)
    st = np.asarray(inp['state_ret'], f)[b]
    d['stin'] = np.ascontiguousarray(st.transpose(0, 1, 3, 2, 4)).reshape(L * 2, 128, 512)
    gk = np.asarray(inp['cache_gqa_k'], f)[b][:, :, :, p128]
    d['cgk'] = np.ascontiguousarray(gk.transpose(0, 2, 3, 1)).reshape(L * 2, 128, cfg.PAST)
    gv = np.asarray(inp['cache_gqa_v'], f)[b]
    d['cgv'] = np.ascontiguousarray(gv.reshape(L, 2, 128, 2, 128).transpose(0, 3, 1, 2, 4)).reshape(L * 4, 128, 128)
    dk = np.asarray(inp['cache_diff_k'], f)[b]
    dk = dk.reshape(L, cfg.PAST, 4, 2, 64)[..., p64].reshape(L, cfg.PAST, 4, 128)
    d['cdk'] = np.ascontiguousarray(dk.transpose(0, 2, 3, 1)).reshape(L * 4, 128, cfg.PAST)
    dv = np.asarray(inp['cache_diff_v'], f)[b]
    d['cdv'] = np.ascontiguousarray(dv.reshape(L, 2, 128, 4, 128).transpose(0, 3, 1, 2, 4)).reshape(L * 8, 128, 128)
    return d


def run(cfg, inp, n_cores):
    nc = build(cfg)
    shared = _prep_shared(cfg, inp)
    in_maps = []
    for b in range(n_cores):
        m = dict(shared)
        m.update(_prep_core(cfg, inp, b))
        in_maps.append(m)
    res = run_bass_kernel_spmd(nc, in_maps, core_ids=list(range(n_cores)))
    L = cfg.L
    inv128 = np.argsort(_perm_deint(128))
    inv64 = np.argsort(_perm_deint(64))
    ys, yp, st, gk, gv, dk, dv = [], [], [], [], [], [], []
    for b in range(n_cores):
        r = res.results[b]
        yy = np.asarray(r['y'])
        ys.append(yy[:cfg.TS])
        yp.append(yy[cfg.TS:].reshape(2, cfg.TP, cfg.D))
        s_ = np.asarray(r['nst']).reshape(2, L, 2, 128, 4, 128).transpose(0, 1, 2, 4, 3, 5)
        st.append(s_)
        k_ = np.asarray(r['ngk']).reshape(2, L, 256, 2, 128)[..., inv128]
        gk.append(k_)
        gv.append(np.asarray(r['ngv']).reshape(2, L, 256, 2, 128))
        k2 = np.asarray(r['ndk']).reshape(2, L, 256, 4, 2, 64)[..., inv64].reshape(2, L, 256, 4, 128)
        dk.append(k2)
        dv.append(np.asarray(r['ndv']).reshape(2, L, 256, 4, 128))
    cat = lambda xs_: np.ascontiguousarray(np.concatenate(xs_, 0)).astype(np.float32)
    return (cat(yp), np.ascontiguousarray(np.stack(ys, 0)).astype(np.float32), cat(st), cat(gk), cat(gv), cat(dk), cat(dv))


def kernel(**inputs):
    cfg = Cfg()
    return run(cfg, inputs, 8)
```
